# Optimizing a Trainium2 kernel written in Bass

```python
import jax, jax.numpy as jnp
from jax import lax
import numpy as np

D_MODEL = 1024
BATCH = 2
SEQ = 8192
DEPTH = 4

GRID_W = 64
CTX_LEN = 256
N_MOD = 6
EPS = 1e-6
D_CONV = 512
CONV_K = 31
ATT_HEADS = 8
ATT_KV_HEADS = 2
ATT_HD = 64
ATT_GROUPS = ATT_HEADS // ATT_KV_HEADS
ROPE_THETA = 10000.0
Q_BLOCK = 128
RET_HEADS = 8
RET_DK = 64
RET_DV = 64
RET_CHUNK = 128
D_FF = 2816
FFN_K = 3
N_BRANCH = 3
IN_SIZES = (2 * D_CONV, ATT_HEADS * ATT_HD, ATT_KV_HEADS * ATT_HD, ATT_KV_HEADS * ATT_HD,
            RET_HEADS * RET_DK, RET_HEADS * RET_DK, RET_HEADS * RET_DV, RET_HEADS * RET_DV,
            N_BRANCH * D_MODEL)
IN_COLS = sum(IN_SIZES)
IN_SPLITS = tuple(sum(IN_SIZES[:i + 1]) for i in range(len(IN_SIZES) - 1))

kernel_name = "hybrid_conv_gqa_retention_dit_block"


def rms_norm(x, g):
    xf = x.astype(jnp.float32)
    y = xf * lax.rsqrt(jnp.mean(xf * xf, axis=-1, keepdims=True) + EPS)
    return (y * g.astype(jnp.float32)).astype(x.dtype)


def layer_norm(x, g, b):
    xf = x.astype(jnp.float32)
    mu = jnp.mean(xf, axis=-1, keepdims=True)
    xc = xf - mu
    y = xc * lax.rsqrt(jnp.mean(xc * xc, axis=-1, keepdims=True) + EPS)
    return (y * g.astype(jnp.float32) + b.astype(jnp.float32)).astype(x.dtype)


def modulate(h, shift, scale):
    return h * (1.0 + scale) + shift


def heads(t, n):
    return t.reshape(t.shape[:-1] + (n, t.shape[-1] // n))


def dwconv(x, w, b):
    k = w.shape[0]
    pad = k // 2
    y = lax.conv_general_dilated(x, w[:, None, :].astype(x.dtype), (1,), [(pad, pad)],
                                 dimension_numbers=('NWC', 'WIO', 'NWC'),
                                 feature_group_count=x.shape[-1])
    return y + b.astype(x.dtype)


def axial_rope_tables(seq_len):
    rows = seq_len // GRID_W
    row = jnp.repeat(jnp.arange(rows, dtype=jnp.float32), GRID_W)
    col = jnp.tile(jnp.arange(GRID_W, dtype=jnp.float32), rows)
    n_freq = ATT_HD // 4
    inv = ROPE_THETA ** (-jnp.arange(n_freq, dtype=jnp.float32) / n_freq)
    ang_r = row[:, None] * inv[None, :]
    ang_c = col[:, None] * inv[None, :]
    return (jnp.cos(ang_r), jnp.sin(ang_r), jnp.cos(ang_c), jnp.sin(ang_c))


def _rotate(x, cos, sin):
    half = x.shape[-1] // 2
    x1, x2 = x[..., :half], x[..., half:]
    cs, sn = cos[None, :, None, :], sin[None, :, None, :]
    return jnp.concatenate([x1 * cs - x2 * sn, x2 * cs + x1 * sn], axis=-1)


def apply_axial_rope(x, tabs):
    cos_r, sin_r, cos_c, sin_c = tabs
    half = ATT_HD // 2
    xf = x.astype(jnp.float32)
    out = jnp.concatenate([_rotate(xf[..., :half], cos_r, sin_r),
                           _rotate(xf[..., half:], cos_c, sin_c)], axis=-1)
    return out.astype(x.dtype)


def gqa_attend(q, k, v):
    b, lq = q.shape[:2]
    qg = q.reshape(b, lq, ATT_KV_HEADS, ATT_GROUPS, ATT_HD)
    s = jnp.einsum('bqkgd,bskd->bkgqs', qg, k).astype(jnp.float32) * (ATT_HD ** -0.5)
    p = jax.nn.softmax(s, axis=-1).astype(v.dtype)
    o = jnp.einsum('bkgqs,bskd->bqkgd', p, v)
    return o.reshape(b, lq, ATT_HEADS * ATT_HD)


def latent_attention(q, k_lat, v_lat, k_ctx, v_ctx):
    b, l = q.shape[:2]
    k_all = jnp.concatenate([k_ctx, k_lat], axis=1)
    v_all = jnp.concatenate([v_ctx, v_lat], axis=1)
    nb = l // Q_BLOCK
    qb = q.reshape(b, nb, Q_BLOCK, ATT_HEADS, ATT_HD).swapaxes(0, 1)
    o = lax.map(lambda qi: gqa_attend(qi, k_all, v_all), qb)
    return o.swapaxes(0, 1).reshape(b, l, ATT_HEADS * ATT_HD)


def retention_scan(q, k, v, log_gamma, s0):
    dt = v.dtype
    q, k, v = q.astype(jnp.float32), k.astype(jnp.float32), v.astype(jnp.float32)
    log_gamma = log_gamma.astype(jnp.float32)
    b, l, h, _ = q.shape
    dv = v.shape[-1]
    n = l // RET_CHUNK

    def to_chunks(a):
        return a.reshape(b, n, RET_CHUNK, h, a.shape[-1]).transpose(1, 0, 2, 3, 4)

    idx = jnp.arange(RET_CHUNK, dtype=jnp.float32)
    rel = idx[:, None] - idx[None, :]
    lower = rel >= 0
    decay_in = jnp.where(lower[None], jnp.exp(jnp.where(lower, rel, 0.0)[None] * log_gamma[:, None, None]), 0.0)
    q_dec = jnp.exp((idx + 1.0)[:, None] * log_gamma[None, :])
    k_dec = jnp.exp((RET_CHUNK - 1.0 - idx)[:, None] * log_gamma[None, :])
    chunk_dec = jnp.exp(RET_CHUNK * log_gamma)

    def step(state, inp):
        qc, kc, vc = inp
        inner = jnp.einsum('bihd,bjhd->bhij', qc, kc) * decay_in
        o = (jnp.einsum('bhij,bjhe->bihe', inner, vc)
             + jnp.einsum('bihd,bhde->bihe', qc * q_dec[:, :, None], state))
        state = state * chunk_dec[:, None, None] + jnp.einsum('bjhd,bjhe->bhde', kc * k_dec[:, :, None], vc)
        return state, o

    state, o = lax.scan(step, s0.astype(jnp.float32), (to_chunks(q), to_chunks(k), to_chunks(v)))
    o = o.transpose(1, 0, 2, 3, 4).reshape(b, l, h, dv)
    return o.astype(dt), state


def bidir_retention(q, k, v, q_c, k_c, v_c, log_gamma):
    b = q.shape[0]
    s0 = jnp.zeros((b, RET_HEADS, RET_DK, RET_DV), jnp.float32)
    flip = lambda a: a[:, ::-1]
    o_cf, s_f = retention_scan(q_c, k_c, v_c, log_gamma[0], s0)
    o_cb, s_b = retention_scan(flip(q_c), flip(k_c), flip(v_c), log_gamma[1], s0)
    o_lf, _ = retention_scan(q, k, v, log_gamma[0], s_f)
    o_lb, _ = retention_scan(flip(q), flip(k), flip(v), log_gamma[1], s_b)
    return o_lf + flip(o_lb), o_cf + flip(o_cb)


def retention_output(o, gate, g):
    of = o.astype(jnp.float32)
    mu = jnp.mean(of, axis=-1, keepdims=True)
    oc = of - mu
    y = oc * lax.rsqrt(jnp.mean(oc * oc, axis=-1, keepdims=True) + EPS)
    y = y.reshape(o.shape[:2] + (RET_HEADS * RET_DV,)) * g.astype(jnp.float32)
    return y.astype(o.dtype) * jax.nn.silu(gate)


def conv_module(glu_in, dw_w, dw_b, ln_g, ln_b):
    a, bg = jnp.split(glu_in, 2, axis=-1)
    u = dwconv(a * jax.nn.sigmoid(bg), dw_w, dw_b)
    return jax.nn.silu(layer_norm(u, ln_g, ln_b))


def merge_branches(gates, a, att, ret, w_pa, w_pb, w_pc):
    ga, gb, gc = jnp.split(jax.nn.sigmoid(gates), N_BRANCH, axis=-1)
    return ga * (a @ w_pa) + gb * (att @ w_pb) + gc * (ret @ w_pc)


def conv_ffn(h, w_up, dw_w, dw_b, w_down):
    gt, val = jnp.split(h @ w_up, 2, axis=-1)
    gt = dwconv(gt, dw_w, dw_b)
    return (jax.nn.silu(gt) * val) @ w_down


def setup_inputs(seed: int = 0) -> dict:
    key = jax.random.key(seed)
    ks = jax.random.split(key, 32)
    f32 = jnp.float32
    nrm = lambda k, shape, s: jax.random.normal(k, shape, f32) * s
    base_logit = jnp.log(2.0 ** (5.0 + jnp.arange(RET_HEADS, dtype=f32)) - 1.0)
    return {
        "x": nrm(ks[0], (BATCH, SEQ, D_MODEL), 1.0),
        "c": nrm(ks[1], (BATCH, D_MODEL), 1.0),
        "ctx": nrm(ks[2], (BATCH, CTX_LEN, D_MODEL), 1.0),
        "c_ctx": nrm(ks[3], (D_MODEL,), 1.0),
        "w_ada": nrm(ks[4], (DEPTH, D_MODEL, N_MOD * D_MODEL), 0.5 * D_MODEL ** -0.5),
        "b_ada": nrm(ks[5], (DEPTH, N_MOD * D_MODEL), 0.02),
        "norm1_g": 1.0 + nrm(ks[6], (DEPTH, D_MODEL), 0.02),
        "w_in": nrm(ks[7], (DEPTH, D_MODEL, IN_COLS), D_MODEL ** -0.5),
        "b_in": nrm(ks[8], (DEPTH, IN_COLS), 0.02),
        "conv_dw_w": nrm(ks[9], (DEPTH, CONV_K, D_CONV), CONV_K ** -0.5),
        "conv_dw_b": nrm(ks[10], (DEPTH, D_CONV), 0.02),
        "conv_ln_g": 1.0 + nrm(ks[11], (DEPTH, D_CONV), 0.02),
        "conv_ln_b": nrm(ks[12], (DEPTH, D_CONV), 0.02),
        "q_norm_g": 1.0 + nrm(ks[13], (DEPTH, ATT_HD), 0.02),
        "k_norm_g": 1.0 + nrm(ks[14], (DEPTH, ATT_HD), 0.02),
        "ret_decay_logit": base_logit[None, None, :] + nrm(ks[15], (DEPTH, 2, RET_HEADS), 0.1),
        "ret_gn_g": 1.0 + nrm(ks[16], (DEPTH, RET_HEADS * RET_DV), 0.02),
        "w_pa": nrm(ks[17], (DEPTH, D_CONV, D_MODEL), D_CONV ** -0.5),
        "w_pb": nrm(ks[18], (DEPTH, ATT_HEADS * ATT_HD, D_MODEL), (ATT_HEADS * ATT_HD) ** -0.5),
        "w_pc": nrm(ks[19], (DEPTH, RET_HEADS * RET_DV, D_MODEL), (RET_HEADS * RET_DV) ** -0.5),
        "w_out": nrm(ks[20], (DEPTH, D_MODEL, D_MODEL), D_MODEL ** -0.5),
        "norm2_g": 1.0 + nrm(ks[21], (DEPTH, D_MODEL), 0.02),
        "w_up": nrm(ks[22], (DEPTH, D_MODEL, 2 * D_FF), D_MODEL ** -0.5),
        "ffn_dw_w": nrm(ks[23], (DEPTH, FFN_K, D_FF), FFN_K ** -0.5),
        "ffn_dw_b": nrm(ks[24], (DEPTH, D_FF), 0.02),
        "w_down": nrm(ks[25], (DEPTH, D_FF, D_MODEL), D_FF ** -0.5),
        "final_norm_g": 1.0 + nrm(ks[26], (D_MODEL,), 0.02),
    }


def reference(x, c, ctx, c_ctx, w_ada, b_ada, norm1_g, w_in, b_in, conv_dw_w, conv_dw_b,
              conv_ln_g, conv_ln_b, q_norm_g, k_norm_g, ret_decay_logit, ret_gn_g,
              w_pa, w_pb, w_pc, w_out, norm2_g, w_up, ffn_dw_w, ffn_dw_b, w_down, final_norm_g):
    seq_len = x.shape[1]
    rope = axial_rope_tables(seq_len)
    xc = ctx
    silu_c = jax.nn.silu(c)
    silu_cc = jax.nn.silu(c_ctx)
    for l in range(DEPTH):
        last = l == DEPTH - 1
        mod_l = (silu_c @ w_ada[l] + b_ada[l])[:, None, :]
        mod_c = silu_cc @ w_ada[l] + b_ada[l]
        sh1, sc1, g1, sh2, sc2, g2 = jnp.split(mod_l, N_MOD, axis=-1)
        csh1, csc1, cg1, csh2, csc2, cg2 = jnp.split(mod_c, N_MOD, axis=-1)

        h = modulate(rms_norm(x, norm1_g[l]), sh1, sc1)
        hc = modulate(rms_norm(xc, norm1_g[l]), csh1, csc1)
        glu, q, k, v, rq, rk, rv, rg, gates = jnp.split(h @ w_in[l] + b_in[l], IN_SPLITS, axis=-1)
        glu_c, q_c, k_c, v_c, rq_c, rk_c, rv_c, rg_c, gates_c = jnp.split(hc @ w_in[l] + b_in[l], IN_SPLITS, axis=-1)

        a = conv_module(glu, conv_dw_w[l], conv_dw_b[l], conv_ln_g[l], conv_ln_b[l])

        q = apply_axial_rope(rms_norm(heads(q, ATT_HEADS), q_norm_g[l]), rope)
        k = apply_axial_rope(rms_norm(heads(k, ATT_KV_HEADS), k_norm_g[l]), rope)
        v = heads(v, ATT_KV_HEADS)
        k_c = rms_norm(heads(k_c, ATT_KV_HEADS), k_norm_g[l])
        v_c = heads(v_c, ATT_KV_HEADS)
        att = latent_attention(q, k, v, k_c, v_c)

        log_gamma = jax.nn.log_sigmoid(ret_decay_logit[l].astype(jnp.float32))
        scale_k = RET_DK ** -0.5
        ret, ret_c = bidir_retention(heads(rq, RET_HEADS), heads(rk, RET_HEADS) * scale_k, heads(rv, RET_HEADS),
                                     heads(rq_c, RET_HEADS), heads(rk_c, RET_HEADS) * scale_k, heads(rv_c, RET_HEADS),
                                     log_gamma)
        ret = retention_output(ret, rg, ret_gn_g[l])

        y = merge_branches(gates, a, att, ret, w_pa[l], w_pb[l], w_pc[l]) @ w_out[l]
        x = x + g1 * y
        if not last:
            a_c = conv_module(glu_c, conv_dw_w[l], conv_dw_b[l], conv_ln_g[l], conv_ln_b[l])
            q_c = rms_norm(heads(q_c, ATT_HEADS), q_norm_g[l])
            att_c = gqa_attend(q_c, k_c, v_c)
            ret_c = retention_output(ret_c, rg_c, ret_gn_g[l])
            y_c = merge_branches(gates_c, a_c, att_c, ret_c, w_pa[l], w_pb[l], w_pc[l]) @ w_out[l]
            xc = xc + cg1 * y_c

        x = x + g2 * conv_ffn(modulate(rms_norm(x, norm2_g[l]), sh2, sc2),
                              w_up[l], ffn_dw_w[l], ffn_dw_b[l], w_down[l])
        if not last:
            xc = xc + cg2 * conv_ffn(modulate(rms_norm(xc, norm2_g[l]), csh2, csc2),
                                     w_up[l], ffn_dw_w[l], ffn_dw_b[l], w_down[l])
    return rms_norm(x, final_norm_g)
```

```python
import numpy as np
STOP = 9.0
from contextlib import ExitStack
import concourse.bass as bass
import concourse.mybir as mybir
from concourse.bass_utils import run_bass_kernel_spmd

F32 = mybir.dt.float32
BF16 = mybir.dt.bfloat16
ALU = mybir.AluOpType
AF = mybir.ActivationFunctionType
AX = mybir.AxisListType

D = 1024; DEPTH = 4; SEQ = 8192; CTX = 256; NLAT = 2048
XW = 2364; CTX0 = 2093
EPS = 1e-6
NPC = 372; NBROW = 1664; NCST = 914


class T:
    def __init__(self, ap):
        self.ap = ap; self.lw = {}; self.rd = {}


class Prog:
    NDS = 10

    def __init__(self):
        self.ops = []
        self.dma_rr = {}

    def op(self, eng, fn, r=(), w=(), dma=False, cc=False):
        idx = len(self.ops)
        if cc:
            cls = ('cc',)
        elif dma:
            k = self.dma_rr.get(eng, 0); self.dma_rr[eng] = k + 1
            cls = ('dma', eng, k % self.NDS)
        else:
            cls = (eng,)
        deps = {}

        def add(c, i):
            if i is not None and deps.get(c, -1) < i:
                deps[c] = i
        for b in r:
            for c, i in b.lw.items(): add(c, i)
        for b in w:
            for c, i in b.lw.items(): add(c, i)
            for c, i in b.rd.items(): add(c, i)
        for b in r: b.rd[cls] = idx
        for b in w: b.lw[cls] = idx; b.rd = {}
        self.ops.append(dict(eng=eng, fn=fn, cls=cls, deps=deps, dma=dma, cc=cc))
        return idx

    def emit(self, nc, es):
        ops = self.ops
        last_in_cls = {}
        for i, o in enumerate(ops):
            if o['dma'] or o['cc']:
                p = last_in_cls.get(o['cls'])
                if p is not None: o['deps'][o['cls']] = max(o['deps'].get(o['cls'], -1), p)
                last_in_cls[o['cls']] = i
        needed = set()
        for o in ops:
            for c, i in o['deps'].items():
                if c == ('pe',) and o['eng'] == 'pe' and not o['dma']:
                    continue
                needed.add(i)
        sems = {}
        classes = sorted({o['cls'] for o in ops}, key=str)
        for c in classes:
            sems[c] = es.enter_context(nc.semaphore("s_" + "_".join(str(x) for x in c)))
        cnt = {c: 0 for c in classes}
        tok = {}
        for i, o in enumerate(ops):
            if i in needed or o['dma'] or o['cc']:
                inc = 16 if o['dma'] else 1
                cnt[o['cls']] += inc
                tok[i] = cnt[o['cls']]
                o['inc'] = inc
            else:
                o['inc'] = 0
        engs = ['pe', 'act', 'dve', 'pool', 'sp']
        streams = {e: [o for o in ops if o['eng'] == e] for e in engs}
        idx_of = {id(o): i for i, o in enumerate(ops)}
        block = es.enter_context(nc.Block())

        def run(ename, e):
            waited = {}
            for o in streams[ename]:
                for c, i in sorted(o['deps'].items(), key=lambda kv: str(kv[0])):
                    if c == ('pe',) and ename == 'pe' and not o['dma']:
                        continue
                    v = tok[i]
                    if waited.get(c, 0) < v:
                        e.wait_ge(sems[c], v); waited[c] = v
                ins = o['fn'](e)
                if o['inc']:
                    ins.then_inc(sems[o['cls']], o['inc'])
            if ename == 'sp':
                for c in classes:
                    if cnt[c] > 0 and (c[0] == 'dma'):
                        e.wait_ge(sems[c], cnt[c])

        @block.tensor
        def _(e): run('pe', e)

        @block.scalar
        def _(e): run('act', e)

        @block.vector
        def _(e): run('dve', e)

        @block.gpsimd
        def _(e): run('pool', e)

        @block.sync
        def _(e): run('sp', e)


class Rot:
    def __init__(self, items): self.items = items; self.i = 0

    def get(self):
        b = self.items[self.i % len(self.items)]; self.i += 1
        return b


def build(depth=DEPTH):
    nc = bass.Bass("TRN2", target_bir_lowering=False)
    P = Prog()
    es = ExitStack()

    def din(name, shape, dt=F32):
        return nc.dram_tensor(name, shape, dt, kind="ExternalInput").ap()

    def dint(name, shape, dt):
        return nc.dram_tensor(name, shape, dt, kind="Internal").ap()

    xT_in = din("xT", [128, 8, XW])
    cT_in = din("cT", [128, 8, 2])
    rope_in = din("rope", [128, 2, 5, 512])
    rkt_in = din("rkt", [128, 32])
    cst_in = din("cst", [128, NCST])
    pcol_in = din("pcol", [depth, 128, NPC])
    brow_in = din("brow", [depth, 1, NBROW])
    fng_in = din("fng", [128, 8])
    w_ada = din("w_ada", [depth, D, 6144]); w_in = din("w_in", [depth, D, 6912])
    w_pa = din("w_pa", [depth, 512, D]); w_pb = din("w_pb", [depth, 512, D]); w_pc = din("w_pc", [depth, 512, D])
    w_out = din("w_out", [depth, D, D]); w_up = din("w_up", [depth, D, 5632]); w_down = din("w_down", [depth, 2816, D])
    out_d = nc.dram_tensor("out", [128, 8, NLAT], F32, kind="ExternalOutput").ap()
    DBG = False
    if DBG:
        dbg16 = nc.dram_tensor("dbg16", [40, 128, 512], BF16, kind="ExternalOutput").ap()
        dbg32 = nc.dram_tensor("dbg32", [12, 128, 512], F32, kind="ExternalOutput").ap()

    def dump16(i, t):
        if DBG: P.op('sp', lambda e: e.dma_start(out=dbg16[i], in_=t.ap), r=[t], dma=True)

    def dump32(i, ap, trs, n=512):
        if DBG: P.op('sp', lambda e: e.dma_start(out=dbg32[i][:, 0:n], in_=ap), r=trs, dma=True)

    wada16 = dint("wada16", [depth, D, 6144], BF16); win16 = dint("win16", [depth, D, 6912], BF16)
    wpa16 = dint("wpa16", [depth, 512, D], BF16); wpb16 = dint("wpb16", [depth, 512, D], BF16)
    wpc16 = dint("wpc16", [depth, 512, D], BF16); wout16 = dint("wout16", [depth, D, D], BF16)
    wup16 = dint("wup16", [depth, D, 5632], BF16); wdn16 = dint("wdn16", [depth, 2816, D], BF16)
    EXC = 2048 + 16 * 128
    ex1_in = [dint(f"ex1i{i}", [128, EXC], BF16) for i in range(2)]
    ex1_out = [dint(f"ex1o{i}", [512, EXC], BF16) for i in range(2)]
    cxkv = dint("cxkv", [128, 256 + 2 * 128], BF16)
    ex2_in = [dint(f"ex2i{i}", [128, 1024], F32) for i in range(2)]
    ex2_out = [dint(f"ex2o{i}", [512, 1024], F32) for i in range(2)]
    ex3_in = [dint(f"ex3i{i}", [128, 240], F32) for i in range(2)]
    ex3_out = [dint(f"ex3o{i}", [512, 240], F32) for i in range(2)]
    T_ex1i = [T(a) for a in ex1_in]; T_ex1o = [T(a) for a in ex1_out]; T_cx = T(cxkv)
    T_ex2i = [T(a) for a in ex2_in]; T_ex2o = [T(a) for a in ex2_out]
    T_ex3i = [T(a) for a in ex3_in]; T_ex3o = [T(a) for a in ex3_out]
    Tw = {}

    def tw(name, l):
        if (name, l) not in Tw: Tw[(name, l)] = T(None)
        return Tw[(name, l)]
    RG = [[0, 1, 2, 3], [4, 5, 6, 7]]

    def sbt(name, shape, dt):
        t = es.enter_context(nc.sbuf_tensor("sb_" + name, shape, dt))
        return T(t[:])

    xT = sbt("xT", [128, 8, XW], F32)
    xTk = [T(None) for _ in range(8)]
    xTh = T(None)
    cst = sbt("cst", [128, NCST], F32)
    rkt = sbt("rkt", [128, 32], F32)
    pc = sbt("pc", [128, NPC], F32)
    brow = sbt("brow", [1, NBROW], BF16)
    fng = sbt("fng", [128, 8], F32)
    modTs = [sbt(f"modT{i}", [128, 48, 2], F32) for i in range(2)]
    badas = [sbt(f"bada{i}", [128, 48], F32) for i in range(2)]
    modT = modTs[0]
    mder = sbt("mder", [128, 4, 8, 2], F32)
    scT = sbt("scT", [128, 8, 2], BF16)
    cTs = sbt("cTs", [128, 8, 2], F32)
    onesb = sbt("onesb", [128, 128], BF16)
    blk64 = sbt("blk64", [128, 128], BF16)
    ident = sbt("ident", [128, 128], BF16)
    rtb = sbt("rtb", [128, 128], BF16)
    onesf = sbt("onesf", [128, 128], F32)
    lgt = sbt("lgt", [128, 24], F32)
    DTt = sbt("DTt", [128, 8, 128], BF16)
    RKL = sbt("RKL", [128, 512], BF16); RKH = sbt("RKH", [128, 512], BF16)
    BMASK = sbt("BMASK", [128, 512], BF16)
    QFt = sbt("QFt", [128, 2, 4, 128], F32)
    KFt = sbt("KFt", [128, 2, 8], F32)
    CDt = sbt("CDt", [128, 2, 4], F32)
    DECt = sbt("DECt", [128, 2, 16, 4], F32)
    coef = sbt("coef", [128, 10, 4], F32)
    smallf = sbt("smallf", [128, 64], F32)
    Wt = [sbt(f"W{i}", [128, 4096], BF16) for i in range(2)]
    Wrot = Rot(Wt)
    GT = 4
    KTG0 = Rot([sbt(f"KTG0{i}", [128, GT * 128], BF16) for i in range(2)])
    KTG1 = Rot([sbt(f"KTG1{i}", [128, GT * 128], BF16) for i in range(2)])
    VG0 = Rot([sbt(f"VG0{i}", [128, GT, 128], BF16) for i in range(2)])
    VG1 = Rot([sbt(f"VG1{i}", [128, GT, 128], BF16) for i in range(2)])
    SBL = Rot([sbt(f"SBL{i}", [128, 512], BF16) for i in range(2)])
    sbscr = dint("sbscr", [18, 128, 512], BF16)
    T_sbs = [T(None) for _ in range(18)]
    NG = 39
    Gall = [sbt(f"G{i}", [128, 512], BF16) for i in range(NG)]
    gfree = list(Gall)

    def galloc(n):
        r = gfree[:n]; del gfree[:n]
        assert len(r) == n, "G pool exhausted"
        return r

    def gfree_(lst): gfree.extend(lst)
    FT = Rot([sbt(f"FT{i}", [128, 514], F32) for i in range(4)])
    STG = [sbt(f"STG{i}", [128, 512], BF16) for i in range(3)]
    D0 = sbt("D0", [128, 512], F32); D1 = sbt("D1", [128, 512], F32); D2 = sbt("D2", [128, 512], F32)
    ACC = [sbt(f"ACC{i}", [128, 512], F32) for i in range(4)]
    SQ = Rot([sbt(f"SQ{i}", [128, 512], BF16) for i in range(3)])
    Sf = sbt("Sf", [128, 512], F32); Sf16 = sbt("Sf16", [128, 512], BF16)
    Sbk = sbt("Sbk", [128, 512], F32); Tfk = sbt("Tfk", [128, 512], F32)
    Ssf = sbt("Ssf", [128, 512], F32); Ssb = sbt("Ssb", [128, 512], F32)
    DG = Rot([sbt(f"DG{i}", [128, 128], BF16) for i in range(4)])
    uh = sbt("uh", [128, 4, 150], BF16)
    gth = sbt("gth", [128, 22, 10], F32)
    ubuf = [sbt(f"ubuf{i}", [128, 542], BF16) for i in range(4)]
    pst = [T(es.enter_context(nc.psum_tensor(f"ps{i}", [128, 512], F32))[:]) for i in range(8)]
    PSL = pst[0:4]
    PSR4 = Rot(pst[4:8]); PSR8 = Rot(pst[0:8])
    PSRh = [PSR4]

    class _PSR:
        def get(self): return PSRh[0].get()
    PSR = _PSR()

    def A(t): return t.ap

    def dma(eng, out_ap, in_ap, r=(), w=()):
        return P.op(eng, lambda e: e.dma_start(out=out_ap, in_=in_ap), r=r, w=w, dma=True)

    def mm(ps_ap, lhsT, rhs, start, stop, r, w):
        return P.op('pe', lambda e: e.matmul(ps_ap, lhsT, rhs, start=start, stop=stop), r=r, w=w)

    def act(out, in_, func, r, w, bias=None, scale=None, eng='act'):
        kw = {}
        if bias is not None: kw['bias'] = bias
        if scale is not None: kw['scale'] = scale
        return P.op('act', lambda e: e.activation(out, in_, func, **kw), r=r, w=w)

    def tt(eng, out, a, b, op, r, w):
        if eng == 'pool': eng = 'dve'
        return P.op(eng, lambda e: e.tensor_tensor(out, a, b, op), r=r, w=w)

    def rsq(out, in_, scale, eps, r, w):
        P.op('dve', lambda e: e.tensor_scalar(out, in_, scale, eps, ALU.mult, ALU.add), r=r, w=w)
        P.op('act', lambda e: e.activation(out, out, AF.Ln), r=w, w=w)
        P.op('act', lambda e: e.activation(out, out, AF.Exp, scale=-0.5), r=w, w=w)

    def ts(eng, out, a, s1, s2, op0, op1, r, w):
        eng = 'dve'
        if s2 is None:
            return P.op(eng, lambda e: e.tensor_scalar(out, a, s1, None, op0), r=r, w=w)
        return P.op(eng, lambda e: e.tensor_scalar(out, a, s1, s2, op0, op1), r=r, w=w)

    def stt(eng, out, a, s, b, op0, op1, r, w):
        eng = 'dve'
        return P.op(eng, lambda e: e.scalar_tensor_tensor(out, a, s, b, op0, op1), r=r, w=w)

    def cp(eng, out, in_, r, w):
        if eng == 'pool': eng = 'dve'
        if eng == 'act':
            return P.op('act', lambda e: e.activation(out, in_, AF.Copy), r=r, w=w)
        return P.op(eng, lambda e: e.tensor_copy(out, in_), r=r, w=w)

    def mset(eng, ap, val, w):
        return P.op(eng, lambda e: e.memset(ap, val), w=w)

    xa = A(xT)

    def xk(k): return [xT, xTk[k]]

    dma('sp', xa, xT_in, w=[xT] + xTk + [xTh])
    dma('sp', A(cst), cst_in, w=[cst])
    dma('sp', A(rkt), rkt_in, w=[rkt])
    dma('sp', A(fng), fng_in, w=[fng])
    dma('sp', A(cTs), cT_in, w=[cTs])
    for l in range(depth if (STOP > 0 and 0) else 0):
        def cast(nm, dst, src, rows, rstep):
            for r0 in range(0, rows, rstep):
                dma('pool', dst[r0:r0 + rstep, :], src[r0:r0 + rstep, :], w=[tw(nm, l)])
        cast('ada', wada16[l], w_ada[l], D, 256)
        for r0 in range(0, D, 256):
            rs = slice(r0, r0 + 256)
            dma('pool', win16[l][rs, 0:1024], w_in[l][rs, 0:1024], w=[tw('in', l)])
            for half in range(2):
                dma('pool', win16[l][rs, 1024:1536].rearrange("k (c h d) -> k h c d", c=4, h=2)[:, half],
                    w_in[l][rs, 1024:1536].rearrange("k (h c d) -> k h c d", h=2, c=4)[:, half], w=[tw('in', l)])
            dma('pool', win16[l][rs, 1536:6912], w_in[l][rs, 1536:6912], w=[tw('in', l)])
        cast('pa', wpa16[l], w_pa[l], 512, 512)
        for half in range(2):
            dma('pool', wpb16[l].rearrange("(c h d) n -> h c d n", c=4, h=2)[half],
                w_pb[l].rearrange("(h c d) n -> h c d n", h=2, c=4)[half], w=[tw('pb', l)])
        cast('pc', wpc16[l], w_pc[l], 512, 512)
        cast('out', wout16[l], w_out[l], D, 512)
        cast('up', wup16[l], w_up[l], D, 256)
        cast('dn', wdn16[l], w_down[l], 2816, 704)

    c_ = A(cst)
    RELA = c_[:, 0:128]; RELB = c_[:, 128:256]; EYE = c_[:, 256:384]
    IOTA1 = c_[:, 384:512]; IOTAC = c_[:, 512:640]; REV = c_[:, 640:641]; JCOL = c_[:, 641:642]
    EXPO = c_[:, 642:658]; RTF = c_[:, 658:786]; SWAP = c_[:, 786:914]
    rk_ = A(rkt)
    mset('dve', A(onesb), 1.0, w=[onesb]); mset('dve', A(onesf), 1.0, w=[onesf])
    mset('dve', A(blk64), 0.0, w=[blk64])
    mset('dve', A(blk64)[0:64, 0:64], 1.0, w=[blk64]); mset('dve', A(blk64)[64:128, 64:128], 1.0, w=[blk64])
    cp('dve', A(ident), EYE, r=[cst], w=[ident]); cp('dve', A(rtb), RTF, r=[cst], w=[rtb])
    for t in KTG0.items + KTG1.items + [RKL, RKH, BMASK]: mset('dve', A(t), 0.0, w=[t])
    for c in range(4):
        mset('dve', A(BMASK)[0:64, c * 128:c * 128 + 64], 1.0, w=[BMASK])
        mset('dve', A(BMASK)[64:128, c * 128 + 64:c * 128 + 128], 1.0, w=[BMASK])
    for t in VG0.items: mset('dve', A(t), 1.0, w=[t])
    for t in VG1.items: mset('dve', A(t), 1.0, w=[t])
    act(A(scT), A(cTs), AF.Silu, r=[cTs], w=[scT])

    BLK = [dict(ws=512 * j, mc=15 + 512 * j, N=512, lat=True, j=j) for j in range(4)]
    BLK.append(dict(ws=2078, mc=CTX0, N=256, lat=False, j=4))

    WE = ['pool']

    def wload(src_ap, kc, n, wtr, eng=None):
        eng = eng or WE[0]
        wt = Wrot.get()
        dst = A(wt)[:, 0:kc * n].rearrange("p (k n) -> p k n", k=kc)
        dma(eng, dst, src_ap.rearrange("(k p) n -> p k n", p=128), r=[wtr], w=[wt])
        return wt, dst

    def normrope(ps_in, bias_col, gcol, gscale, N, out_ap, out_tr):
        xb = FT.get()
        act(A(xb)[:, :N], A(ps_in)[:, :N], AF.Identity, r=[ps_in, pc], w=[xb], bias=bias_col)
        sq = SQ.get()
        act(A(sq)[:, :N], A(xb)[:, :N], AF.Square, r=[xb], w=[sq])
        ps2 = PSR.get()
        mm(A(ps2)[:, :N], A(blk64), A(sq)[:, :N], True, True, r=[blk64, sq], w=[ps2])
        Rr = FT.get()
        rsq(A(Rr)[:, :N], A(ps2)[:, :N], 1.0, 64 * EPS, [ps2], [Rr])
        ts('dve', A(xb)[:, :N], A(xb)[:, :N], gcol, gscale, ALU.mult, ALU.mult, r=[xb, pc], w=[xb])
        xg = SQ.get()
        cp('act', A(xg)[:, :N], A(xb)[:, :N], r=[xb], w=[xg])
        ps3 = PSR.get()
        mm(A(ps3)[:, :N], A(rtb), A(xg)[:, :N], True, True, r=[rtb, xg], w=[ps3])
        t1 = FT.get(); t2 = FT.get()
        tt('pool', A(t1)[:, :N], A(xb)[:, :N], A(D1)[:, :N], ALU.mult, r=[xb, D1], w=[t1])
        tt('dve', A(t2)[:, :N], A(ps3)[:, :N], A(D2)[:, :N], ALU.mult, r=[ps3, D2], w=[t2])
        tt('dve', A(t1)[:, :N], A(t1)[:, :N], A(t2)[:, :N], ALU.add, r=[t1, t2], w=[t1])
        tt('dve', out_ap, A(t1)[:, :N], A(Rr)[:, :N], ALU.mult, r=[t1, Rr], w=out_tr)

    def make_h(xsrc, rdeps, N, Gap, SHap, outs):
        ps = PSR.get()
        for k in range(8):
            sq = SQ.get()
            act(A(sq)[:, :N], xsrc[k], AF.Square, r=rdeps[k], w=[sq])
            mm(A(ps)[:, :N], A(onesb), A(sq)[:, :N], k == 0, k == 7, r=[onesb, sq], w=[ps])
        R = D0
        rsq(A(R)[:, :N], A(ps)[:, :N], 1.0, D * EPS, [ps], [R])
        for k in range(8):
            t = FT.get()
            stt('dve', A(t)[:, :N], xsrc[k], Gap(k), A(R)[:, :N], ALU.mult, ALU.mult, r=rdeps[k] + [R, mder], w=[t])
            oap, otr = outs[k]
            act(oap, A(t)[:, :N], AF.Identity, r=[t, MTH[0]], w=otr, bias=SHap(k))

    def exch_xhalo(par):
        exch_issue(par); exch_finish(par)

    def exch_issue(par):
        o = ex3_in[par].rearrange("p (k s) -> p k s", k=8)
        dma('sp', o[:, :, 0:15], xa[:, :, 15:30], r=[xT] + xTk, w=[T_ex3i[par]])
        dma('sp', o[:, :, 15:30], xa[:, :, 15 + 2033:15 + 2048], r=[xT] + xTk, w=[T_ex3i[par]])
        P.op('pool', lambda e: e.collective_compute("AllGather", ALU.bypass, replica_groups=RG,
                                                    ins=[ex3_in[par]], outs=[ex3_out[par]]),
             r=[T_ex3i[par]], w=[T_ex3o[par]], cc=True)

    def exch_finish(par):
        L = xa[:, :, 0:15]; Rr = xa[:, :, 2063:2078]
        for q in range(4):
            xq = FT.get()
            dma('sp', A(xq)[:, 0:240], ex3_out[par][q * 128:(q + 1) * 128, :], r=[T_ex3o[par]], w=[xq])
            xr = A(xq)[:, 0:240].rearrange("p (k s) -> p k s", k=8)
            if q == 0:
                ts('dve', L, xr[:, :, 15:30], rk_[:, 0:1], None, ALU.mult, None, r=[xq, rkt, xTh], w=[xTh])
                ts('dve', Rr, xr[:, :, 0:15], rk_[:, 4:5], None, ALU.mult, None, r=[xq, rkt, xTh], w=[xTh])
            else:
                stt('dve', L, xr[:, :, 15:30], rk_[:, q:q + 1], L, ALU.mult, ALU.add, r=[xq, rkt, xTh], w=[xTh])
                stt('dve', Rr, xr[:, :, 0:15], rk_[:, 4 + q:5 + q], Rr, ALU.mult, ALU.add, r=[xq, rkt, xTh], w=[xTh])

    pca = A(pc)

    def bcol(j): return pca[:, 64 + j:65 + j]

    md = A(mder)
    mo = A(modT)
    MTH = [modT]

    def cvt_items(l):
        items = []

        def plain(nm, dst, src, rows, cols, c_lo=0, c_hi=None):
            c_hi = cols if c_hi is None else c_hi
            for k in range(rows // 128):
                for c0 in range(c_lo, c_hi, 512):
                    n = min(512, c_hi - c0)
                    rs = slice(k * 128, (k + 1) * 128)
                    items.append(([(lambda sl, n=n: sl[:, 0:n], src[rs, c0:c0 + n])], dst[rs, c0:c0 + n], n, tw(nm, l)))
        plain('in', win16[l], w_in[l], D, 6912, 0, 1024)
        for k in range(8):
            rs = slice(k * 128, (k + 1) * 128)
            lds = []
            for c in range(4):
                for hf in range(2):
                    h_ = hf * 4 + c
                    lds.append((lambda sl, o=c * 128 + hf * 64: sl[:, o:o + 64], w_in[l][rs, 1024 + h_ * 64:1024 + (h_ + 1) * 64]))
            items.append((lds, win16[l][rs, 1024:1536], 512, tw('in', l)))
        plain('in', win16[l], w_in[l], D, 6912, 1536, 6912)
        plain('pa', wpa16[l], w_pa[l], 512, D)
        for c in range(4):
            for c0 in (0, 512):
                lds = []
                for hf in range(2):
                    h_ = hf * 4 + c
                    lds.append((lambda sl, hf=hf: sl[hf * 64:(hf + 1) * 64, 0:512], w_pb[l][h_ * 64:(h_ + 1) * 64, c0:c0 + 512]))
                items.append((lds, wpb16[l][c * 128:(c + 1) * 128, c0:c0 + 512], 512, tw('pb', l)))
        plain('pc', wpc16[l], w_pc[l], 512, D)
        plain('out', wout16[l], w_out[l], D, D)
        plain('up', wup16[l], w_up[l], D, 5632)
        plain('dn', wdn16[l], w_down[l], 2816, D)
        return items

    class Cvt:
        def __init__(self): self.q = []; self.pending = []; self.stg = None; self.i = 0

        def start(self, l, stg=None):
            self.q = cvt_items(l); self.stg = stg or STG; self.i = 0; self.pending = []; self.own = stg

        def step(self, nitems):
            for _ in range(nitems):
                if not self.q: break
                lds, dst, n, trk = self.q.pop(0)
                slot = self.stg[self.i % len(self.stg)]; self.i += 1
                for fn, src in lds:
                    dma('pool', fn(A(slot)), src, w=[slot])
                self.pending.append((slot, dst, n, trk))
                if len(self.pending) > len(self.stg) - 1: self.flush(len(self.stg) - 1)

        def flush(self, keep):
            while len(self.pending) > keep:
                slot, dst, n, trk = self.pending.pop(0)
                dma('pool', dst, A(slot)[:, 0:n], r=[slot], w=[trk])

        def finish(self):
            self.step(10 ** 9); self.flush(0)
            if self.own: gfree_(self.own); self.own = None
    CV = Cvt()

    for l in range(depth):
        par = l % 2
        WIN, WADA, WPA, WPB, WPC, WOUT, WUP, WDN = (win16[l], wada16[l], wpa16[l], wpb16[l], wpc16[l], wout16[l], wup16[l], wdn16[l])
        WENG = 'sp'
        WE[0] = WENG
        exch_issue(par)
        if l == 0:
            CV.start(0)
        elif l + 1 < depth:
            CV.start(l + 1)
            n_cv = (len(CV.q) + 9) // 10
        dma('sp', pca, pcol_in[l], w=[pc])
        dma('pool', A(brow), brow_in[l], w=[brow])
        def emit_adaln(ll):
            mt_ = modTs[ll % 2]; ba_ = badas[ll % 2]
            dma('sp', A(ba_), pcol_in[ll][:, 0:48], w=[ba_])
            for pcs in range(12):
                wt, wv = wload(w_ada[ll][:, pcs * 512:(pcs + 1) * 512], 8, 512, tw('ada_unused', ll), eng='pool')
                for jj in range(4):
                    j_ = pcs * 4 + jj
                    ps = PSR.get()
                    for k in range(8):
                        mm(A(ps)[:, 0:2], wv[:, k, jj * 128:(jj + 1) * 128], A(scT)[:, k, :], k == 0, k == 7,
                           r=[wt, scT], w=[ps])
                    act(A(mt_)[:, j_, :], A(ps)[:, 0:2], AF.Identity, r=[ps, ba_], w=[mt_], bias=A(ba_)[:, j_:j_ + 1])
        if l == 0: emit_adaln(0)
        modT = modTs[l % 2]; mo = A(modT); MTH[0] = modT
        for (gi, ncol, sccol) in ((0, 48, 8), (1, 56, 32)):
            ts('dve', md[:, gi], mo[:, sccol:sccol + 8, :], 1.0, 32.0, ALU.add, ALU.mult, r=[modT], w=[mder])
            tt('dve', md[:, gi], md[:, gi], pca[:, ncol:ncol + 8].unsqueeze(2).to_broadcast([128, 8, 2]), ALU.mult,
               r=[mder, pc], w=[mder])
        lg = A(lgt)
        act(lg, pca[:, 344:368], AF.Exp, r=[pc], w=[lgt], scale=-1.0)
        act(lg, lg, AF.Ln, r=[lgt], w=[lgt], bias=1.0)
        ts('dve', lg, lg, -1.0, None, ALU.mult, None, r=[lgt], w=[lgt])
        sf = A(smallf)
        for h in range(8):
            t1 = FT.get()
            ts('dve', A(t1)[:, 0:128], RELA, lg[:, 8 + h:9 + h], None, ALU.mult, None, r=[cst, lgt], w=[t1])
            stt('dve', A(t1)[:, 0:128], RELB, lg[:, 16 + h:17 + h], A(t1)[:, 0:128], ALU.mult, ALU.add, r=[cst, lgt, t1], w=[t1])
            act(A(t1)[:, 0:128], A(t1)[:, 0:128], AF.Exp, r=[t1], w=[t1])
            tt('dve', A(DTt)[:, h, :], A(t1)[:, 0:128], EYE, ALU.add, r=[t1, cst], w=[DTt])
        for d_ in range(2):
            for c in range(4):
                act(A(QFt)[:, d_, c, :], IOTA1 if d_ == 0 else IOTAC, AF.Exp, r=[cst, lgt], w=[QFt],
                    scale=lg[:, d_ * 4 + c:d_ * 4 + c + 1])
            ts('dve', A(KFt)[:, d_, :], lg[:, 8 + 8 * d_:16 + 8 * d_], REV if d_ == 0 else JCOL, None, ALU.mult, None,
               r=[lgt, cst], w=[KFt])
            act(A(KFt)[:, d_, :], A(KFt)[:, d_, :], AF.Exp, r=[KFt], w=[KFt])
            act(A(CDt)[:, d_, :], lg[:, d_ * 4:d_ * 4 + 4], AF.Exp, r=[lgt], w=[CDt], scale=128.0)
            tt('dve', A(DECt)[:, d_], lg[:, d_ * 4:d_ * 4 + 4].unsqueeze(1).to_broadcast([128, 16, 4]),
               EXPO.unsqueeze(2).to_broadcast([128, 16, 4]), ALU.mult, r=[lgt, cst], w=[DECt])
            act(A(DECt)[:, d_], A(DECt)[:, d_], AF.Exp, r=[DECt], w=[DECt])
            eo = 10 if d_ == 0 else 19
            for q in range(5):
                ecol = rk_[:, eo + q:eo + q + 1] if q < 4 else rk_[:, eo + 8:eo + 9]
                ci = d_ * 5 + q
                ts('dve', A(coef)[:, ci, :], lg[:, d_ * 4:d_ * 4 + 4], ecol, None, ALU.mult, None, r=[lgt, rkt], w=[coef])
                act(A(coef)[:, ci, :], A(coef)[:, ci, :], AF.Exp, r=[coef], w=[coef])
                if q < 4:
                    ts('dve', A(coef)[:, ci, :], A(coef)[:, ci, :], rk_[:, eo + 4 + q:eo + 5 + q], None, ALU.mult, None,
                       r=[coef, rkt], w=[coef])

        G1 = lambda t: (lambda k: md[:, 0, k, t:t + 1])
        SH1 = lambda t: (lambda k: mo[:, k, t:t + 1])
        G2 = lambda t: (lambda k: md[:, 1, k, t:t + 1])
        SH2 = lambda t: (lambda k: mo[:, 24 + k, t:t + 1])

        if STOP <= 1: break
        if l == 0:
            CV.step(112); CV.flush(0)

        if STOP <= 2: break
        v4 = lambda ap: ap.rearrange("p (c n) -> p c n", c=4)
        mset('dve', A(Sbk), 0.0, w=[Sbk]); mset('dve', A(Tfk), 0.0, w=[Tfk])
        order = [BLK[4], BLK[3], BLK[2], BLK[1], BLK[0]]
        for b in order:
            j = b['j']; N = b['N']; mc = b['mc']; lat = b['lat']; tcol = 0 if lat else 1
            nt = N // 128
            hT = galloc(8)
            make_h([xa[:, k, mc:mc + N] for k in range(8)], [xk(k) for k in range(8)], N, G1(tcol), SH1(tcol),
                   [(A(hT[k])[:, :N], [hT[k]]) for k in range(8)])
            wk, wkv = wload(WIN[:, 1536:1792], 8, 256, tw('in', l))
            dma('sp', A(D1)[:, :N], rope_in[:, 0, j, 0:N], w=[D1])
            dma('sp', A(D2)[:, :N], rope_in[:, 1, j, 0:N], w=[D2])
            ps = PSR.get()
            for k in range(8):
                mm(A(ps)[:, :N], wkv[:, k, 0:128], A(hT[k])[:, :N], k == 0, k == 7, r=[wk, hT[k]], w=[ps])
            kT = galloc(1)[0]
            normrope(ps, bcol(12), pca[:, 255:256], 8.0, N, A(kT)[:, :N], [kT])
            vt = galloc(1)[0]
            for t in range(nt):
                ps2 = PSR.get()
                for k in range(8):
                    mm(A(ps2)[:, 0:128], A(hT[k])[:, t * 128:(t + 1) * 128], wkv[:, k, 128:256], k == 0, False,
                       r=[wk, hT[k]], w=[ps2])
                mm(A(ps2)[:, 0:128], A(onesb)[0:1, :], A(brow)[0:1, 0:128], False, True, r=[onesb, brow], w=[ps2])
                cp('act', A(vt)[:, t * 128:(t + 1) * 128], A(ps2)[:, 0:128], r=[ps2], w=[vt])
            if lat:
                dma('pool', ex1_in[par][:, j * 512:(j + 1) * 512], A(kT)[:, :N], r=[kT], w=[T_ex1i[par]])
                dma('pool', ex1_in[par][:, 2048 + j * 512:2048 + (j + 1) * 512], A(vt)[:, :N], r=[vt], w=[T_ex1i[par]])
            else:
                dma('pool', cxkv[:, 0:256], A(kT)[:, :N], r=[kT], w=[T_cx])
                dma('pool', cxkv[:, 256:512], A(vt)[:, :N], r=[vt], w=[T_cx])
            gfree_([kT, vt])
            wrk, wrkv = wload(WIN[:, 2304:2816], 8, 512, tw('in', l))
            wrv, wrvv = wload(WIN[:, 2816:3328], 8, 512, tw('in', l))
            def p1_proj(t):
                rkt_ = galloc(1)[0]; rvt_ = galloc(1)[0]
                for (wt_, wv_, boff, dst, scale) in ((wrk, wrkv, 128, rkt_, 0.125), (wrv, wrvv, 640, rvt_, 1.0)):
                    ps3 = PSR.get()
                    for k in range(8):
                        mm(A(ps3), A(hT[k])[:, t * 128:(t + 1) * 128], wv_[:, k, :], k == 0, False, r=[wt_, hT[k]], w=[ps3])
                    mm(A(ps3), A(onesb)[0:1, :], A(brow)[0:1, boff:boff + 512], False, True, r=[onesb, brow], w=[ps3])
                    act(A(dst), A(ps3), AF.Copy, r=[ps3], w=[dst], scale=scale)
                kd = galloc(2)
                for d_ in range(2):
                    tt('dve', A(kd[d_]).rearrange("p (h d) -> p h d", h=8), A(rkt_).rearrange("p (h d) -> p h d", h=8),
                       A(KFt)[:, d_, :].unsqueeze(2).to_broadcast([128, 8, 64]), ALU.mult, r=[rkt_, KFt], w=[kd[d_]])
                return (t, rkt_, rvt_, kd)

            def p1_scan(st):
                t, rkt_, rvt_, kd = st
                cidx = (j * 4 + t) if lat else (16 + t)
                dci = (j * 4 + t) if lat else (14 + t)
                sbl = SBL.get()
                tt('dve', A(sbl), A(Sbk), A(BMASK), ALU.mult, r=[Sbk, BMASK], w=[sbl])
                if True:
                    dma('pool', sbscr[cidx], A(sbl), r=[sbl], w=[T_sbs[cidx]])
                psB = PSR.get(); psF = PSR.get()
                for c in range(4):
                    cs_ = slice(c * 128, (c + 1) * 128)
                    mm(A(psB)[:, cs_], A(kd[1])[:, cs_], A(rvt_)[:, cs_], True, True, r=[kd[1], rvt_], w=[psB])
                    mm(A(psF)[:, cs_], A(kd[0])[:, cs_], A(rvt_)[:, cs_], True, True, r=[kd[0], rvt_], w=[psF])
                tt('dve', v4(A(Sbk)), v4(A(Sbk)), A(CDt)[:, 1, :].unsqueeze(2).to_broadcast([128, 4, 128]), ALU.mult,
                   r=[Sbk, CDt], w=[Sbk])
                tt('dve', A(Sbk), A(Sbk), A(psB), ALU.add, r=[Sbk, psB], w=[Sbk])
                tf = FT.get()
                tt('dve', v4(A(tf)[:, 0:512]), v4(A(psF)), A(DECt)[:, 0, dci, :].unsqueeze(2).to_broadcast([128, 4, 128]),
                   ALU.mult, r=[psF, DECt], w=[tf])
                tt('pool', A(Tfk), A(Tfk), A(tf)[:, 0:512], ALU.add, r=[Tfk, tf], w=[Tfk])
                gfree_([rkt_, rvt_] + kd)
            pend1 = None
            for t in reversed(range(nt)):
                st = p1_proj(t)
                if pend1 is not None: p1_scan(pend1)
                pend1 = st
            p1_scan(pend1)
            if not lat:
                tt('dve', v4(A(Ssf)), v4(A(Tfk)), A(coef)[:, 4, :].unsqueeze(2).to_broadcast([128, 4, 128]), ALU.mult,
                   r=[Tfk, coef], w=[Ssf])
                tt('dve', v4(A(Ssb)), v4(A(Sbk)), A(coef)[:, 9, :].unsqueeze(2).to_broadcast([128, 4, 128]), ALU.mult,
                   r=[Sbk, coef], w=[Ssb])
                mset('dve', A(Sbk), 0.0, w=[Sbk]); mset('dve', A(Tfk), 0.0, w=[Tfk])
            gfree_(hT)
        dma('pool', ex2_in[par][:, 0:512], A(Tfk), r=[Tfk], w=[T_ex2i[par]])
        dma('pool', ex2_in[par][:, 512:1024], A(Sbk), r=[Sbk], w=[T_ex2i[par]])
        if STOP <= 3: break
        P.op('pool', lambda e, par=par: e.collective_compute("AllGather", ALU.bypass, replica_groups=RG,
                                                              ins=[ex1_in[par]], outs=[ex1_out[par]]),
             r=[T_ex1i[par]], w=[T_ex1o[par]], cc=True)
        P.op('pool', lambda e, par=par: e.collective_compute("AllGather", ALU.bypass, replica_groups=RG,
                                                              ins=[ex2_in[par]], outs=[ex2_out[par]]),
             r=[T_ex2i[par]], w=[T_ex2o[par]], cc=True)
        exch_finish(par)
        xhk = lambda k: A(ACC[k // 2])[:, (k % 2) * 256:(k % 2) * 256 + 150]
        xht = lambda k: ACC[k // 2]
        for b in BLK:
            j = b['j']; ws = b['ws']; N = b['N']
            for k in range(8):
                cp('pool', xhk(k)[:, j * 30:j * 30 + 15], xa[:, k, ws:ws + 15], r=xk(k) + [xTh], w=[xht(k)])
                cp('pool', xhk(k)[:, j * 30 + 15:j * 30 + 30], xa[:, k, ws + 15 + N:ws + 30 + N], r=xk(k) + [xTh], w=[xht(k)])
        hhs = galloc(3)
        hhk = lambda k: A(hhs[k // 3])[:, (k % 3) * 150:(k % 3) * 150 + 150]
        hht = lambda k: hhs[k // 3]
        make_h([xhk(k)[:, 0:120] for k in range(8)], [[xht(k)] for k in range(8)], 120, G1(0), SH1(0),
               [(hhk(k)[:, 0:120], [hht(k)]) for k in range(8)])
        make_h([xhk(k)[:, 120:150] for k in range(8)], [[xht(k)] for k in range(8)], 30, G1(1), SH1(1),
               [(hhk(k)[:, 120:150], [hht(k)]) for k in range(8)])
        wa, wav = wload(WIN[:, 0:512], 8, 512, tw('in', l))
        wb, wbv = wload(WIN[:, 512:1024], 8, 512, tw('in', l))
        uha = A(uh)
        for c in range(4):
            psa = PSR.get(); psb = PSR.get()
            for k in range(8):
                mm(A(psa)[:, :150], wav[:, k, c * 128:(c + 1) * 128], hhk(k), k == 0, k == 7, r=[wa, hht(k)], w=[psa])
            for k in range(8):
                mm(A(psb)[:, :150], wbv[:, k, c * 128:(c + 1) * 128], hhk(k), k == 0, k == 7, r=[wb, hht(k)], w=[psb])
            sg = FT.get()
            act(A(sg)[:, :150], A(psb)[:, :150], AF.Sigmoid, r=[psb, pc], w=[sg], bias=bcol(4 + c))
            stt('dve', uha[:, c, :], A(psa)[:, :150], bcol(c), A(sg)[:, :150], ALU.add, ALU.mult, r=[psa, sg, pc], w=[uh])
            ts('dve', uha[:, c, 0:15], uha[:, c, 0:15], rk_[:, 8:9], None, ALU.mult, None, r=[uh, rkt], w=[uh])
            ts('dve', uha[:, c, 105:120], uha[:, c, 105:120], rk_[:, 9:10], None, ALU.mult, None, r=[uh, rkt], w=[uh])
            mset('dve', uha[:, c, 120:150], 0.0, w=[uh])
        gfree_(hhs)

        if STOP <= 4: break
        if l == 0:
            CV.finish()
            if depth > 1:
                CV.start(1)
                n_cv = (len(CV.q) + 9) // 10
        for b in BLK:
            j = b['j']; N = b['N']; mc = b['mc']; ws = b['ws']; lat = b['lat']; tcol = 0 if lat else 1
            nt = N // 128
            if l == depth - 1 and not lat: continue
            if l + 1 < depth: CV.step(n_cv)
            hT = galloc(8)
            make_h([xa[:, k, mc:mc + N] for k in range(8)], [xk(k) for k in range(8)], N, G1(tcol), SH1(tcol),
                   [(A(hT[k])[:, :N], [hT[k]]) for k in range(8)])

            if l == 0 and j == 0:
                for k in range(8): dump16(k, hT[k])
            wrq, wrqv = wload(WIN[:, 1792:2304], 8, 512, tw('in', l))
            rqT = galloc(4); rkT = galloc(4)
            for c in range(4):
                ps = PSR.get()
                for k in range(8):
                    mm(A(ps)[:, :N], wrqv[:, k, c * 128:(c + 1) * 128], A(hT[k])[:, :N], k == 0, k == 7, r=[wrq, hT[k]], w=[ps])
                act(A(rqT[c])[:, :N], A(ps)[:, :N], AF.Identity, r=[ps, pc], w=[rqT[c]], bias=bcol(14 + c))
            wrk, wrkv = wload(WIN[:, 2304:2816], 8, 512, tw('in', l))
            for c in range(4):
                ps = PSR.get()
                for k in range(8):
                    mm(A(ps)[:, :N], wrkv[:, k, c * 128:(c + 1) * 128], A(hT[k])[:, :N], k == 0, k == 7, r=[wrk, hT[k]], w=[ps])
                ts('dve', A(rkT[c])[:, :N], A(ps)[:, :N], bcol(18 + c), 0.125, ALU.add, ALU.mult, r=[ps, pc], w=[rkT[c]])
            rkM = galloc(nt); rvM = galloc(nt); rgM = galloc(nt)
            for t in range(nt):
                ps3 = PSR.get()
                for k in range(8):
                    mm(A(ps3), A(hT[k])[:, t * 128:(t + 1) * 128], wrkv[:, k, :], k == 0, False, r=[wrk, hT[k]], w=[ps3])
                mm(A(ps3), A(onesb)[0:1, :], A(brow)[0:1, 128:640], False, True, r=[onesb, brow], w=[ps3])
                act(A(rkM[t]), A(ps3), AF.Copy, r=[ps3], w=[rkM[t]], scale=0.125)
            if STOP <= 4.05: break
            wrv, wrvv = wload(WIN[:, 2816:3328], 8, 512, tw('in', l))
            for t in range(nt):
                ps3 = PSR.get()
                for k in range(8):
                    mm(A(ps3), A(hT[k])[:, t * 128:(t + 1) * 128], wrvv[:, k, :], k == 0, False, r=[wrv, hT[k]], w=[ps3])
                mm(A(ps3), A(onesb)[0:1, :], A(brow)[0:1, 640:1152], False, True, r=[onesb, brow], w=[ps3])
                cp('act', A(rvM[t]), A(ps3), r=[ps3], w=[rvM[t]])
            wrg, wrgv = wload(WIN[:, 3328:3840], 8, 512, tw('in', l))
            for t in range(nt):
                ps3 = PSR.get()
                for k in range(8):
                    mm(A(ps3), A(hT[k])[:, t * 128:(t + 1) * 128], wrgv[:, k, :], k == 0, False, r=[wrg, hT[k]], w=[ps3])
                mm(A(ps3), A(onesb)[0:1, :], A(brow)[0:1, 1152:1664], False, True, r=[onesb, brow], w=[ps3])
                act(A(rgM[t]), A(ps3), AF.Silu, r=[ps3], w=[rgM[t]])
            wa, wav = wload(WIN[:, 0:512], 8, 512, tw('in', l))
            wb, wbv = wload(WIN[:, 512:1024], 8, 512, tw('in', l))
            for c in range(4):
                psa = PSR.get(); psb = PSR.get()
                for k in range(8):
                    mm(A(psa)[:, :N], wav[:, k, c * 128:(c + 1) * 128], A(hT[k])[:, :N], k == 0, k == 7, r=[wa, hT[k]], w=[psa])
                for k in range(8):
                    mm(A(psb)[:, :N], wbv[:, k, c * 128:(c + 1) * 128], A(hT[k])[:, :N], k == 0, k == 7, r=[wb, hT[k]], w=[psb])
                sg = FT.get()
                act(A(sg)[:, :N], A(psb)[:, :N], AF.Sigmoid, r=[psb, pc], w=[sg], bias=bcol(4 + c))
                u = ubuf[c]
                stt('dve', A(u)[:, 15:15 + N], A(psa)[:, :N], bcol(c), A(sg)[:, :N], ALU.add, ALU.mult, r=[psa, sg, pc], w=[u])
                cp('pool', A(u)[:, 0:15], A(uh)[:, c, j * 30:j * 30 + 15], r=[uh], w=[u])
                cp('pool', A(u)[:, 15 + N:30 + N], A(uh)[:, c, j * 30 + 15:j * 30 + 30], r=[uh], w=[u])
            wc = lambda kk, c: pca[:, 118 + c * 31 + kk:119 + c * 31 + kk]
            def conv_pe(c):
                psc = PSR.get()
                for kk in range(31):
                    dg = DG.get()
                    act(A(dg), A(ident), AF.Copy, r=[ident, pc], w=[dg], scale=wc(kk, c))
                    mm(A(psc)[:, :N], A(dg), A(ubuf[c])[:, kk:kk + N], kk == 0, kk == 30, r=[dg, ubuf[c]], w=[psc])
                act(A(ACC[c])[:, :N], A(psc)[:, :N], AF.Identity, r=[psc, pc], w=[ACC[c]], bias=pca[:, 242 + c:243 + c])
            cpc = 4 // nt
            if j == 0:
                v4 = lambda ap: ap.rearrange("p (c n) -> p c n", c=4)
                for d_, Sst in enumerate((Ssf, Ssb)):
                    for q in range(4):
                        tl = FT.get()
                        dma('sp', A(tl)[:, 0:512], ex2_out[par][q * 128:(q + 1) * 128, d_ * 512:(d_ + 1) * 512], r=[T_ex2o[par]], w=[tl])
                        tt('dve', v4(A(tl)[:, 0:512]), v4(A(tl)[:, 0:512]),
                           A(coef)[:, d_ * 5 + q, :].unsqueeze(2).to_broadcast([128, 4, 128]), ALU.mult, r=[tl, coef], w=[tl])
                        tt('dve', A(Sst), A(Sst), A(tl)[:, 0:512], ALU.add, r=[Sst, tl], w=[Sst])

                tt('dve', A(Ssb), A(Ssb), A(BMASK), ALU.mult, r=[Ssb, BMASK], w=[Ssb])
            if j == 0:
                cp('dve', A(Sf), A(Ssf), r=[Ssf], w=[Sf])
            if not lat:
                mset('dve', A(Sf), 0.0, w=[Sf])
            retT = galloc(4)
            v8 = lambda ap: ap.rearrange("p (h e) -> p h e", h=8)
            wq, wqv = wload(WIN[:, 1024:1536], 8, 512, tw('in', l))
            dma('sp', A(D1)[:, :N], rope_in[:, 0, j, 0:N], w=[D1])
            dma('sp', A(D2)[:, :N], rope_in[:, 1, j, 0:N], w=[D2])
            qT = [None] * 4

            def q_proj(c):
                qT[c] = galloc(1)[0]
                ps = PSR.get()
                for k in range(8):
                    mm(A(ps)[:, :N], wqv[:, k, c * 128:(c + 1) * 128], A(hT[k])[:, :N], k == 0, k == 7, r=[wq, hT[k]], w=[ps])
                normrope(ps, bcol(8 + c), pca[:, 254:255], 1.0, N, A(qT[c])[:, :N], [qT[c]])
            for t in range(nt):
                cidx = (j * 4 + t) if lat else (16 + t)
                tsl = slice(t * 128, (t + 1) * 128)
                tt('dve', A(Sf16), A(Sf), A(BMASK), ALU.mult, r=[Sf, BMASK], w=[Sf16])
                sbl = SBL.get()
                dma('sp', A(sbl), sbscr[cidx], r=[T_sbs[cidx]], w=[sbl])
                if STOP <= 4.101: break
                if lat:
                    sbu = galloc(1)[0]
                    tl = FT.get()
                    tt('dve', v4(A(tl)[:, 0:512]), v4(A(Ssb)), A(DECt)[:, 1, j * 4 + t, :].unsqueeze(2).to_broadcast([128, 4, 128]),
                       ALU.mult, r=[Ssb, DECt], w=[tl])
                    tt('dve', A(sbu), A(sbl), A(tl)[:, 0:512], ALU.add, r=[sbl, tl], w=[sbu])
                else:
                    sbu = sbl
                if STOP <= 4.102: break
                qd = galloc(2)
                for d_ in range(2):
                    for c in range(4):
                        tt('dve' if d_ == 0 else 'pool', A(qd[d_])[:, c * 128:(c + 1) * 128], A(rqT[c])[:, tsl], A(QFt)[:, d_, c, :],
                           ALU.mult, r=[rqT[c], QFt], w=[qd[d_]])
                if STOP <= 4.103: break
                kdf = galloc(1)[0]
                tt('pool', v8(A(kdf)), v8(A(rkM[t])), A(KFt)[:, 0, :].unsqueeze(2).to_broadcast([128, 8, 64]), ALU.mult,
                   r=[rkM[t], KFt], w=[kdf])
                if STOP <= 4.104: break
                WT = galloc(2)
                for c in range(4):
                    cp('act', A(RKL)[0:64, c * 128:(c + 1) * 128], A(rkT[c])[0:64, tsl], r=[rkT[c]], w=[RKL])
                    cp('act', A(RKH)[64:128, c * 128:(c + 1) * 128], A(rkT[c])[64:128, tsl], r=[rkT[c]], w=[RKH])
                for hp in range(2):
                    psa = PSR.get()
                    for hh_ in range(4):
                        h = hp * 4 + hh_; c = h // 2
                        RKm = RKL if h % 2 == 0 else RKH
                        mm(A(psa)[:, hh_ * 128:(hh_ + 1) * 128], A(RKm)[:, c * 128:(c + 1) * 128], A(rqT[c])[:, tsl], True, True,
                           r=[RKm, rqT[c]], w=[psa])
                    tt('dve', A(WT[hp]), A(psa), A(DTt)[:, hp * 4:(hp + 1) * 4, :].rearrange("p h n -> p (h n)"), ALU.mult,
                       r=[psa, DTt], w=[WT[hp]])
                if STOP <= 4.11: break
                po = PSL[t % 4]
                for h in range(8):
                    c = h // 2; off = (h % 2) * 64
                    osl = slice(h * 64, (h + 1) * 64)
                    mm(A(po)[:, osl], A(WT[h // 4])[:, (h % 4) * 128:(h % 4 + 1) * 128], A(rvM[t])[:, osl], True, False,
                       r=[WT[h // 4], rvM[t]], w=[po])
                    mm(A(po)[:, osl], A(qd[0])[:, c * 128:(c + 1) * 128],
                       A(Sf16)[:, c * 128 + off:c * 128 + off + 64], False, False, r=[qd[0], Sf16], w=[po])
                    mm(A(po)[:, osl], A(qd[1])[:, c * 128:(c + 1) * 128],
                       A(sbu)[:, c * 128 + off:c * 128 + off + 64], False, True, r=[qd[1], sbu], w=[po])
                ob = FT.get(); sq = FT.get()
                cp('act', A(ob)[:, 0:512], A(po), r=[po], w=[ob])
                act(A(sq)[:, 0:512], A(po), AF.Square, r=[po], w=[sq])
                psS = PSR.get()
                for c in range(4):
                    cs_ = slice(c * 128, (c + 1) * 128)
                    mm(A(psS)[:, cs_], A(kdf)[:, cs_], A(rvM[t])[:, cs_], True, True, r=[kdf, rvM[t]], w=[psS])
                tt('dve', v4(A(Sf)), v4(A(Sf)), A(CDt)[:, 0, :].unsqueeze(2).to_broadcast([128, 4, 128]), ALU.mult, r=[Sf, CDt], w=[Sf])
                tt('dve', A(Sf), A(Sf), A(psS), ALU.add, r=[Sf, psS], w=[Sf])
                for c in range(t * cpc, (t + 1) * cpc): conv_pe(c)
                s1 = sf[:, 0:8]; s2 = sf[:, 8:16]; s3 = sf[:, 16:24]
                P.op('dve', lambda e, ob=ob, s1=s1: e.tensor_reduce(s1, v8(A(ob)[:, 0:512]), AX.X, ALU.add), r=[ob], w=[smallf])
                P.op('dve', lambda e, sq=sq, s2=s2: e.tensor_reduce(s2, v8(A(sq)[:, 0:512]), AX.X, ALU.add), r=[sq], w=[smallf])
                ts('dve', s1, s1, 1.0 / 64, None, ALU.mult, None, r=[smallf], w=[smallf])
                tt('dve', s3, s1, s1, ALU.mult, r=[smallf], w=[smallf])
                stt('dve', s2, s2, 1.0 / 64, s3, ALU.mult, ALU.subtract, r=[smallf], w=[smallf])
                rsq(s2, s2, 1.0, EPS, [smallf], [smallf])
                tt('dve', v8(A(ob)[:, 0:512]), v8(A(ob)[:, 0:512]), s1.unsqueeze(2).to_broadcast([128, 8, 64]), ALU.subtract,
                   r=[ob, smallf], w=[ob])
                tt('dve', v8(A(ob)[:, 0:512]), v8(A(ob)[:, 0:512]), s2.unsqueeze(2).to_broadcast([128, 8, 64]), ALU.mult,
                   r=[ob, smallf], w=[ob])
                yb = galloc(1)[0]
                tt('dve', A(yb), A(ob)[:, 0:512], A(rgM[t]), ALU.mult, r=[ob, rgM[t]], w=[yb])
                pst_ = PSR.get()
                for c in range(4):
                    mm(A(pst_)[:, c * 128:(c + 1) * 128], A(yb)[:, c * 128:(c + 1) * 128], A(ident), True, True, r=[yb, ident], w=[pst_])
                for c in range(4):
                    act(A(retT[c])[:, tsl], A(pst_)[:, c * 128:(c + 1) * 128], AF.Copy, r=[pst_, pc], w=[retT[c]],
                        scale=pca[:, 368 + c:369 + c])
                gfree_(qd + [kdf, yb] + WT + ([sbu] if lat else []))
                gfree_([rkM[t], rvM[t], rgM[t]])
                for c in range(t * cpc, (t + 1) * cpc): q_proj(c)
            if l == 0 and j == 0:
                for c in range(4): dump16(8 + c, retT[c])
                for c in range(4): dump16(28 + c, rqT[c])
                for c in range(4): dump16(32 + c, rkT[c])
                for c in range(4): dump16(36 + c, rvM[c])
            gfree_(rqT + rkT)

            if STOP <= 4.2: break
            aT = galloc(4)
            ps1 = PSR.get(); ps2 = PSR.get()
            for c in range(4):
                mm(A(ps1)[:, :N], A(onesf), A(ACC[c])[:, :N], c == 0, c == 3, r=[onesf, ACC[c]], w=[ps1])
            for c in range(4):
                sq = FT.get()
                act(A(sq)[:, :N], A(ACC[c])[:, :N], AF.Square, r=[ACC[c]], w=[sq])
                mm(A(ps2)[:, :N], A(onesf), A(sq)[:, :N], c == 0, c == 3, r=[onesf, sq], w=[ps2])
            mt = D1; vt_ = D2
            ts('dve', A(mt)[:, :N], A(ps1)[:, :N], 1.0 / 512, None, ALU.mult, None, r=[ps1], w=[mt])
            tt('dve', A(vt_)[:, :N], A(mt)[:, :N], A(mt)[:, :N], ALU.mult, r=[mt], w=[vt_])
            stt('dve', A(vt_)[:, :N], A(ps2)[:, :N], 1.0 / 512, A(vt_)[:, :N], ALU.mult, ALU.subtract, r=[ps2, vt_], w=[vt_])
            rsq(A(vt_)[:, :N], A(vt_)[:, :N], 1.0, EPS, [vt_], [vt_])
            for c in range(4):
                acc = ACC[c]
                tt('dve', A(acc)[:, :N], A(acc)[:, :N], A(mt)[:, :N], ALU.subtract, r=[acc, mt], w=[acc])
                tt('pool', A(acc)[:, :N], A(acc)[:, :N], A(vt_)[:, :N], ALU.mult, r=[acc, vt_], w=[acc])
                ts('dve', A(acc)[:, :N], A(acc)[:, :N], pca[:, 246 + c:247 + c], pca[:, 250 + c:251 + c], ALU.mult, ALU.add,
                   r=[acc, pc], w=[acc])
                act(A(aT[c])[:, :N], A(acc)[:, :N], AF.Silu, r=[acc], w=[aT[c]])

            if STOP <= 4.3: break
            if l == 0 and j == 0:
                for c in range(4): dump16(12 + c, aT[c])
                for c in range(4): dump16(24 + c, qT[c])
            attT = galloc(4)
            NS_ = 16 // GT
            groups = [('c', 0)] + ([(q, s_) for q in range(4) for s_ in range(NS_)] if lat else [])
            for half in range(2):
                hs = slice(half * 64, half * 64 + 64)
                os_ = slice((1 - half) * 64, (1 - half) * 64 + 64)
                VGr = VG0 if half == 0 else VG1
                voff = 0 if half == 0 else 64
                first = True
                pend = []

                def flush_pv(keep):
                    while len(pend) > keep:
                        (po_, vg_, t_, pt_, st_, sp_) = pend.pop(0)
                        mm(A(po_)[:, :N], A(vg_)[:, t_, :], A(pt_)[:, :N], st_, sp_, r=[vg_, pt_], w=[po_])
                for gi, (gq, gs) in enumerate(groups):
                    ktg = (KTG0 if half == 0 else KTG1).get(); vg = VGr.get()
                    if gq == 'c':
                        ntile = 2
                        dma('sp', A(ktg)[hs, 0:256], cxkv[hs, 0:256], r=[T_cx], w=[ktg])
                        dma('sp', A(vg)[:, 0:2, voff:voff + 64],
                            cxkv[:, 256:512].rearrange("p (t e) -> p t e", t=2)[:, :, half * 64:half * 64 + 64],
                            r=[T_cx], w=[vg])
                    else:
                        ntile = GT
                        GW = GT * 128
                        dma('sp', A(ktg)[hs, :], ex1_out[par][gq * 128 + half * 64:gq * 128 + half * 64 + 64, gs * GW:(gs + 1) * GW],
                            r=[T_ex1o[par]], w=[ktg])
                        dma('sp', A(vg)[:, :, voff:voff + 64],
                            ex1_out[par][gq * 128:(gq + 1) * 128, 2048 + gs * GW:2048 + (gs + 1) * GW]
                            .rearrange("p (t e) -> p t e", t=GT)[:, :, half * 64:half * 64 + 64],
                            r=[T_ex1o[par]], w=[vg])
                    last_g = gi == len(groups) - 1
                    for c in range(4):
                        po = PSL[c]
                        for t in range(ntile):
                            ps = PSR.get()
                            mm(A(ps)[:, :N], A(ktg)[:, t * 128:(t + 1) * 128], A(qT[c])[:, :N], True, True, r=[ktg, qT[c]], w=[ps])
                            pt = SQ.get()
                            act(A(pt)[:, :N], A(ps)[:, :N], AF.Exp, r=[ps], w=[pt])
                            pend.append((po, vg, t, pt, first and t == 0, last_g and t == ntile - 1))
                            flush_pv(2)
                    first = False
                flush_pv(0)
                for c in range(4):
                    po = PSL[c]
                    rc = FT.get()
                    mset('dve', A(rc)[hs, :N], 0.0, w=[rc])
                    P.op('act', lambda e, rc=rc, po=po, os_=os_, N=N: e.activation(A(rc)[os_, :N], A(po)[os_, :N], AF.Ln), r=[po], w=[rc])
                    P.op('act', lambda e, rc=rc, os_=os_, N=N: e.activation(A(rc)[os_, :N], A(rc)[os_, :N], AF.Exp, scale=-1.0), r=[rc], w=[rc])
                    ob = FT.get()
                    cp('act', A(ob)[hs, :N], A(po)[hs, :N], r=[po], w=[ob])
                    ps = PSR.get()
                    mm(A(ps)[:, :N], SWAP, A(rc)[:, :N], True, True, r=[cst, rc], w=[ps])
                    tt('dve', A(attT[c])[hs, :N], A(ob)[hs, :N], A(ps)[hs, :N], ALU.mult, r=[ob, ps], w=[attT[c]])
            gfree_(qT)

            if STOP <= 4.4: break
            zT = galloc(8)
            for gsec, (wsrc, wnm, br) in enumerate(((WPA, 'pa', aT), (WPB, 'pb', attT), (WPC, 'pc', retT))):
                for og in range(2):
                    if False:
                        wp_ = Wrot.get()
                        wpv = A(wp_)[:, 0:4096].rearrange("p (k n) -> p k n", k=4)
                        for c in range(4):
                            for hf in range(2):
                                h_ = hf * 4 + c
                                dma('pool', wpv[hf * 64:(hf + 1) * 64, c, :], w_pb[0][h_ * 64:(h_ + 1) * 64, :], w=[wp_])
                    else:
                        wp_, wpv = wload(wsrc, 4, 1024, tw(wnm, l))
                    c0 = 3840 + gsec * 1024 + og * 512
                    wg, wgv = wload(WIN[:, c0:c0 + 512], 8, 512, tw('in', l))
                    for oo in range(4):
                        o = og * 4 + oo
                        psg = PSR.get(); psp = PSR.get()
                        for k in range(8):
                            mm(A(psg)[:, :N], wgv[:, k, oo * 128:(oo + 1) * 128], A(hT[k])[:, :N], k == 0, k == 7, r=[wg, hT[k]], w=[psg])
                        for c in range(4):
                            mm(A(psp)[:, :N], wpv[:, c, o * 128:(o + 1) * 128], A(br[c])[:, :N], c == 0, c == 3, r=[wp_, br[c]], w=[psp])
                        sg = FT.get()
                        act(A(sg)[:, :N], A(psg)[:, :N], AF.Sigmoid, r=[psg, pc], w=[sg], bias=bcol(30 + gsec * 8 + o))
                        if gsec == 0:
                            tt('dve', A(zT[o])[:, :N], A(psp)[:, :N], A(sg)[:, :N], ALU.mult, r=[psp, sg], w=[zT[o]])
                        else:
                            tt('dve', A(sg)[:, :N], A(psp)[:, :N], A(sg)[:, :N], ALU.mult, r=[psp, sg], w=[sg])
                            tt('pool', A(zT[o])[:, :N], A(zT[o])[:, :N], A(sg)[:, :N], ALU.add, r=[zT[o], sg], w=[zT[o]])
            if l == 0 and j == 0:
                for c in range(4): dump16(16 + c, attT[c])
                for o in range(4): dump16(20 + o, zT[o])
            gfree_(aT + attT + retT + hT)
            for og in range(2):
                wo, wov = wload(WOUT[:, og * 512:(og + 1) * 512], 8, 512, tw('out', l))
                for oo in range(4):
                    o = og * 4 + oo
                    ps = PSR.get()
                    for k in range(8):
                        mm(A(ps)[:, :N], wov[:, k, oo * 128:(oo + 1) * 128], A(zT[k])[:, :N], k == 0, k == 7, r=[wo, zT[k]], w=[ps])
                    stt('dve', xa[:, o, mc:mc + N], A(ps)[:, :N], mo[:, 16 + o, tcol:tcol + 1], xa[:, o, mc:mc + N], ALU.mult, ALU.add,
                        r=[ps, modT] + xk(o), w=[xTk[o]])
            gfree_(zT)
            if l == 0 and j == 0:
                for o in range(8): dump32(o, xa[:, o, mc:mc + N], xk(o))
                dump32(8, mo.rearrange("p j t -> p (j t)"), [modT], 96)

        if STOP <= 5: break
        exch_xhalo(1 - par)
        PSRh[0] = PSR8
        for b in BLK:
            j = b['j']; ws = b['ws']; N = b['N']
            for k in range(8):
                cp('pool', xhk(k)[:, 2 * j:2 * j + 1], xa[:, k, ws + 14:ws + 15], r=xk(k) + [xTh], w=[xht(k)])
                cp('pool', xhk(k)[:, 2 * j + 1:2 * j + 2], xa[:, k, ws + 15 + N:ws + 16 + N], r=xk(k) + [xTh], w=[xht(k)])
        hhs = galloc(3)
        make_h([xhk(k)[:, 0:8] for k in range(8)], [[xht(k)] for k in range(8)], 8, G2(0), SH2(0), [(hhk(k)[:, 0:8], [hht(k)]) for k in range(8)])
        make_h([xhk(k)[:, 8:10] for k in range(8)], [[xht(k)] for k in range(8)], 2, G2(1), SH2(1), [(hhk(k)[:, 8:10], [hht(k)]) for k in range(8)])
        for pcs in range(11):
            wt, wv = wload(WUP[:, pcs * 256:(pcs + 1) * 256], 8, 256, tw('up', l))
            for ff in range(2):
                f = pcs * 2 + ff
                ps = PSR.get()
                for k in range(8):
                    mm(A(ps)[:, 0:10], wv[:, k, ff * 128:(ff + 1) * 128], hhk(k)[:, 0:10], k == 0, k == 7, r=[wt, hht(k)], w=[ps])
                cp('act', A(gth)[:, f, :], A(ps)[:, 0:10], r=[ps], w=[gth])
        gfree_(hhs)
        ga = A(gth)
        ts('dve', ga[:, :, 0:1], ga[:, :, 0:1], rk_[:, 8:9], None, ALU.mult, None, r=[gth, rkt], w=[gth])
        ts('dve', ga[:, :, 7:8], ga[:, :, 7:8], rk_[:, 9:10], None, ALU.mult, None, r=[gth, rkt], w=[gth])
        mset('dve', ga[:, :, 8:10], 0.0, w=[gth])
        for b in BLK:
            j = b['j']; N = b['N']; mc = b['mc']; lat = b['lat']; tcol = 0 if lat else 1
            if l == depth - 1 and not lat: continue
            if l + 1 < depth: CV.step(n_cv)
            hT = galloc(8)
            make_h([xa[:, k, mc:mc + N] for k in range(8)], [xk(k) for k in range(8)], N, G2(tcol), SH2(tcol),
                   [(A(hT[k])[:, :N], [hT[k]]) for k in range(8)])
            actT = galloc(22)
            for pcs in range(11):
                wt = Wrot.get()
                wv = A(wt)[:, 0:4096].rearrange("p (k n) -> p k n", k=8)
                dma(WENG, wv[:, :, 0:256], WUP[:, pcs * 256:(pcs + 1) * 256].rearrange("(k p) n -> p k n", p=128), r=[tw('up', l)], w=[wt])
                dma(WENG, wv[:, :, 256:512], WUP[:, 2816 + pcs * 256:2816 + (pcs + 1) * 256].rearrange("(k p) n -> p k n", p=128),
                    r=[tw('up', l)], w=[wt])
                for ff in range(2):
                    f = pcs * 2 + ff
                    psg = PSR.get(); psv = PSR.get()
                    for k in range(8):
                        mm(A(psg)[:, :N], wv[:, k, ff * 128:(ff + 1) * 128], A(hT[k])[:, :N], k == 0, k == 7, r=[wt, hT[k]], w=[psg])
                    for k in range(8):
                        mm(A(psv)[:, :N], wv[:, k, 256 + ff * 128:256 + (ff + 1) * 128], A(hT[k])[:, :N], k == 0, k == 7, r=[wt, hT[k]], w=[psv])
                    gt = FT.get()
                    cp('act', A(gt)[:, 1:1 + N], A(psg)[:, :N], r=[psg], w=[gt])
                    cp('act', A(gt)[:, 0:1], ga[:, f, 2 * j:2 * j + 1], r=[gth], w=[gt])
                    cp('act', A(gt)[:, 1 + N:2 + N], ga[:, f, 2 * j + 1:2 * j + 2], r=[gth], w=[gt])
                    acc = FT.get()
                    wf = lambda kk: pca[:, 256 + f * 3 + kk:257 + f * 3 + kk]
                    ts('dve', A(acc)[:, :N], A(gt)[:, 0:N], wf(0), pca[:, 322 + f:323 + f], ALU.mult, ALU.add, r=[gt, pc], w=[acc])
                    stt('dve', A(acc)[:, :N], A(gt)[:, 1:1 + N], wf(1), A(acc)[:, :N], ALU.mult, ALU.add, r=[gt, pc, acc], w=[acc])
                    stt('pool', A(acc)[:, :N], A(gt)[:, 2:2 + N], wf(2), A(acc)[:, :N], ALU.mult, ALU.add, r=[gt, pc, acc], w=[acc])
                    act(A(acc)[:, :N], A(acc)[:, :N], AF.Silu, r=[acc], w=[acc])
                    tt('dve', A(actT[f])[:, :N], A(acc)[:, :N], A(psv)[:, :N], ALU.mult, r=[acc, psv], w=[actT[f]])
            for o in range(8):
                wt = Wrot.get()
                wv = A(wt)[:, 0:22 * 128].rearrange("p (k n) -> p k n", k=22)
                dma(WENG, wv, WDN[:, o * 128:(o + 1) * 128].rearrange("(k p) n -> p k n", p=128), r=[tw('dn', l)], w=[wt])
                ps = PSR.get()
                for f in range(22):
                    mm(A(ps)[:, :N], wv[:, f, :], A(actT[f])[:, :N], f == 0, f == 21, r=[wt, actT[f]], w=[ps])
                stt('dve', xa[:, o, mc:mc + N], A(ps)[:, :N], mo[:, 40 + o, tcol:tcol + 1], xa[:, o, mc:mc + N], ALU.mult, ALU.add,
                    r=[ps, modT] + xk(o), w=[xTk[o]])
            gfree_(hT + actT)
            if j == 1 and l + 1 < depth: emit_adaln(l + 1)
        if l + 1 < depth: CV.finish()
        PSRh[0] = PSR4

    fa = A(fng)
    for j in range(4):
        mc = 15 + 512 * j
        ps = PSR.get()
        for k in range(8):
            sq = SQ.get()
            act(A(sq), xa[:, k, mc:mc + 512], AF.Square, r=xk(k), w=[sq])
            mm(A(ps), A(onesb), A(sq), k == 0, k == 7, r=[onesb, sq], w=[ps])
        R = D0
        rsq(A(R)[:, 0:512], A(ps), 1.0 / D, EPS, [ps], [R])
        for k in range(8):
            t = FT.get()
            stt('dve', A(t)[:, 0:512], xa[:, k, mc:mc + 512], fa[:, k:k + 1], A(R)[:, 0:512], ALU.mult, ALU.mult, r=xk(k) + [R, fng], w=[t])
            dma('sp', out_d[:, k, j * 512:(j + 1) * 512], A(t)[:, 0:512], r=[t], w=[])

    P.emit(nc, es)
    es.close()
    return nc


def _fm(v, nch):
    return np.ascontiguousarray(np.asarray(v, np.float32).reshape(nch, 128).T)


def _consts():
    c = np.zeros((128, NCST), np.float32)
    j = np.arange(128, dtype=np.float32)[:, None]; i = np.arange(128, dtype=np.float32)[None, :]
    c[:, 0:128] = np.maximum(i - j, 0); c[:, 128:256] = np.maximum(j - i, 0); c[:, 256:384] = np.eye(128)
    c[:, 384:512] = i + 1; c[:, 512:640] = 128 - i
    c[:, 640] = 127 - j[:, 0]; c[:, 641] = j[:, 0]
    c[:, 642:658] = 128.0 * (15 - np.arange(16))[None, :]
    rt = np.zeros((128, 128), np.float32)
    for g in range(4):
        for t in range(16):
            a = g * 32 + t
            rt[a + 16, a] = -1.0
            rt[a, a + 16] = 1.0
    c[:, 658:786] = rt
    sw = np.zeros((128, 128), np.float32)
    for k in range(128): sw[k, (k + 64) % 128] = 1.0
    c[:, 786:914] = sw
    return c


def _rope(start):
    tab = np.zeros((128, 2, 5, 512), np.float32)
    tab[:, 0, 4, :] = 1.0
    p = np.arange(128); d = p % 64; f = (d % 16).astype(np.float32)
    inv = (np.float32(10000.0) ** (-f / np.float32(16.0))).astype(np.float32)
    for j in range(4):
        t = start + j * 512 + np.arange(512)
        row = (t // 64).astype(np.float32); col = (t % 64).astype(np.float32)
        pos = np.where((d < 32)[:, None], row[None, :], col[None, :]).astype(np.float32)
        ang = (pos * inv[:, None]).astype(np.float32)
        tab[:, 0, j, :] = np.cos(ang); tab[:, 1, j, :] = np.sin(ang)
    return tab


def _prep(inputs, depth):
    f = lambda k: np.asarray(inputs[k], np.float32)
    x = f("x"); c = f("c"); ctx = f("ctx"); c_ctx = f("c_ctx")
    pcol = np.zeros((depth, 128, NPC), np.float32)
    brow = np.zeros((depth, 1, NBROW), np.float32)
    p = np.arange(128)
    for l in range(depth):
        pc = pcol[l]
        pc[:, 0:48] = _fm(f("b_ada")[l], 48)
        pc[:, 48:56] = _fm(f("norm1_g")[l], 8); pc[:, 56:64] = _fm(f("norm2_g")[l], 8)
        b = f("b_in")[l].copy()
        b[1024:1536] = b[1024:1536].reshape(2, 4, 64).transpose(1, 0, 2).reshape(512)
        pc[:, 64:118] = _fm(b, 54)
        pc[:, 118:242] = f("conv_dw_w")[l].T.reshape(4, 128, 31).transpose(1, 0, 2).reshape(128, 124)
        pc[:, 242:246] = _fm(f("conv_dw_b")[l], 4); pc[:, 246:250] = _fm(f("conv_ln_g")[l], 4)
        pc[:, 250:254] = _fm(f("conv_ln_b")[l], 4)
        pc[:, 254] = f("q_norm_g")[l][p % 64]; pc[:, 255] = f("k_norm_g")[l][p % 64]
        pc[:, 256:322] = f("ffn_dw_w")[l].T.reshape(22, 128, 3).transpose(1, 0, 2).reshape(128, 66)
        pc[:, 322:344] = _fm(f("ffn_dw_b")[l], 22)
        lg = f("ret_decay_logit")[l]
        for d_ in range(2):
            for cc in range(4):
                pc[:, 344 + d_ * 4 + cc] = lg[d_, 2 * cc + p // 64]
            pc[:, 352 + d_ * 8:360 + d_ * 8] = lg[d_][None, :]
        pc[:, 368:372] = _fm(f("ret_gn_g")[l], 4)
        bi = f("b_in")[l]
        brow[l, 0] = np.concatenate([bi[1664:1792], bi[2304:2816], bi[2816:3328], bi[3328:3840]])
    cst = _consts()
    fng = _fm(f("final_norm_g"), 8)
    shared = dict(cst=cst, pcol=pcol, brow=brow, fng=fng)
    for k in ("w_ada", "w_in", "w_pa", "w_pb", "w_pc", "w_out", "w_up", "w_down"):
        shared[k] = np.ascontiguousarray(f(k)[:depth])
    maps = []
    for r in range(8):
        b_ = r // 4; q = r % 4; start = q * NLAT
        xe = np.zeros((XW, D), np.float32)
        xe[15:15 + NLAT] = x[b_, start:start + NLAT]
        xe[CTX0:CTX0 + CTX] = ctx[b_]
        xT = np.ascontiguousarray(xe.T.reshape(8, 128, XW).transpose(1, 0, 2))
        cv = np.stack([c[b_], c_ctx], 0)
        cT = np.ascontiguousarray(cv.T.reshape(8, 128, 2).transpose(1, 0, 2))
        rk = np.zeros((128, 32), np.float32)
        for q2 in range(4):
            rk[:, q2] = 1.0 if q2 == q - 1 else 0.0
            rk[:, 4 + q2] = 1.0 if q2 == q + 1 else 0.0
            rk[:, 10 + q2] = 2048.0 * max(q - 1 - q2, 0); rk[:, 14 + q2] = 1.0 if q2 < q else 0.0
            rk[:, 19 + q2] = 2048.0 * max(q2 - q - 1, 0); rk[:, 23 + q2] = 1.0 if q2 > q else 0.0
        rk[:, 8] = 1.0 if q > 0 else 0.0; rk[:, 9] = 1.0 if q < 3 else 0.0
        rk[:, 18] = 2048.0 * q; rk[:, 27] = 2048.0 * (3 - q)
        m = dict(shared); m.update(xT=xT, cT=cT, rope=_rope(start), rkt=rk)
        maps.append(m)
    return maps


_NC = {}


def kernel(**inputs):
    depth = int(inputs.pop("_depth", DEPTH))
    if depth not in _NC:
        _NC[depth] = build(depth)
    maps = _prep(inputs, depth)
    res = run_bass_kernel_spmd(_NC[depth], maps, core_ids=list(range(8)))
    out = np.zeros((2, SEQ, D), np.float32)
    for r in range(8):
        o = np.asarray(res.results[r]["out"], np.float32)
        out[r // 4, (r % 4) * NLAT:(r % 4 + 1) * NLAT] = o.transpose(2, 1, 0).reshape(NLAT, D)
    return out
```

```python
import numpy as np
STOP = 9.0
from contextlib import ExitStack
import concourse.bass as bass
import concourse.mybir as mybir
from concourse.bass_utils import run_bass_kernel_spmd

F32 = mybir.dt.float32
BF16 = mybir.dt.bfloat16
ALU = mybir.AluOpType
AF = mybir.ActivationFunctionType
AX = mybir.AxisListType

D = 1024; DEPTH = 4; SEQ = 8192; CTX = 256; NLAT = 2048
XW = 2364; CTX0 = 2093
EPS = 1e-6
NPC = 372; NBROW = 1664; NCST = 914


class T:
    def __init__(self, ap):
        self.ap = ap; self.lw = {}; self.rd = {}


class Prog:
    NDS = 10

    def __init__(self):
        self.ops = []
        self.dma_rr = {}

    def op(self, eng, fn, r=(), w=(), dma=False, cc=False):
        idx = len(self.ops)
        if cc:
            cls = ('cc',)
        elif dma:
            k = self.dma_rr.get(eng, 0); self.dma_rr[eng] = k + 1
            cls = ('dma', eng, k % self.NDS)
        else:
            cls = (eng,)
        deps = {}

        def add(c, i):
            if i is not None and deps.get(c, -1) < i:
                deps[c] = i
        for b in r:
            for c, i in b.lw.items(): add(c, i)
        for b in w:
            for c, i in b.lw.items(): add(c, i)
            for c, i in b.rd.items(): add(c, i)
        for b in r: b.rd[cls] = idx
        for b in w: b.lw[cls] = idx; b.rd = {}
        self.ops.append(dict(eng=eng, fn=fn, cls=cls, deps=deps, dma=dma, cc=cc))
        return idx

    def emit(self, nc, es):
        ops = self.ops
        last_in_cls = {}
        for i, o in enumerate(ops):
            if o['dma'] or o['cc']:
                p = last_in_cls.get(o['cls'])
                if p is not None: o['deps'][o['cls']] = max(o['deps'].get(o['cls'], -1), p)
                last_in_cls[o['cls']] = i
        needed = set()
        for o in ops:
            for c, i in o['deps'].items():
                if c == ('pe',) and o['eng'] == 'pe' and not o['dma']:
                    continue
                needed.add(i)
        sems = {}
        classes = sorted({o['cls'] for o in ops}, key=str)
        for c in classes:
            sems[c] = es.enter_context(nc.semaphore("s_" + "_".join(str(x) for x in c)))
        cnt = {c: 0 for c in classes}
        tok = {}
        for i, o in enumerate(ops):
            if i in needed or o['dma'] or o['cc']:
                inc = 16 if o['dma'] else 1
                cnt[o['cls']] += inc
                tok[i] = cnt[o['cls']]
                o['inc'] = inc
            else:
                o['inc'] = 0
        engs = ['pe', 'act', 'dve', 'pool', 'sp']
        streams = {e: [o for o in ops if o['eng'] == e] for e in engs}
        idx_of = {id(o): i for i, o in enumerate(ops)}
        block = es.enter_context(nc.Block())

        def run(ename, e):
            waited = {}
            for o in streams[ename]:
                for c, i in sorted(o['deps'].items(), key=lambda kv: str(kv[0])):
                    if c == ('pe',) and ename == 'pe' and not o['dma']:
                        continue
                    v = tok[i]
                    if waited.get(c, 0) < v:
                        e.wait_ge(sems[c], v); waited[c] = v
                ins = o['fn'](e)
                if o['inc']:
                    ins.then_inc(sems[o['cls']], o['inc'])
            if ename == 'sp':
                for c in classes:
                    if cnt[c] > 0 and (c[0] == 'dma'):
                        e.wait_ge(sems[c], cnt[c])

        @block.tensor
        def _(e): run('pe', e)

        @block.scalar
        def _(e): run('act', e)

        @block.vector
        def _(e): run('dve', e)

        @block.gpsimd
        def _(e): run('pool', e)

        @block.sync
        def _(e): run('sp', e)


class Rot:
    def __init__(self, items): self.items = items; self.i = 0

    def get(self):
        b = self.items[self.i % len(self.items)]; self.i += 1
        return b


def build(depth=DEPTH):
    nc = bass.Bass("TRN2", target_bir_lowering=False)
    P = Prog()
    es = ExitStack()

    def din(name, shape, dt=F32):
        return nc.dram_tensor(name, shape, dt, kind="ExternalInput").ap()

    def dint(name, shape, dt):
        return nc.dram_tensor(name, shape, dt, kind="Internal").ap()

    xT_in = din("xT", [128, 8, XW])
    cT_in = din("cT", [128, 8, 2])
    rope_in = din("rope", [128, 2, 5, 512])
    rkt_in = din("rkt", [128, 32])
    cst_in = din("cst", [128, NCST])
    pcol_in = din("pcol", [depth, 128, NPC])
    brow_in = din("brow", [depth, 1, NBROW])
    fng_in = din("fng", [128, 8])
    w_ada = din("w_ada", [depth, D, 6144]); w_in = din("w_in", [depth, D, 6912])
    w_pa = din("w_pa", [depth, 512, D]); w_pb = din("w_pb", [depth, 512, D]); w_pc = din("w_pc", [depth, 512, D])
    w_out = din("w_out", [depth, D, D]); w_up = din("w_up", [depth, D, 5632]); w_down = din("w_down", [depth, 2816, D])
    out_d = nc.dram_tensor("out", [128, 8, NLAT], F32, kind="ExternalOutput").ap()
    DBG = False
    if DBG:
        dbg16 = nc.dram_tensor("dbg16", [40, 128, 512], BF16, kind="ExternalOutput").ap()
        dbg32 = nc.dram_tensor("dbg32", [12, 128, 512], F32, kind="ExternalOutput").ap()

    def dump16(i, t):
        if DBG: P.op('sp', lambda e: e.dma_start(out=dbg16[i], in_=t.ap), r=[t], dma=True)

    def dump32(i, ap, trs, n=512):
        if DBG: P.op('sp', lambda e: e.dma_start(out=dbg32[i][:, 0:n], in_=ap), r=trs, dma=True)

    wada16 = dint("wada16", [depth, D, 6144], BF16); win16 = dint("win16", [depth, D, 6912], BF16)
    wpa16 = dint("wpa16", [depth, 512, D], BF16); wpb16 = dint("wpb16", [depth, 512, D], BF16)
    wpc16 = dint("wpc16", [depth, 512, D], BF16); wout16 = dint("wout16", [depth, D, D], BF16)
    wup16 = dint("wup16", [depth, D, 5632], BF16); wdn16 = dint("wdn16", [depth, 2816, D], BF16)
    EXC = 2048 + 16 * 128
    ex1_in = [dint(f"ex1i{i}", [128, EXC], BF16) for i in range(2)]
    ex1_out = [dint(f"ex1o{i}", [512, EXC], BF16) for i in range(2)]
    cxkv = dint("cxkv", [128, 256 + 2 * 128], BF16)
    ex2_in = [dint(f"ex2i{i}", [128, 1024], F32) for i in range(2)]
    ex2_out = [dint(f"ex2o{i}", [512, 1024], F32) for i in range(2)]
    ex3_in = [dint(f"ex3i{i}", [128, 240], F32) for i in range(2)]
    ex3_out = [dint(f"ex3o{i}", [512, 240], F32) for i in range(2)]
    T_ex1i = [T(a) for a in ex1_in]; T_ex1o = [T(a) for a in ex1_out]; T_cx = T(cxkv)
    T_ex2i = [T(a) for a in ex2_in]; T_ex2o = [T(a) for a in ex2_out]
    T_ex3i = [T(a) for a in ex3_in]; T_ex3o = [T(a) for a in ex3_out]
    Tw = {}

    def tw(name, l):
        if (name, l) not in Tw: Tw[(name, l)] = T(None)
        return Tw[(name, l)]
    RG = [[0, 1, 2, 3], [4, 5, 6, 7]]

    def sbt(name, shape, dt):
        t = es.enter_context(nc.sbuf_tensor("sb_" + name, shape, dt))
        return T(t[:])

    xT = sbt("xT", [128, 8, XW], F32)
    xTk = [T(None) for _ in range(8)]
    xTh = T(None)
    cst = sbt("cst", [128, NCST], F32)
    rkt = sbt("rkt", [128, 32], F32)
    pc = sbt("pc", [128, NPC], F32)
    brow = sbt("brow", [1, NBROW], BF16)
    fng = sbt("fng", [128, 8], F32)
    modTs = [sbt(f"modT{i}", [128, 48, 2], F32) for i in range(2)]
    badas = [sbt(f"bada{i}", [128, 48], F32) for i in range(2)]
    modT = modTs[0]
    mder = sbt("mder", [128, 4, 8, 2], F32)
    scT = sbt("scT", [128, 8, 2], BF16)
    cTs = sbt("cTs", [128, 8, 2], F32)
    onesb = sbt("onesb", [128, 128], BF16)
    blk64 = sbt("blk64", [128, 128], BF16)
    ident = sbt("ident", [128, 128], BF16)
    rtb = sbt("rtb", [128, 128], BF16)
    onesf = sbt("onesf", [128, 128], F32)
    lgt = sbt("lgt", [128, 24], F32)
    DTt = sbt("DTt", [128, 8, 128], BF16)
    RKL = sbt("RKL", [128, 512], BF16); RKH = sbt("RKH", [128, 512], BF16)
    BMASK = sbt("BMASK", [128, 512], BF16)
    QFt = sbt("QFt", [128, 2, 4, 128], F32)
    KFt = sbt("KFt", [128, 2, 8], F32)
    CDt = sbt("CDt", [128, 2, 4], F32)
    DECt = sbt("DECt", [128, 2, 16, 4], F32)
    coef = sbt("coef", [128, 10, 4], F32)
    smallf = sbt("smallf", [128, 64], F32)
    Wt = [sbt(f"W{i}", [128, 4096], BF16) for i in range(2)]
    Wrot = Rot(Wt)
    GT = 4
    KTG0 = Rot([sbt(f"KTG0{i}", [128, GT * 128], BF16) for i in range(2)])
    KTG1 = Rot([sbt(f"KTG1{i}", [128, GT * 128], BF16) for i in range(2)])
    VG0 = Rot([sbt(f"VG0{i}", [128, GT, 128], BF16) for i in range(2)])
    VG1 = Rot([sbt(f"VG1{i}", [128, GT, 128], BF16) for i in range(2)])
    SBL = Rot([sbt(f"SBL{i}", [128, 512], BF16) for i in range(2)])
    sbscr = dint("sbscr", [18, 128, 512], BF16)
    T_sbs = [T(None) for _ in range(18)]
    NG = 39
    Gall = [sbt(f"G{i}", [128, 512], BF16) for i in range(NG)]
    gfree = list(Gall)

    def galloc(n):
        r = gfree[:n]; del gfree[:n]
        assert len(r) == n, "G pool exhausted"
        return r

    def gfree_(lst): gfree.extend(lst)
    FT = Rot([sbt(f"FT{i}", [128, 514], F32) for i in range(4)])
    STG = [sbt(f"STG{i}", [128, 512], BF16) for i in range(3)]
    D0 = sbt("D0", [128, 512], F32); D1 = sbt("D1", [128, 512], F32); D2 = sbt("D2", [128, 512], F32)
    ACC = [sbt(f"ACC{i}", [128, 512], F32) for i in range(4)]
    SQ = Rot([sbt(f"SQ{i}", [128, 512], BF16) for i in range(3)])
    Sf = sbt("Sf", [128, 512], F32); Sf16 = sbt("Sf16", [128, 512], BF16)
    Sbk = sbt("Sbk", [128, 512], F32); Tfk = sbt("Tfk", [128, 512], F32)
    Ssf = sbt("Ssf", [128, 512], F32); Ssb = sbt("Ssb", [128, 512], F32)
    DG = Rot([sbt(f"DG{i}", [128, 128], BF16) for i in range(4)])
    uh = sbt("uh", [128, 4, 150], BF16)
    gth = sbt("gth", [128, 22, 10], F32)
    ubuf = [sbt(f"ubuf{i}", [128, 542], BF16) for i in range(4)]
    pst = [T(es.enter_context(nc.psum_tensor(f"ps{i}", [128, 512], F32))[:]) for i in range(8)]
    PSL = pst[0:4]
    PSR4 = Rot(pst[4:8]); PSR8 = Rot(pst[0:8])
    PSRh = [PSR4]

    class _PSR:
        def get(self): return PSRh[0].get()
    PSR = _PSR()

    def A(t): return t.ap

    def dma(eng, out_ap, in_ap, r=(), w=()):
        return P.op(eng, lambda e: e.dma_start(out=out_ap, in_=in_ap), r=r, w=w, dma=True)

    def mm(ps_ap, lhsT, rhs, start, stop, r, w):
        return P.op('pe', lambda e: e.matmul(ps_ap, lhsT, rhs, start=start, stop=stop), r=r, w=w)

    def act(out, in_, func, r, w, bias=None, scale=None, eng='act'):
        kw = {}
        if bias is not None: kw['bias'] = bias
        if scale is not None: kw['scale'] = scale
        return P.op('act', lambda e: e.activation(out, in_, func, **kw), r=r, w=w)

    def tt(eng, out, a, b, op, r, w):
        if eng == 'pool': eng = 'dve'
        return P.op(eng, lambda e: e.tensor_tensor(out, a, b, op), r=r, w=w)

    def rsq(out, in_, scale, eps, r, w):
        P.op('dve', lambda e: e.tensor_scalar(out, in_, scale, eps, ALU.mult, ALU.add), r=r, w=w)
        P.op('act', lambda e: e.activation(out, out, AF.Ln), r=w, w=w)
        P.op('act', lambda e: e.activation(out, out, AF.Exp, scale=-0.5), r=w, w=w)

    def ts(eng, out, a, s1, s2, op0, op1, r, w):
        eng = 'dve'
        if s2 is None:
            return P.op(eng, lambda e: e.tensor_scalar(out, a, s1, None, op0), r=r, w=w)
        return P.op(eng, lambda e: e.tensor_scalar(out, a, s1, s2, op0, op1), r=r, w=w)

    def stt(eng, out, a, s, b, op0, op1, r, w):
        eng = 'dve'
        return P.op(eng, lambda e: e.scalar_tensor_tensor(out, a, s, b, op0, op1), r=r, w=w)

    def cp(eng, out, in_, r, w):
        if eng == 'pool': eng = 'dve'
        if eng == 'act':
            return P.op('act', lambda e: e.activation(out, in_, AF.Copy), r=r, w=w)
        return P.op(eng, lambda e: e.tensor_copy(out, in_), r=r, w=w)

    def mset(eng, ap, val, w):
        return P.op(eng, lambda e: e.memset(ap, val), w=w)

    xa = A(xT)

    def xk(k): return [xT, xTk[k]]

    dma('sp', xa, xT_in, w=[xT] + xTk + [xTh])
    dma('sp', A(cst), cst_in, w=[cst])
    dma('sp', A(rkt), rkt_in, w=[rkt])
    dma('sp', A(fng), fng_in, w=[fng])
    dma('sp', A(cTs), cT_in, w=[cTs])
    for l in range(depth if (STOP > 0 and 0) else 0):
        def cast(nm, dst, src, rows, rstep):
            for r0 in range(0, rows, rstep):
                dma('pool', dst[r0:r0 + rstep, :], src[r0:r0 + rstep, :], w=[tw(nm, l)])
        cast('ada', wada16[l], w_ada[l], D, 256)
        for r0 in range(0, D, 256):
            rs = slice(r0, r0 + 256)
            dma('pool', win16[l][rs, 0:1024], w_in[l][rs, 0:1024], w=[tw('in', l)])
            for half in range(2):
                dma('pool', win16[l][rs, 1024:1536].rearrange("k (c h d) -> k h c d", c=4, h=2)[:, half],
                    w_in[l][rs, 1024:1536].rearrange("k (h c d) -> k h c d", h=2, c=4)[:, half], w=[tw('in', l)])
            dma('pool', win16[l][rs, 1536:6912], w_in[l][rs, 1536:6912], w=[tw('in', l)])
        cast('pa', wpa16[l], w_pa[l], 512, 512)
        for half in range(2):
            dma('pool', wpb16[l].rearrange("(c h d) n -> h c d n", c=4, h=2)[half],
                w_pb[l].rearrange("(h c d) n -> h c d n", h=2, c=4)[half], w=[tw('pb', l)])
        cast('pc', wpc16[l], w_pc[l], 512, 512)
        cast('out', wout16[l], w_out[l], D, 512)
        cast('up', wup16[l], w_up[l], D, 256)
        cast('dn', wdn16[l], w_down[l], 2816, 704)

    c_ = A(cst)
    RELA = c_[:, 0:128]; RELB = c_[:, 128:256]; EYE = c_[:, 256:384]
    IOTA1 = c_[:, 384:512]; IOTAC = c_[:, 512:640]; REV = c_[:, 640:641]; JCOL = c_[:, 641:642]
    EXPO = c_[:, 642:658]; RTF = c_[:, 658:786]; SWAP = c_[:, 786:914]
    rk_ = A(rkt)
    mset('dve', A(onesb), 1.0, w=[onesb]); mset('dve', A(onesf), 1.0, w=[onesf])
    mset('dve', A(blk64), 0.0, w=[blk64])
    mset('dve', A(blk64)[0:64, 0:64], 1.0, w=[blk64]); mset('dve', A(blk64)[64:128, 64:128], 1.0, w=[blk64])
    cp('dve', A(ident), EYE, r=[cst], w=[ident]); cp('dve', A(rtb), RTF, r=[cst], w=[rtb])
    for t in KTG0.items + KTG1.items + [RKL, RKH, BMASK]: mset('dve', A(t), 0.0, w=[t])
    for c in range(4):
        mset('dve', A(BMASK)[0:64, c * 128:c * 128 + 64], 1.0, w=[BMASK])
        mset('dve', A(BMASK)[64:128, c * 128 + 64:c * 128 + 128], 1.0, w=[BMASK])
    for t in VG0.items: mset('dve', A(t), 1.0, w=[t])
    for t in VG1.items: mset('dve', A(t), 1.0, w=[t])
    act(A(scT), A(cTs), AF.Silu, r=[cTs], w=[scT])

    BLK = [dict(ws=512 * j, mc=15 + 512 * j, N=512, lat=True, j=j) for j in range(4)]
    BLK.append(dict(ws=2078, mc=CTX0, N=256, lat=False, j=4))

    WE = ['pool']

    def wload(src_ap, kc, n, wtr, eng=None):
        eng = eng or WE[0]
        wt = Wrot.get()
        dst = A(wt)[:, 0:kc * n].rearrange("p (k n) -> p k n", k=kc)
        dma(eng, dst, src_ap.rearrange("(k p) n -> p k n", p=128), r=[wtr], w=[wt])
        return wt, dst

    def normrope(ps_in, bias_col, gcol, gscale, N, out_ap, out_tr):
        xb = FT.get()
        act(A(xb)[:, :N], A(ps_in)[:, :N], AF.Identity, r=[ps_in, pc], w=[xb], bias=bias_col)
        sq = SQ.get()
        act(A(sq)[:, :N], A(xb)[:, :N], AF.Square, r=[xb], w=[sq])
        ps2 = PSR.get()
        mm(A(ps2)[:, :N], A(blk64), A(sq)[:, :N], True, True, r=[blk64, sq], w=[ps2])
        Rr = FT.get()
        rsq(A(Rr)[:, :N], A(ps2)[:, :N], 1.0, 64 * EPS, [ps2], [Rr])
        ts('dve', A(xb)[:, :N], A(xb)[:, :N], gcol, gscale, ALU.mult, ALU.mult, r=[xb, pc], w=[xb])
        xg = SQ.get()
        cp('act', A(xg)[:, :N], A(xb)[:, :N], r=[xb], w=[xg])
        ps3 = PSR.get()
        mm(A(ps3)[:, :N], A(rtb), A(xg)[:, :N], True, True, r=[rtb, xg], w=[ps3])
        t1 = FT.get(); t2 = FT.get()
        tt('pool', A(t1)[:, :N], A(xb)[:, :N], A(D1)[:, :N], ALU.mult, r=[xb, D1], w=[t1])
        tt('dve', A(t2)[:, :N], A(ps3)[:, :N], A(D2)[:, :N], ALU.mult, r=[ps3, D2], w=[t2])
        tt('dve', A(t1)[:, :N], A(t1)[:, :N], A(t2)[:, :N], ALU.add, r=[t1, t2], w=[t1])
        tt('dve', out_ap, A(t1)[:, :N], A(Rr)[:, :N], ALU.mult, r=[t1, Rr], w=out_tr)

    def make_h(xsrc, rdeps, N, Gap, SHap, outs):
        ps = PSR.get()
        for k in range(8):
            sq = SQ.get()
            act(A(sq)[:, :N], xsrc[k], AF.Square, r=rdeps[k], w=[sq])
            mm(A(ps)[:, :N], A(onesb), A(sq)[:, :N], k == 0, k == 7, r=[onesb, sq], w=[ps])
        R = D0
        rsq(A(R)[:, :N], A(ps)[:, :N], 1.0, D * EPS, [ps], [R])
        for k in range(8):
            t = FT.get()
            stt('dve', A(t)[:, :N], xsrc[k], Gap(k), A(R)[:, :N], ALU.mult, ALU.mult, r=rdeps[k] + [R, mder], w=[t])
            oap, otr = outs[k]
            act(oap, A(t)[:, :N], AF.Identity, r=[t, MTH[0]], w=otr, bias=SHap(k))

    def exch_xhalo(par):
        exch_issue(par); exch_finish(par)

    def exch_issue(par):
        o = ex3_in[par].rearrange("p (k s) -> p k s", k=8)
        dma('sp', o[:, :, 0:15], xa[:, :, 15:30], r=[xT] + xTk, w=[T_ex3i[par]])
        dma('sp', o[:, :, 15:30], xa[:, :, 15 + 2033:15 + 2048], r=[xT] + xTk, w=[T_ex3i[par]])
        P.op('pool', lambda e: e.collective_compute("AllGather", ALU.bypass, replica_groups=RG,
                                                    ins=[ex3_in[par]], outs=[ex3_out[par]]),
             r=[T_ex3i[par]], w=[T_ex3o[par]], cc=True)

    def exch_finish(par):
        L = xa[:, :, 0:15]; Rr = xa[:, :, 2063:2078]
        for q in range(4):
            xq = FT.get()
            dma('sp', A(xq)[:, 0:240], ex3_out[par][q * 128:(q + 1) * 128, :], r=[T_ex3o[par]], w=[xq])
            xr = A(xq)[:, 0:240].rearrange("p (k s) -> p k s", k=8)
            if q == 0:
                ts('dve', L, xr[:, :, 15:30], rk_[:, 0:1], None, ALU.mult, None, r=[xq, rkt, xTh], w=[xTh])
                ts('dve', Rr, xr[:, :, 0:15], rk_[:, 4:5], None, ALU.mult, None, r=[xq, rkt, xTh], w=[xTh])
            else:
                stt('dve', L, xr[:, :, 15:30], rk_[:, q:q + 1], L, ALU.mult, ALU.add, r=[xq, rkt, xTh], w=[xTh])
                stt('dve', Rr, xr[:, :, 0:15], rk_[:, 4 + q:5 + q], Rr, ALU.mult, ALU.add, r=[xq, rkt, xTh], w=[xTh])

    pca = A(pc)

    def bcol(j): return pca[:, 64 + j:65 + j]

    md = A(mder)
    mo = A(modT)
    MTH = [modT]

    def cvt_items(l):
        items = []

        def plain(nm, dst, src, rows, cols, c_lo=0, c_hi=None):
            c_hi = cols if c_hi is None else c_hi
            for k in range(rows // 128):
                for c0 in range(c_lo, c_hi, 512):
                    n = min(512, c_hi - c0)
                    rs = slice(k * 128, (k + 1) * 128)
                    items.append(([(lambda sl, n=n: sl[:, 0:n], src[rs, c0:c0 + n])], dst[rs, c0:c0 + n], n, tw(nm, l)))
        plain('in', win16[l], w_in[l], D, 6912, 0, 1024)
        for k in range(8):
            rs = slice(k * 128, (k + 1) * 128)
            lds = []
            for c in range(4):
                for hf in range(2):
                    h_ = hf * 4 + c
                    lds.append((lambda sl, o=c * 128 + hf * 64: sl[:, o:o + 64], w_in[l][rs, 1024 + h_ * 64:1024 + (h_ + 1) * 64]))
            items.append((lds, win16[l][rs, 1024:1536], 512, tw('in', l)))
        plain('in', win16[l], w_in[l], D, 6912, 1536, 6912)
        plain('pa', wpa16[l], w_pa[l], 512, D)
        for c in range(4):
            for c0 in (0, 512):
                lds = []
                for hf in range(2):
                    h_ = hf * 4 + c
                    lds.append((lambda sl, hf=hf: sl[hf * 64:(hf + 1) * 64, 0:512], w_pb[l][h_ * 64:(h_ + 1) * 64, c0:c0 + 512]))
                items.append((lds, wpb16[l][c * 128:(c + 1) * 128, c0:c0 + 512], 512, tw('pb', l)))
        plain('pc', wpc16[l], w_pc[l], 512, D)
        plain('out', wout16[l], w_out[l], D, D)
        plain('up', wup16[l], w_up[l], D, 5632)
        plain('dn', wdn16[l], w_down[l], 2816, D)
        return items

    class Cvt:
        def __init__(self): self.q = []; self.pending = []; self.stg = None; self.i = 0

        def start(self, l, stg=None):
            self.q = cvt_items(l); self.stg = stg or STG; self.i = 0; self.pending = []; self.own = stg

        def step(self, nitems):
            for _ in range(nitems):
                if not self.q: break
                lds, dst, n, trk = self.q.pop(0)
                slot = self.stg[self.i % len(self.stg)]; self.i += 1
                for fn, src in lds:
                    dma('pool', fn(A(slot)), src, w=[slot])
                self.pending.append((slot, dst, n, trk))
                if len(self.pending) > len(self.stg) - 1: self.flush(len(self.stg) - 1)

        def flush(self, keep):
            while len(self.pending) > keep:
                slot, dst, n, trk = self.pending.pop(0)
                dma('pool', dst, A(slot)[:, 0:n], r=[slot], w=[trk])

        def finish(self):
            self.step(10 ** 9); self.flush(0)
            if self.own: gfree_(self.own); self.own = None
    CV = Cvt()

    for l in range(depth):
        par = l % 2
        WIN, WADA, WPA, WPB, WPC, WOUT, WUP, WDN = (win16[l], wada16[l], wpa16[l], wpb16[l], wpc16[l], wout16[l], wup16[l], wdn16[l])
        WENG = 'sp'
        WE[0] = WENG
        exch_issue(par)
        if l == 0:
            CV.start(0)
        elif l + 1 < depth:
            CV.start(l + 1)
            n_cv = (len(CV.q) + 9) // 10
        dma('sp', pca, pcol_in[l], w=[pc])
        dma('pool', A(brow), brow_in[l], w=[brow])
        def emit_adaln(ll):
            mt_ = modTs[ll % 2]; ba_ = badas[ll % 2]
            dma('sp', A(ba_), pcol_in[ll][:, 0:48], w=[ba_])
            for pcs in range(12):
                wt, wv = wload(w_ada[ll][:, pcs * 512:(pcs + 1) * 512], 8, 512, tw('ada_unused', ll), eng='pool')
                for jj in range(4):
                    j_ = pcs * 4 + jj
                    ps = PSR.get()
                    for k in range(8):
                        mm(A(ps)[:, 0:2], wv[:, k, jj * 128:(jj + 1) * 128], A(scT)[:, k, :], k == 0, k == 7,
                           r=[wt, scT], w=[ps])
                    act(A(mt_)[:, j_, :], A(ps)[:, 0:2], AF.Identity, r=[ps, ba_], w=[mt_], bias=A(ba_)[:, j_:j_ + 1])
        if l == 0: emit_adaln(0)
        modT = modTs[l % 2]; mo = A(modT); MTH[0] = modT
        for (gi, ncol, sccol) in ((0, 48, 8), (1, 56, 32)):
            ts('dve', md[:, gi], mo[:, sccol:sccol + 8, :], 1.0, 32.0, ALU.add, ALU.mult, r=[modT], w=[mder])
            tt('dve', md[:, gi], md[:, gi], pca[:, ncol:ncol + 8].unsqueeze(2).to_broadcast([128, 8, 2]), ALU.mult,
               r=[mder, pc], w=[mder])
        lg = A(lgt)
        act(lg, pca[:, 344:368], AF.Exp, r=[pc], w=[lgt], scale=-1.0)
        act(lg, lg, AF.Ln, r=[lgt], w=[lgt], bias=1.0)
        ts('dve', lg, lg, -1.0, None, ALU.mult, None, r=[lgt], w=[lgt])
        sf = A(smallf)
        for h in range(8):
            t1 = FT.get()
            ts('dve', A(t1)[:, 0:128], RELA, lg[:, 8 + h:9 + h], None, ALU.mult, None, r=[cst, lgt], w=[t1])
            stt('dve', A(t1)[:, 0:128], RELB, lg[:, 16 + h:17 + h], A(t1)[:, 0:128], ALU.mult, ALU.add, r=[cst, lgt, t1], w=[t1])
            act(A(t1)[:, 0:128], A(t1)[:, 0:128], AF.Exp, r=[t1], w=[t1])
            tt('dve', A(DTt)[:, h, :], A(t1)[:, 0:128], EYE, ALU.add, r=[t1, cst], w=[DTt])
        for d_ in range(2):
            for c in range(4):
                act(A(QFt)[:, d_, c, :], IOTA1 if d_ == 0 else IOTAC, AF.Exp, r=[cst, lgt], w=[QFt],
                    scale=lg[:, d_ * 4 + c:d_ * 4 + c + 1])
            ts('dve', A(KFt)[:, d_, :], lg[:, 8 + 8 * d_:16 + 8 * d_], REV if d_ == 0 else JCOL, None, ALU.mult, None,
               r=[lgt, cst], w=[KFt])
            act(A(KFt)[:, d_, :], A(KFt)[:, d_, :], AF.Exp, r=[KFt], w=[KFt])
            act(A(CDt)[:, d_, :], lg[:, d_ * 4:d_ * 4 + 4], AF.Exp, r=[lgt], w=[CDt], scale=128.0)
            tt('dve', A(DECt)[:, d_], lg[:, d_ * 4:d_ * 4 + 4].unsqueeze(1).to_broadcast([128, 16, 4]),
               EXPO.unsqueeze(2).to_broadcast([128, 16, 4]), ALU.mult, r=[lgt, cst], w=[DECt])
            act(A(DECt)[:, d_], A(DECt)[:, d_], AF.Exp, r=[DECt], w=[DECt])
            eo = 10 if d_ == 0 else 19
            for q in range(5):
                ecol = rk_[:, eo + q:eo + q + 1] if q < 4 else rk_[:, eo + 8:eo + 9]
                ci = d_ * 5 + q
                ts('dve', A(coef)[:, ci, :], lg[:, d_ * 4:d_ * 4 + 4], ecol, None, ALU.mult, None, r=[lgt, rkt], w=[coef])
                act(A(coef)[:, ci, :], A(coef)[:, ci, :], AF.Exp, r=[coef], w=[coef])
                if q < 4:
                    ts('dve', A(coef)[:, ci, :], A(coef)[:, ci, :], rk_[:, eo + 4 + q:eo + 5 + q], None, ALU.mult, None,
                       r=[coef, rkt], w=[coef])

        G1 = lambda t: (lambda k: md[:, 0, k, t:t + 1])
        SH1 = lambda t: (lambda k: mo[:, k, t:t + 1])
        G2 = lambda t: (lambda k: md[:, 1, k, t:t + 1])
        SH2 = lambda t: (lambda k: mo[:, 24 + k, t:t + 1])

        if STOP <= 1: break
        if l == 0:
            CV.step(112); CV.flush(0)

        if STOP <= 2: break
        PSRh[0] = PSR8
        v4 = lambda ap: ap.rearrange("p (c n) -> p c n", c=4)
        mset('dve', A(Sbk), 0.0, w=[Sbk]); mset('dve', A(Tfk), 0.0, w=[Tfk])
        order = [BLK[4], BLK[3], BLK[2], BLK[1], BLK[0]]
        for b in order:
            j = b['j']; N = b['N']; mc = b['mc']; lat = b['lat']; tcol = 0 if lat else 1
            nt = N // 128
            hT = galloc(8)
            make_h([xa[:, k, mc:mc + N] for k in range(8)], [xk(k) for k in range(8)], N, G1(tcol), SH1(tcol),
                   [(A(hT[k])[:, :N], [hT[k]]) for k in range(8)])
            wk, wkv = wload(WIN[:, 1536:1792], 8, 256, tw('in', l))
            dma('sp', A(D1)[:, :N], rope_in[:, 0, j, 0:N], w=[D1])
            dma('sp', A(D2)[:, :N], rope_in[:, 1, j, 0:N], w=[D2])
            ps = PSR.get()
            for k in range(8):
                mm(A(ps)[:, :N], wkv[:, k, 0:128], A(hT[k])[:, :N], k == 0, k == 7, r=[wk, hT[k]], w=[ps])
            kT = galloc(1)[0]
            normrope(ps, bcol(12), pca[:, 255:256], 8.0, N, A(kT)[:, :N], [kT])
            vt = galloc(1)[0]
            for t in range(nt):
                ps2 = PSR.get()
                for k in range(8):
                    mm(A(ps2)[:, 0:128], A(hT[k])[:, t * 128:(t + 1) * 128], wkv[:, k, 128:256], k == 0, False,
                       r=[wk, hT[k]], w=[ps2])
                mm(A(ps2)[:, 0:128], A(onesb)[0:1, :], A(brow)[0:1, 0:128], False, True, r=[onesb, brow], w=[ps2])
                cp('act', A(vt)[:, t * 128:(t + 1) * 128], A(ps2)[:, 0:128], r=[ps2], w=[vt])
            if lat:
                dma('pool', ex1_in[par][:, j * 512:(j + 1) * 512], A(kT)[:, :N], r=[kT], w=[T_ex1i[par]])
                dma('pool', ex1_in[par][:, 2048 + j * 512:2048 + (j + 1) * 512], A(vt)[:, :N], r=[vt], w=[T_ex1i[par]])
            else:
                dma('pool', cxkv[:, 0:256], A(kT)[:, :N], r=[kT], w=[T_cx])
                dma('pool', cxkv[:, 256:512], A(vt)[:, :N], r=[vt], w=[T_cx])
            gfree_([kT, vt])
            wrk, wrkv = wload(WIN[:, 2304:2816], 8, 512, tw('in', l))
            wrv, wrvv = wload(WIN[:, 2816:3328], 8, 512, tw('in', l))
            def p1_proj(t):
                rkt_ = galloc(1)[0]; rvt_ = galloc(1)[0]
                for (wt_, wv_, boff, dst, scale) in ((wrk, wrkv, 128, rkt_, 0.125), (wrv, wrvv, 640, rvt_, 1.0)):
                    ps3 = PSR.get()
                    for k in range(8):
                        mm(A(ps3), A(hT[k])[:, t * 128:(t + 1) * 128], wv_[:, k, :], k == 0, False, r=[wt_, hT[k]], w=[ps3])
                    mm(A(ps3), A(onesb)[0:1, :], A(brow)[0:1, boff:boff + 512], False, True, r=[onesb, brow], w=[ps3])
                    act(A(dst), A(ps3), AF.Copy, r=[ps3], w=[dst], scale=scale)
                kd = galloc(2)
                for d_ in range(2):
                    tt('dve', A(kd[d_]).rearrange("p (h d) -> p h d", h=8), A(rkt_).rearrange("p (h d) -> p h d", h=8),
                       A(KFt)[:, d_, :].unsqueeze(2).to_broadcast([128, 8, 64]), ALU.mult, r=[rkt_, KFt], w=[kd[d_]])
                return (t, rkt_, rvt_, kd)

            def p1_scan(st):
                t, rkt_, rvt_, kd = st
                cidx = (j * 4 + t) if lat else (16 + t)
                dci = (j * 4 + t) if lat else (14 + t)
                sbl = SBL.get()
                tt('dve', A(sbl), A(Sbk), A(BMASK), ALU.mult, r=[Sbk, BMASK], w=[sbl])
                if True:
                    dma('pool', sbscr[cidx], A(sbl), r=[sbl], w=[T_sbs[cidx]])
                psB = PSR.get(); psF = PSR.get()
                for c in range(4):
                    cs_ = slice(c * 128, (c + 1) * 128)
                    mm(A(psB)[:, cs_], A(kd[1])[:, cs_], A(rvt_)[:, cs_], True, True, r=[kd[1], rvt_], w=[psB])
                    mm(A(psF)[:, cs_], A(kd[0])[:, cs_], A(rvt_)[:, cs_], True, True, r=[kd[0], rvt_], w=[psF])
                tt('dve', v4(A(Sbk)), v4(A(Sbk)), A(CDt)[:, 1, :].unsqueeze(2).to_broadcast([128, 4, 128]), ALU.mult,
                   r=[Sbk, CDt], w=[Sbk])
                tt('dve', A(Sbk), A(Sbk), A(psB), ALU.add, r=[Sbk, psB], w=[Sbk])
                tf = FT.get()
                tt('dve', v4(A(tf)[:, 0:512]), v4(A(psF)), A(DECt)[:, 0, dci, :].unsqueeze(2).to_broadcast([128, 4, 128]),
                   ALU.mult, r=[psF, DECt], w=[tf])
                tt('pool', A(Tfk), A(Tfk), A(tf)[:, 0:512], ALU.add, r=[Tfk, tf], w=[Tfk])
                gfree_([rkt_, rvt_] + kd)
            pend1 = None
            for t in reversed(range(nt)):
                st = p1_proj(t)
                if pend1 is not None: p1_scan(pend1)
                pend1 = st
            p1_scan(pend1)
            if not lat:
                tt('dve', v4(A(Ssf)), v4(A(Tfk)), A(coef)[:, 4, :].unsqueeze(2).to_broadcast([128, 4, 128]), ALU.mult,
                   r=[Tfk, coef], w=[Ssf])
                tt('dve', v4(A(Ssb)), v4(A(Sbk)), A(coef)[:, 9, :].unsqueeze(2).to_broadcast([128, 4, 128]), ALU.mult,
                   r=[Sbk, coef], w=[Ssb])
                mset('dve', A(Sbk), 0.0, w=[Sbk]); mset('dve', A(Tfk), 0.0, w=[Tfk])
            gfree_(hT)
        dma('pool', ex2_in[par][:, 0:512], A(Tfk), r=[Tfk], w=[T_ex2i[par]])
        dma('pool', ex2_in[par][:, 512:1024], A(Sbk), r=[Sbk], w=[T_ex2i[par]])
        PSRh[0] = PSR4
        if STOP <= 3: break
        P.op('pool', lambda e, par=par: e.collective_compute("AllGather", ALU.bypass, replica_groups=RG,
                                                              ins=[ex1_in[par]], outs=[ex1_out[par]]),
             r=[T_ex1i[par]], w=[T_ex1o[par]], cc=True)
        P.op('pool', lambda e, par=par: e.collective_compute("AllGather", ALU.bypass, replica_groups=RG,
                                                              ins=[ex2_in[par]], outs=[ex2_out[par]]),
             r=[T_ex2i[par]], w=[T_ex2o[par]], cc=True)
        exch_finish(par)
        xhk = lambda k: A(ACC[k // 2])[:, (k % 2) * 256:(k % 2) * 256 + 150]
        xht = lambda k: ACC[k // 2]
        for b in BLK:
            j = b['j']; ws = b['ws']; N = b['N']
            for k in range(8):
                cp('pool', xhk(k)[:, j * 30:j * 30 + 15], xa[:, k, ws:ws + 15], r=xk(k) + [xTh], w=[xht(k)])
                cp('pool', xhk(k)[:, j * 30 + 15:j * 30 + 30], xa[:, k, ws + 15 + N:ws + 30 + N], r=xk(k) + [xTh], w=[xht(k)])
        hhs = galloc(3)
        hhk = lambda k: A(hhs[k // 3])[:, (k % 3) * 150:(k % 3) * 150 + 150]
        hht = lambda k: hhs[k // 3]
        make_h([xhk(k)[:, 0:120] for k in range(8)], [[xht(k)] for k in range(8)], 120, G1(0), SH1(0),
               [(hhk(k)[:, 0:120], [hht(k)]) for k in range(8)])
        make_h([xhk(k)[:, 120:150] for k in range(8)], [[xht(k)] for k in range(8)], 30, G1(1), SH1(1),
               [(hhk(k)[:, 120:150], [hht(k)]) for k in range(8)])
        wa, wav = wload(WIN[:, 0:512], 8, 512, tw('in', l))
        wb, wbv = wload(WIN[:, 512:1024], 8, 512, tw('in', l))
        uha = A(uh)
        for c in range(4):
            psa = PSR.get(); psb = PSR.get()
            for k in range(8):
                mm(A(psa)[:, :150], wav[:, k, c * 128:(c + 1) * 128], hhk(k), k == 0, k == 7, r=[wa, hht(k)], w=[psa])
            for k in range(8):
                mm(A(psb)[:, :150], wbv[:, k, c * 128:(c + 1) * 128], hhk(k), k == 0, k == 7, r=[wb, hht(k)], w=[psb])
            sg = FT.get()
            act(A(sg)[:, :150], A(psb)[:, :150], AF.Sigmoid, r=[psb, pc], w=[sg], bias=bcol(4 + c))
            stt('dve', uha[:, c, :], A(psa)[:, :150], bcol(c), A(sg)[:, :150], ALU.add, ALU.mult, r=[psa, sg, pc], w=[uh])
            ts('dve', uha[:, c, 0:15], uha[:, c, 0:15], rk_[:, 8:9], None, ALU.mult, None, r=[uh, rkt], w=[uh])
            ts('dve', uha[:, c, 105:120], uha[:, c, 105:120], rk_[:, 9:10], None, ALU.mult, None, r=[uh, rkt], w=[uh])
            mset('dve', uha[:, c, 120:150], 0.0, w=[uh])
        gfree_(hhs)

        if STOP <= 4: break
        if l == 0:
            CV.finish()
            if depth > 1:
                CV.start(1)
                n_cv = (len(CV.q) + 9) // 10
        for b in BLK:
            j = b['j']; N = b['N']; mc = b['mc']; ws = b['ws']; lat = b['lat']; tcol = 0 if lat else 1
            nt = N // 128
            if not lat: exch_issue(1 - par)
            if l == depth - 1 and not lat: continue
            if l + 1 < depth: CV.step(n_cv)
            hT = galloc(8)
            make_h([xa[:, k, mc:mc + N] for k in range(8)], [xk(k) for k in range(8)], N, G1(tcol), SH1(tcol),
                   [(A(hT[k])[:, :N], [hT[k]]) for k in range(8)])

            if l == 0 and j == 0:
                for k in range(8): dump16(k, hT[k])
            wrq, wrqv = wload(WIN[:, 1792:2304], 8, 512, tw('in', l))
            rqT = galloc(4); rkT = galloc(4)
            for c in range(4):
                ps = PSR.get()
                for k in range(8):
                    mm(A(ps)[:, :N], wrqv[:, k, c * 128:(c + 1) * 128], A(hT[k])[:, :N], k == 0, k == 7, r=[wrq, hT[k]], w=[ps])
                act(A(rqT[c])[:, :N], A(ps)[:, :N], AF.Identity, r=[ps, pc], w=[rqT[c]], bias=bcol(14 + c))
            wrk, wrkv = wload(WIN[:, 2304:2816], 8, 512, tw('in', l))
            for c in range(4):
                ps = PSR.get()
                for k in range(8):
                    mm(A(ps)[:, :N], wrkv[:, k, c * 128:(c + 1) * 128], A(hT[k])[:, :N], k == 0, k == 7, r=[wrk, hT[k]], w=[ps])
                ts('dve', A(rkT[c])[:, :N], A(ps)[:, :N], bcol(18 + c), 0.125, ALU.add, ALU.mult, r=[ps, pc], w=[rkT[c]])
            rkM = galloc(nt); rvM = galloc(nt); rgM = galloc(nt)
            for t in range(nt):
                ps3 = PSR.get()
                for k in range(8):
                    mm(A(ps3), A(hT[k])[:, t * 128:(t + 1) * 128], wrkv[:, k, :], k == 0, False, r=[wrk, hT[k]], w=[ps3])
                mm(A(ps3), A(onesb)[0:1, :], A(brow)[0:1, 128:640], False, True, r=[onesb, brow], w=[ps3])
                act(A(rkM[t]), A(ps3), AF.Copy, r=[ps3], w=[rkM[t]], scale=0.125)
            if STOP <= 4.05: break
            wrv, wrvv = wload(WIN[:, 2816:3328], 8, 512, tw('in', l))
            for t in range(nt):
                ps3 = PSR.get()
                for k in range(8):
                    mm(A(ps3), A(hT[k])[:, t * 128:(t + 1) * 128], wrvv[:, k, :], k == 0, False, r=[wrv, hT[k]], w=[ps3])
                mm(A(ps3), A(onesb)[0:1, :], A(brow)[0:1, 640:1152], False, True, r=[onesb, brow], w=[ps3])
                cp('act', A(rvM[t]), A(ps3), r=[ps3], w=[rvM[t]])
            wrg, wrgv = wload(WIN[:, 3328:3840], 8, 512, tw('in', l))
            for t in range(nt):
                ps3 = PSR.get()
                for k in range(8):
                    mm(A(ps3), A(hT[k])[:, t * 128:(t + 1) * 128], wrgv[:, k, :], k == 0, False, r=[wrg, hT[k]], w=[ps3])
                mm(A(ps3), A(onesb)[0:1, :], A(brow)[0:1, 1152:1664], False, True, r=[onesb, brow], w=[ps3])
                act(A(rgM[t]), A(ps3), AF.Silu, r=[ps3], w=[rgM[t]])
            wa, wav = wload(WIN[:, 0:512], 8, 512, tw('in', l))
            wb, wbv = wload(WIN[:, 512:1024], 8, 512, tw('in', l))
            for c in range(4):
                psa = PSR.get(); psb = PSR.get()
                for k in range(8):
                    mm(A(psa)[:, :N], wav[:, k, c * 128:(c + 1) * 128], A(hT[k])[:, :N], k == 0, k == 7, r=[wa, hT[k]], w=[psa])
                for k in range(8):
                    mm(A(psb)[:, :N], wbv[:, k, c * 128:(c + 1) * 128], A(hT[k])[:, :N], k == 0, k == 7, r=[wb, hT[k]], w=[psb])
                sg = FT.get()
                act(A(sg)[:, :N], A(psb)[:, :N], AF.Sigmoid, r=[psb, pc], w=[sg], bias=bcol(4 + c))
                u = ubuf[c]
                stt('dve', A(u)[:, 15:15 + N], A(psa)[:, :N], bcol(c), A(sg)[:, :N], ALU.add, ALU.mult, r=[psa, sg, pc], w=[u])
                cp('pool', A(u)[:, 0:15], A(uh)[:, c, j * 30:j * 30 + 15], r=[uh], w=[u])
                cp('pool', A(u)[:, 15 + N:30 + N], A(uh)[:, c, j * 30 + 15:j * 30 + 30], r=[uh], w=[u])
            wc = lambda kk, c: pca[:, 118 + c * 31 + kk:119 + c * 31 + kk]
            def conv_pe(c):
                psc = PSR.get()
                for kk in range(31):
                    dg = DG.get()
                    act(A(dg), A(ident), AF.Copy, r=[ident, pc], w=[dg], scale=wc(kk, c))
                    mm(A(psc)[:, :N], A(dg), A(ubuf[c])[:, kk:kk + N], kk == 0, kk == 30, r=[dg, ubuf[c]], w=[psc])
                act(A(ACC[c])[:, :N], A(psc)[:, :N], AF.Identity, r=[psc, pc], w=[ACC[c]], bias=pca[:, 242 + c:243 + c])
            cpc = 4 // nt
            if j == 0:
                v4 = lambda ap: ap.rearrange("p (c n) -> p c n", c=4)
                for d_, Sst in enumerate((Ssf, Ssb)):
                    for q in range(4):
                        tl = FT.get()
                        dma('sp', A(tl)[:, 0:512], ex2_out[par][q * 128:(q + 1) * 128, d_ * 512:(d_ + 1) * 512], r=[T_ex2o[par]], w=[tl])
                        tt('dve', v4(A(tl)[:, 0:512]), v4(A(tl)[:, 0:512]),
                           A(coef)[:, d_ * 5 + q, :].unsqueeze(2).to_broadcast([128, 4, 128]), ALU.mult, r=[tl, coef], w=[tl])
                        tt('dve', A(Sst), A(Sst), A(tl)[:, 0:512], ALU.add, r=[Sst, tl], w=[Sst])

                tt('dve', A(Ssb), A(Ssb), A(BMASK), ALU.mult, r=[Ssb, BMASK], w=[Ssb])
            if j == 0:
                cp('dve', A(Sf), A(Ssf), r=[Ssf], w=[Sf])
            if not lat:
                mset('dve', A(Sf), 0.0, w=[Sf])
            retT = galloc(4)
            v8 = lambda ap: ap.rearrange("p (h e) -> p h e", h=8)
            for t in range(nt):
                cidx = (j * 4 + t) if lat else (16 + t)
                tsl = slice(t * 128, (t + 1) * 128)
                tt('dve', A(Sf16), A(Sf), A(BMASK), ALU.mult, r=[Sf, BMASK], w=[Sf16])
                sbl = SBL.get()
                dma('sp', A(sbl), sbscr[cidx], r=[T_sbs[cidx]], w=[sbl])
                if STOP <= 4.101: break
                if lat:
                    sbu = galloc(1)[0]
                    tl = FT.get()
                    tt('dve', v4(A(tl)[:, 0:512]), v4(A(Ssb)), A(DECt)[:, 1, j * 4 + t, :].unsqueeze(2).to_broadcast([128, 4, 128]),
                       ALU.mult, r=[Ssb, DECt], w=[tl])
                    tt('dve', A(sbu), A(sbl), A(tl)[:, 0:512], ALU.add, r=[sbl, tl], w=[sbu])
                else:
                    sbu = sbl
                if STOP <= 4.102: break
                qd = galloc(2)
                for d_ in range(2):
                    for c in range(4):
                        tt('dve' if d_ == 0 else 'pool', A(qd[d_])[:, c * 128:(c + 1) * 128], A(rqT[c])[:, tsl], A(QFt)[:, d_, c, :],
                           ALU.mult, r=[rqT[c], QFt], w=[qd[d_]])
                if STOP <= 4.103: break
                kdf = galloc(1)[0]
                tt('pool', v8(A(kdf)), v8(A(rkM[t])), A(KFt)[:, 0, :].unsqueeze(2).to_broadcast([128, 8, 64]), ALU.mult,
                   r=[rkM[t], KFt], w=[kdf])
                if STOP <= 4.104: break
                WT = galloc(2)
                for c in range(4):
                    cp('act', A(RKL)[0:64, c * 128:(c + 1) * 128], A(rkT[c])[0:64, tsl], r=[rkT[c]], w=[RKL])
                    cp('act', A(RKH)[64:128, c * 128:(c + 1) * 128], A(rkT[c])[64:128, tsl], r=[rkT[c]], w=[RKH])
                for hp in range(2):
                    psa = PSR.get()
                    for hh_ in range(4):
                        h = hp * 4 + hh_; c = h // 2
                        RKm = RKL if h % 2 == 0 else RKH
                        mm(A(psa)[:, hh_ * 128:(hh_ + 1) * 128], A(RKm)[:, c * 128:(c + 1) * 128], A(rqT[c])[:, tsl], True, True,
                           r=[RKm, rqT[c]], w=[psa])
                    tt('dve', A(WT[hp]), A(psa), A(DTt)[:, hp * 4:(hp + 1) * 4, :].rearrange("p h n -> p (h n)"), ALU.mult,
                       r=[psa, DTt], w=[WT[hp]])
                if STOP <= 4.11: break
                po = PSL[t % 4]
                for h in range(8):
                    c = h // 2; off = (h % 2) * 64
                    osl = slice(h * 64, (h + 1) * 64)
                    mm(A(po)[:, osl], A(WT[h // 4])[:, (h % 4) * 128:(h % 4 + 1) * 128], A(rvM[t])[:, osl], True, False,
                       r=[WT[h // 4], rvM[t]], w=[po])
                    mm(A(po)[:, osl], A(qd[0])[:, c * 128:(c + 1) * 128],
                       A(Sf16)[:, c * 128 + off:c * 128 + off + 64], False, False, r=[qd[0], Sf16], w=[po])
                    mm(A(po)[:, osl], A(qd[1])[:, c * 128:(c + 1) * 128],
                       A(sbu)[:, c * 128 + off:c * 128 + off + 64], False, True, r=[qd[1], sbu], w=[po])
                ob = FT.get(); sq = FT.get()
                cp('act', A(ob)[:, 0:512], A(po), r=[po], w=[ob])
                act(A(sq)[:, 0:512], A(po), AF.Square, r=[po], w=[sq])
                psS = PSR.get()
                for c in range(4):
                    cs_ = slice(c * 128, (c + 1) * 128)
                    mm(A(psS)[:, cs_], A(kdf)[:, cs_], A(rvM[t])[:, cs_], True, True, r=[kdf, rvM[t]], w=[psS])
                tt('dve', v4(A(Sf)), v4(A(Sf)), A(CDt)[:, 0, :].unsqueeze(2).to_broadcast([128, 4, 128]), ALU.mult, r=[Sf, CDt], w=[Sf])
                tt('dve', A(Sf), A(Sf), A(psS), ALU.add, r=[Sf, psS], w=[Sf])
                for c in range(t * cpc, (t + 1) * cpc): conv_pe(c)
                s1 = sf[:, 0:8]; s2 = sf[:, 8:16]; s3 = sf[:, 16:24]
                P.op('dve', lambda e, ob=ob, s1=s1: e.tensor_reduce(s1, v8(A(ob)[:, 0:512]), AX.X, ALU.add), r=[ob], w=[smallf])
                P.op('dve', lambda e, sq=sq, s2=s2: e.tensor_reduce(s2, v8(A(sq)[:, 0:512]), AX.X, ALU.add), r=[sq], w=[smallf])
                ts('dve', s1, s1, 1.0 / 64, None, ALU.mult, None, r=[smallf], w=[smallf])
                tt('dve', s3, s1, s1, ALU.mult, r=[smallf], w=[smallf])
                stt('dve', s2, s2, 1.0 / 64, s3, ALU.mult, ALU.subtract, r=[smallf], w=[smallf])
                rsq(s2, s2, 1.0, EPS, [smallf], [smallf])
                tt('dve', v8(A(ob)[:, 0:512]), v8(A(ob)[:, 0:512]), s1.unsqueeze(2).to_broadcast([128, 8, 64]), ALU.subtract,
                   r=[ob, smallf], w=[ob])
                tt('dve', v8(A(ob)[:, 0:512]), v8(A(ob)[:, 0:512]), s2.unsqueeze(2).to_broadcast([128, 8, 64]), ALU.mult,
                   r=[ob, smallf], w=[ob])
                yb = galloc(1)[0]
                tt('dve', A(yb), A(ob)[:, 0:512], A(rgM[t]), ALU.mult, r=[ob, rgM[t]], w=[yb])
                pst_ = PSR.get()
                for c in range(4):
                    mm(A(pst_)[:, c * 128:(c + 1) * 128], A(yb)[:, c * 128:(c + 1) * 128], A(ident), True, True, r=[yb, ident], w=[pst_])
                for c in range(4):
                    act(A(retT[c])[:, tsl], A(pst_)[:, c * 128:(c + 1) * 128], AF.Copy, r=[pst_, pc], w=[retT[c]],
                        scale=pca[:, 368 + c:369 + c])
                gfree_(qd + [kdf, yb] + WT + ([sbu] if lat else []))
            if l == 0 and j == 0:
                for c in range(4): dump16(8 + c, retT[c])
                for c in range(4): dump16(28 + c, rqT[c])
                for c in range(4): dump16(32 + c, rkT[c])
                for c in range(4): dump16(36 + c, rvM[c])
            gfree_(rqT + rkT + rkM + rvM + rgM)

            if STOP <= 4.2: break
            aT = galloc(4)
            ps1 = PSR.get(); ps2 = PSR.get()
            for c in range(4):
                mm(A(ps1)[:, :N], A(onesf), A(ACC[c])[:, :N], c == 0, c == 3, r=[onesf, ACC[c]], w=[ps1])
            for c in range(4):
                sq = FT.get()
                act(A(sq)[:, :N], A(ACC[c])[:, :N], AF.Square, r=[ACC[c]], w=[sq])
                mm(A(ps2)[:, :N], A(onesf), A(sq)[:, :N], c == 0, c == 3, r=[onesf, sq], w=[ps2])
            mt = D1; vt_ = D2
            ts('dve', A(mt)[:, :N], A(ps1)[:, :N], 1.0 / 512, None, ALU.mult, None, r=[ps1], w=[mt])
            tt('dve', A(vt_)[:, :N], A(mt)[:, :N], A(mt)[:, :N], ALU.mult, r=[mt], w=[vt_])
            stt('dve', A(vt_)[:, :N], A(ps2)[:, :N], 1.0 / 512, A(vt_)[:, :N], ALU.mult, ALU.subtract, r=[ps2, vt_], w=[vt_])
            rsq(A(vt_)[:, :N], A(vt_)[:, :N], 1.0, EPS, [vt_], [vt_])
            for c in range(4):
                acc = ACC[c]
                tt('dve', A(acc)[:, :N], A(acc)[:, :N], A(mt)[:, :N], ALU.subtract, r=[acc, mt], w=[acc])
                tt('pool', A(acc)[:, :N], A(acc)[:, :N], A(vt_)[:, :N], ALU.mult, r=[acc, vt_], w=[acc])
                ts('dve', A(acc)[:, :N], A(acc)[:, :N], pca[:, 246 + c:247 + c], pca[:, 250 + c:251 + c], ALU.mult, ALU.add,
                   r=[acc, pc], w=[acc])
                act(A(aT[c])[:, :N], A(acc)[:, :N], AF.Silu, r=[acc], w=[aT[c]])

            if STOP <= 4.3: break
            if False:
                wq = Wrot.get()
                wqv = A(wq)[:, 0:4096].rearrange("p (k n) -> p k n", k=8)
                for c in range(4):
                    for hf in range(2):
                        h_ = hf * 4 + c
                        dma('pool', wqv[:, :, c * 128 + hf * 64:c * 128 + hf * 64 + 64],
                            w_in[0][:, 1024 + h_ * 64:1024 + (h_ + 1) * 64].rearrange("(k p) n -> p k n", p=128), w=[wq])
            else:
                wq, wqv = wload(WIN[:, 1024:1536], 8, 512, tw('in', l))
            dma('sp', A(D1)[:, :N], rope_in[:, 0, j, 0:N], w=[D1])
            dma('sp', A(D2)[:, :N], rope_in[:, 1, j, 0:N], w=[D2])
            qT = galloc(4)
            for c in range(4):
                ps = PSR.get()
                for k in range(8):
                    mm(A(ps)[:, :N], wqv[:, k, c * 128:(c + 1) * 128], A(hT[k])[:, :N], k == 0, k == 7, r=[wq, hT[k]], w=[ps])
                normrope(ps, bcol(8 + c), pca[:, 254:255], 1.0, N, A(qT[c])[:, :N], [qT[c]])
            if l == 0 and j == 0:
                for c in range(4): dump16(12 + c, aT[c])
                for c in range(4): dump16(24 + c, qT[c])
            attT = galloc(4)
            NS_ = 16 // GT
            groups = [('c', 0)] + ([(q, s_) for q in range(4) for s_ in range(NS_)] if lat else [])
            for half in range(2):
                hs = slice(half * 64, half * 64 + 64)
                os_ = slice((1 - half) * 64, (1 - half) * 64 + 64)
                VGr = VG0 if half == 0 else VG1
                voff = 0 if half == 0 else 64
                first = True
                pend = []

                def flush_pv(keep):
                    while len(pend) > keep:
                        (po_, vg_, t_, pt_, st_, sp_) = pend.pop(0)
                        mm(A(po_)[:, :N], A(vg_)[:, t_, :], A(pt_)[:, :N], st_, sp_, r=[vg_, pt_], w=[po_])
                for gi, (gq, gs) in enumerate(groups):
                    ktg = (KTG0 if half == 0 else KTG1).get(); vg = VGr.get()
                    if gq == 'c':
                        ntile = 2
                        dma('sp', A(ktg)[hs, 0:256], cxkv[hs, 0:256], r=[T_cx], w=[ktg])
                        dma('sp', A(vg)[:, 0:2, voff:voff + 64],
                            cxkv[:, 256:512].rearrange("p (t e) -> p t e", t=2)[:, :, half * 64:half * 64 + 64],
                            r=[T_cx], w=[vg])
                    else:
                        ntile = GT
                        GW = GT * 128
                        dma('sp', A(ktg)[hs, :], ex1_out[par][gq * 128 + half * 64:gq * 128 + half * 64 + 64, gs * GW:(gs + 1) * GW],
                            r=[T_ex1o[par]], w=[ktg])
                        dma('sp', A(vg)[:, :, voff:voff + 64],
                            ex1_out[par][gq * 128:(gq + 1) * 128, 2048 + gs * GW:2048 + (gs + 1) * GW]
                            .rearrange("p (t e) -> p t e", t=GT)[:, :, half * 64:half * 64 + 64],
                            r=[T_ex1o[par]], w=[vg])
                    last_g = gi == len(groups) - 1
                    for c in range(4):
                        po = PSL[c]
                        for t in range(ntile):
                            ps = PSR.get()
                            mm(A(ps)[:, :N], A(ktg)[:, t * 128:(t + 1) * 128], A(qT[c])[:, :N], True, True, r=[ktg, qT[c]], w=[ps])
                            pt = SQ.get()
                            act(A(pt)[:, :N], A(ps)[:, :N], AF.Exp, r=[ps], w=[pt])
                            pend.append((po, vg, t, pt, first and t == 0, last_g and t == ntile - 1))
                            flush_pv(2)
                    first = False
                flush_pv(0)
                for c in range(4):
                    po = PSL[c]
                    rc = FT.get()
                    mset('dve', A(rc)[hs, :N], 0.0, w=[rc])
                    P.op('act', lambda e, rc=rc, po=po, os_=os_, N=N: e.activation(A(rc)[os_, :N], A(po)[os_, :N], AF.Ln), r=[po], w=[rc])
                    P.op('act', lambda e, rc=rc, os_=os_, N=N: e.activation(A(rc)[os_, :N], A(rc)[os_, :N], AF.Exp, scale=-1.0), r=[rc], w=[rc])
                    ob = FT.get()
                    cp('act', A(ob)[hs, :N], A(po)[hs, :N], r=[po], w=[ob])
                    ps = PSR.get()
                    mm(A(ps)[:, :N], SWAP, A(rc)[:, :N], True, True, r=[cst, rc], w=[ps])
                    tt('dve', A(attT[c])[hs, :N], A(ob)[hs, :N], A(ps)[hs, :N], ALU.mult, r=[ob, ps], w=[attT[c]])
            gfree_(qT)

            if STOP <= 4.4: break
            zT = galloc(8)
            for gsec, (wsrc, wnm, br) in enumerate(((WPA, 'pa', aT), (WPB, 'pb', attT), (WPC, 'pc', retT))):
                for og in range(2):
                    if False:
                        wp_ = Wrot.get()
                        wpv = A(wp_)[:, 0:4096].rearrange("p (k n) -> p k n", k=4)
                        for c in range(4):
                            for hf in range(2):
                                h_ = hf * 4 + c
                                dma('pool', wpv[hf * 64:(hf + 1) * 64, c, :], w_pb[0][h_ * 64:(h_ + 1) * 64, :], w=[wp_])
                    else:
                        wp_, wpv = wload(wsrc, 4, 1024, tw(wnm, l))
                    c0 = 3840 + gsec * 1024 + og * 512
                    wg, wgv = wload(WIN[:, c0:c0 + 512], 8, 512, tw('in', l))
                    for oo in range(4):
                        o = og * 4 + oo
                        psg = PSR.get(); psp = PSR.get()
                        for k in range(8):
                            mm(A(psg)[:, :N], wgv[:, k, oo * 128:(oo + 1) * 128], A(hT[k])[:, :N], k == 0, k == 7, r=[wg, hT[k]], w=[psg])
                        for c in range(4):
                            mm(A(psp)[:, :N], wpv[:, c, o * 128:(o + 1) * 128], A(br[c])[:, :N], c == 0, c == 3, r=[wp_, br[c]], w=[psp])
                        sg = FT.get()
                        act(A(sg)[:, :N], A(psg)[:, :N], AF.Sigmoid, r=[psg, pc], w=[sg], bias=bcol(30 + gsec * 8 + o))
                        if gsec == 0:
                            tt('dve', A(zT[o])[:, :N], A(psp)[:, :N], A(sg)[:, :N], ALU.mult, r=[psp, sg], w=[zT[o]])
                        else:
                            tt('dve', A(sg)[:, :N], A(psp)[:, :N], A(sg)[:, :N], ALU.mult, r=[psp, sg], w=[sg])
                            tt('pool', A(zT[o])[:, :N], A(zT[o])[:, :N], A(sg)[:, :N], ALU.add, r=[zT[o], sg], w=[zT[o]])
            if l == 0 and j == 0:
                for c in range(4): dump16(16 + c, attT[c])
                for o in range(4): dump16(20 + o, zT[o])
            gfree_(aT + attT + retT + hT)
            for og in range(2):
                wo, wov = wload(WOUT[:, og * 512:(og + 1) * 512], 8, 512, tw('out', l))
                for oo in range(4):
                    o = og * 4 + oo
                    ps = PSR.get()
                    for k in range(8):
                        mm(A(ps)[:, :N], wov[:, k, oo * 128:(oo + 1) * 128], A(zT[k])[:, :N], k == 0, k == 7, r=[wo, zT[k]], w=[ps])
                    stt('dve', xa[:, o, mc:mc + N], A(ps)[:, :N], mo[:, 16 + o, tcol:tcol + 1], xa[:, o, mc:mc + N], ALU.mult, ALU.add,
                        r=[ps, modT] + xk(o), w=[xTk[o]])
            gfree_(zT)
            if l == 0 and j == 0:
                for o in range(8): dump32(o, xa[:, o, mc:mc + N], xk(o))
                dump32(8, mo.rearrange("p j t -> p (j t)"), [modT], 96)

        if STOP <= 5: break
        exch_finish(1 - par)
        PSRh[0] = PSR8
        for b in BLK:
            j = b['j']; ws = b['ws']; N = b['N']
            for k in range(8):
                cp('pool', xhk(k)[:, 2 * j:2 * j + 1], xa[:, k, ws + 14:ws + 15], r=xk(k) + [xTh], w=[xht(k)])
                cp('pool', xhk(k)[:, 2 * j + 1:2 * j + 2], xa[:, k, ws + 15 + N:ws + 16 + N], r=xk(k) + [xTh], w=[xht(k)])
        hhs = galloc(3)
        make_h([xhk(k)[:, 0:8] for k in range(8)], [[xht(k)] for k in range(8)], 8, G2(0), SH2(0), [(hhk(k)[:, 0:8], [hht(k)]) for k in range(8)])
        make_h([xhk(k)[:, 8:10] for k in range(8)], [[xht(k)] for k in range(8)], 2, G2(1), SH2(1), [(hhk(k)[:, 8:10], [hht(k)]) for k in range(8)])
        for pcs in range(11):
            wt, wv = wload(WUP[:, pcs * 256:(pcs + 1) * 256], 8, 256, tw('up', l))
            for ff in range(2):
                f = pcs * 2 + ff
                ps = PSR.get()
                for k in range(8):
                    mm(A(ps)[:, 0:10], wv[:, k, ff * 128:(ff + 1) * 128], hhk(k)[:, 0:10], k == 0, k == 7, r=[wt, hht(k)], w=[ps])
                cp('act', A(gth)[:, f, :], A(ps)[:, 0:10], r=[ps], w=[gth])
        gfree_(hhs)
        ga = A(gth)
        ts('dve', ga[:, :, 0:1], ga[:, :, 0:1], rk_[:, 8:9], None, ALU.mult, None, r=[gth, rkt], w=[gth])
        ts('dve', ga[:, :, 7:8], ga[:, :, 7:8], rk_[:, 9:10], None, ALU.mult, None, r=[gth, rkt], w=[gth])
        mset('dve', ga[:, :, 8:10], 0.0, w=[gth])
        for b in BLK:
            j = b['j']; N = b['N']; mc = b['mc']; lat = b['lat']; tcol = 0 if lat else 1
            if l == depth - 1 and not lat: continue
            if l + 1 < depth: CV.step(n_cv)
            hT = galloc(8)
            make_h([xa[:, k, mc:mc + N] for k in range(8)], [xk(k) for k in range(8)], N, G2(tcol), SH2(tcol),
                   [(A(hT[k])[:, :N], [hT[k]]) for k in range(8)])
            actT = galloc(22)
            for pcs in range(11):
                wt = Wrot.get()
                wv = A(wt)[:, 0:4096].rearrange("p (k n) -> p k n", k=8)
                dma(WENG, wv[:, :, 0:256], WUP[:, pcs * 256:(pcs + 1) * 256].rearrange("(k p) n -> p k n", p=128), r=[tw('up', l)], w=[wt])
                dma(WENG, wv[:, :, 256:512], WUP[:, 2816 + pcs * 256:2816 + (pcs + 1) * 256].rearrange("(k p) n -> p k n", p=128),
                    r=[tw('up', l)], w=[wt])
                for ff in range(2):
                    f = pcs * 2 + ff
                    psg = PSR.get(); psv = PSR.get()
                    for k in range(8):
                        mm(A(psg)[:, :N], wv[:, k, ff * 128:(ff + 1) * 128], A(hT[k])[:, :N], k == 0, k == 7, r=[wt, hT[k]], w=[psg])
                    for k in range(8):
                        mm(A(psv)[:, :N], wv[:, k, 256 + ff * 128:256 + (ff + 1) * 128], A(hT[k])[:, :N], k == 0, k == 7, r=[wt, hT[k]], w=[psv])
                    gt = FT.get()
                    cp('act', A(gt)[:, 1:1 + N], A(psg)[:, :N], r=[psg], w=[gt])
                    cp('act', A(gt)[:, 0:1], ga[:, f, 2 * j:2 * j + 1], r=[gth], w=[gt])
                    cp('act', A(gt)[:, 1 + N:2 + N], ga[:, f, 2 * j + 1:2 * j + 2], r=[gth], w=[gt])
                    acc = FT.get()
                    wf = lambda kk: pca[:, 256 + f * 3 + kk:257 + f * 3 + kk]
                    ts('dve', A(acc)[:, :N], A(gt)[:, 0:N], wf(0), pca[:, 322 + f:323 + f], ALU.mult, ALU.add, r=[gt, pc], w=[acc])
                    stt('dve', A(acc)[:, :N], A(gt)[:, 1:1 + N], wf(1), A(acc)[:, :N], ALU.mult, ALU.add, r=[gt, pc, acc], w=[acc])
                    stt('pool', A(acc)[:, :N], A(gt)[:, 2:2 + N], wf(2), A(acc)[:, :N], ALU.mult, ALU.add, r=[gt, pc, acc], w=[acc])
                    act(A(acc)[:, :N], A(acc)[:, :N], AF.Silu, r=[acc], w=[acc])
                    tt('dve', A(actT[f])[:, :N], A(acc)[:, :N], A(psv)[:, :N], ALU.mult, r=[acc, psv], w=[actT[f]])
            for o in range(8):
                wt = Wrot.get()
                wv = A(wt)[:, 0:22 * 128].rearrange("p (k n) -> p k n", k=22)
                dma(WENG, wv, WDN[:, o * 128:(o + 1) * 128].rearrange("(k p) n -> p k n", p=128), r=[tw('dn', l)], w=[wt])
                ps = PSR.get()
                for f in range(22):
                    mm(A(ps)[:, :N], wv[:, f, :], A(actT[f])[:, :N], f == 0, f == 21, r=[wt, actT[f]], w=[ps])
                stt('dve', xa[:, o, mc:mc + N], A(ps)[:, :N], mo[:, 40 + o, tcol:tcol + 1], xa[:, o, mc:mc + N], ALU.mult, ALU.add,
                    r=[ps, modT] + xk(o), w=[xTk[o]])
            gfree_(hT + actT)
            if j == 1 and l + 1 < depth: emit_adaln(l + 1)
        if l + 1 < depth: CV.finish()
        PSRh[0] = PSR4

    fa = A(fng)
    for j in range(4):
        mc = 15 + 512 * j
        ps = PSR.get()
        for k in range(8):
            sq = SQ.get()
            act(A(sq), xa[:, k, mc:mc + 512], AF.Square, r=xk(k), w=[sq])
            mm(A(ps), A(onesb), A(sq), k == 0, k == 7, r=[onesb, sq], w=[ps])
        R = D0
        rsq(A(R)[:, 0:512], A(ps), 1.0 / D, EPS, [ps], [R])
        for k in range(8):
            t = FT.get()
            stt('dve', A(t)[:, 0:512], xa[:, k, mc:mc + 512], fa[:, k:k + 1], A(R)[:, 0:512], ALU.mult, ALU.mult, r=xk(k) + [R, fng], w=[t])
            dma('sp', out_d[:, k, j * 512:(j + 1) * 512], A(t)[:, 0:512], r=[t], w=[])

    P.emit(nc, es)
    es.close()
    return nc


def _fm(v, nch):
    return np.ascontiguousarray(np.asarray(v, np.float32).reshape(nch, 128).T)


def _consts():
    c = np.zeros((128, NCST), np.float32)
    j = np.arange(128, dtype=np.float32)[:, None]; i = np.arange(128, dtype=np.float32)[None, :]
    c[:, 0:128] = np.maximum(i - j, 0); c[:, 128:256] = np.maximum(j - i, 0); c[:, 256:384] = np.eye(128)
    c[:, 384:512] = i + 1; c[:, 512:640] = 128 - i
    c[:, 640] = 127 - j[:, 0]; c[:, 641] = j[:, 0]
    c[:, 642:658] = 128.0 * (15 - np.arange(16))[None, :]
    rt = np.zeros((128, 128), np.float32)
    for g in range(4):
        for t in range(16):
            a = g * 32 + t
            rt[a + 16, a] = -1.0
            rt[a, a + 16] = 1.0
    c[:, 658:786] = rt
    sw = np.zeros((128, 128), np.float32)
    for k in range(128): sw[k, (k + 64) % 128] = 1.0
    c[:, 786:914] = sw
    return c


def _rope(start):
    tab = np.zeros((128, 2, 5, 512), np.float32)
    tab[:, 0, 4, :] = 1.0
    p = np.arange(128); d = p % 64; f = (d % 16).astype(np.float32)
    inv = (np.float32(10000.0) ** (-f / np.float32(16.0))).astype(np.float32)
    for j in range(4):
        t = start + j * 512 + np.arange(512)
        row = (t // 64).astype(np.float32); col = (t % 64).astype(np.float32)
        pos = np.where((d < 32)[:, None], row[None, :], col[None, :]).astype(np.float32)
        ang = (pos * inv[:, None]).astype(np.float32)
        tab[:, 0, j, :] = np.cos(ang); tab[:, 1, j, :] = np.sin(ang)
    return tab


def _prep(inputs, depth):
    f = lambda k: np.asarray(inputs[k], np.float32)
    x = f("x"); c = f("c"); ctx = f("ctx"); c_ctx = f("c_ctx")
    pcol = np.zeros((depth, 128, NPC), np.float32)
    brow = np.zeros((depth, 1, NBROW), np.float32)
    p = np.arange(128)
    for l in range(depth):
        pc = pcol[l]
        pc[:, 0:48] = _fm(f("b_ada")[l], 48)
        pc[:, 48:56] = _fm(f("norm1_g")[l], 8); pc[:, 56:64] = _fm(f("norm2_g")[l], 8)
        b = f("b_in")[l].copy()
        b[1024:1536] = b[1024:1536].reshape(2, 4, 64).transpose(1, 0, 2).reshape(512)
        pc[:, 64:118] = _fm(b, 54)
        pc[:, 118:242] = f("conv_dw_w")[l].T.reshape(4, 128, 31).transpose(1, 0, 2).reshape(128, 124)
        pc[:, 242:246] = _fm(f("conv_dw_b")[l], 4); pc[:, 246:250] = _fm(f("conv_ln_g")[l], 4)
        pc[:, 250:254] = _fm(f("conv_ln_b")[l], 4)
        pc[:, 254] = f("q_norm_g")[l][p % 64]; pc[:, 255] = f("k_norm_g")[l][p % 64]
        pc[:, 256:322] = f("ffn_dw_w")[l].T.reshape(22, 128, 3).transpose(1, 0, 2).reshape(128, 66)
        pc[:, 322:344] = _fm(f("ffn_dw_b")[l], 22)
        lg = f("ret_decay_logit")[l]
        for d_ in range(2):
            for cc in range(4):
                pc[:, 344 + d_ * 4 + cc] = lg[d_, 2 * cc + p // 64]
            pc[:, 352 + d_ * 8:360 + d_ * 8] = lg[d_][None, :]
        pc[:, 368:372] = _fm(f("ret_gn_g")[l], 4)
        bi = f("b_in")[l]
        brow[l, 0] = np.concatenate([bi[1664:1792], bi[2304:2816], bi[2816:3328], bi[3328:3840]])
    cst = _consts()
    fng = _fm(f("final_norm_g"), 8)
    shared = dict(cst=cst, pcol=pcol, brow=brow, fng=fng)
    for k in ("w_ada", "w_in", "w_pa", "w_pb", "w_pc", "w_out", "w_up", "w_down"):
        shared[k] = np.ascontiguousarray(f(k)[:depth])
    maps = []
    for r in range(8):
        b_ = r // 4; q = r % 4; start = q * NLAT
        xe = np.zeros((XW, D), np.float32)
        xe[15:15 + NLAT] = x[b_, start:start + NLAT]
        xe[CTX0:CTX0 + CTX] = ctx[b_]
        xT = np.ascontiguousarray(xe.T.reshape(8, 128, XW).transpose(1, 0, 2))
        cv = np.stack([c[b_], c_ctx], 0)
        cT = np.ascontiguousarray(cv.T.reshape(8, 128, 2).transpose(1, 0, 2))
        rk = np.zeros((128, 32), np.float32)
        for q2 in range(4):
            rk[:, q2] = 1.0 if q2 == q - 1 else 0.0
            rk[:, 4 + q2] = 1.0 if q2 == q + 1 else 0.0
            rk[:, 10 + q2] = 2048.0 * max(q - 1 - q2, 0); rk[:, 14 + q2] = 1.0 if q2 < q else 0.0
            rk[:, 19 + q2] = 2048.0 * max(q2 - q - 1, 0); rk[:, 23 + q2] = 1.0 if q2 > q else 0.0
        rk[:, 8] = 1.0 if q > 0 else 0.0; rk[:, 9] = 1.0 if q < 3 else 0.0
        rk[:, 18] = 2048.0 * q; rk[:, 27] = 2048.0 * (3 - q)
        m = dict(shared); m.update(xT=xT, cT=cT, rope=_rope(start), rkt=rk)
        maps.append(m)
    return maps


_NC = {}


def kernel(**inputs):
    depth = int(inputs.pop("_depth", DEPTH))
    if depth not in _NC:
        _NC[depth] = build(depth)
    maps = _prep(inputs, depth)
    res = run_bass_kernel_spmd(_NC[depth], maps, core_ids=list(range(8)))
    out = np.zeros((2, SEQ, D), np.float32)
    for r in range(8):
        o = np.asarray(res.results[r]["out"], np.float32)
        out[r // 4, (r % 4) * NLAT:(r % 4 + 1) * NLAT] = o.transpose(2, 1, 0).reshape(NLAT, D)
    return out
```

```python
import numpy as np
STOP = 9.0
from contextlib import ExitStack
import concourse.bass as bass
import concourse.mybir as mybir
from concourse.bass_utils import run_bass_kernel_spmd

F32 = mybir.dt.float32
BF16 = mybir.dt.bfloat16
ALU = mybir.AluOpType
AF = mybir.ActivationFunctionType
AX = mybir.AxisListType

D = 1024; DEPTH = 4; SEQ = 8192; CTX = 256; NLAT = 2048
XW = 2364; CTX0 = 2093
EPS = 1e-6
NPC = 372; NBROW = 1664; NCST = 914


class T:
    def __init__(self, ap):
        self.ap = ap; self.lw = {}; self.rd = {}


class Prog:
    NDS = 10

    def __init__(self):
        self.ops = []
        self.dma_rr = {}

    def op(self, eng, fn, r=(), w=(), dma=False, cc=False):
        idx = len(self.ops)
        if cc:
            cls = ('cc',)
        elif dma:
            k = self.dma_rr.get(eng, 0); self.dma_rr[eng] = k + 1
            cls = ('dma', eng, k % self.NDS)
        else:
            cls = (eng,)
        deps = {}

        def add(c, i):
            if i is not None and deps.get(c, -1) < i:
                deps[c] = i
        for b in r:
            for c, i in b.lw.items(): add(c, i)
        for b in w:
            for c, i in b.lw.items(): add(c, i)
            for c, i in b.rd.items(): add(c, i)
        for b in r: b.rd[cls] = idx
        for b in w: b.lw[cls] = idx; b.rd = {}
        self.ops.append(dict(eng=eng, fn=fn, cls=cls, deps=deps, dma=dma, cc=cc))
        return idx

    def emit(self, nc, es):
        ops = self.ops
        last_in_cls = {}
        for i, o in enumerate(ops):
            if o['dma'] or o['cc']:
                p = last_in_cls.get(o['cls'])
                if p is not None: o['deps'][o['cls']] = max(o['deps'].get(o['cls'], -1), p)
                last_in_cls[o['cls']] = i
        needed = set()
        for o in ops:
            for c, i in o['deps'].items():
                if c == ('pe',) and o['eng'] == 'pe' and not o['dma']:
                    continue
                needed.add(i)
        sems = {}
        classes = sorted({o['cls'] for o in ops}, key=str)
        for c in classes:
            sems[c] = es.enter_context(nc.semaphore("s_" + "_".join(str(x) for x in c)))
        cnt = {c: 0 for c in classes}
        tok = {}
        for i, o in enumerate(ops):
            if i in needed or o['dma'] or o['cc']:
                inc = 16 if o['dma'] else 1
                cnt[o['cls']] += inc
                tok[i] = cnt[o['cls']]
                o['inc'] = inc
            else:
                o['inc'] = 0
        engs = ['pe', 'act', 'dve', 'pool', 'sp']
        streams = {e: [o for o in ops if o['eng'] == e] for e in engs}
        idx_of = {id(o): i for i, o in enumerate(ops)}
        block = es.enter_context(nc.Block())

        def run(ename, e):
            waited = {}
            for o in streams[ename]:
                for c, i in sorted(o['deps'].items(), key=lambda kv: str(kv[0])):
                    if c == ('pe',) and ename == 'pe' and not o['dma']:
                        continue
                    v = tok[i]
                    if waited.get(c, 0) < v:
                        e.wait_ge(sems[c], v); waited[c] = v
                ins = o['fn'](e)
                if o['inc']:
                    ins.then_inc(sems[o['cls']], o['inc'])
            if ename == 'sp':
                for c in classes:
                    if cnt[c] > 0 and (c[0] == 'dma'):
                        e.wait_ge(sems[c], cnt[c])

        @block.tensor
        def _(e): run('pe', e)

        @block.scalar
        def _(e): run('act', e)

        @block.vector
        def _(e): run('dve', e)

        @block.gpsimd
        def _(e): run('pool', e)

        @block.sync
        def _(e): run('sp', e)


class Rot:
    def __init__(self, items): self.items = items; self.i = 0

    def get(self):
        b = self.items[self.i % len(self.items)]; self.i += 1
        return b


def build(depth=DEPTH):
    nc = bass.Bass("TRN2", target_bir_lowering=False)
    P = Prog()
    es = ExitStack()

    def din(name, shape, dt=F32):
        return nc.dram_tensor(name, shape, dt, kind="ExternalInput").ap()

    def dint(name, shape, dt):
        return nc.dram_tensor(name, shape, dt, kind="Internal").ap()

    xT_in = din("xT", [128, 8, XW])
    cT_in = din("cT", [128, 8, 2])
    rope_in = din("rope", [128, 2, 5, 512])
    rkt_in = din("rkt", [128, 32])
    cst_in = din("cst", [128, NCST])
    pcol_in = din("pcol", [depth, 128, NPC])
    brow_in = din("brow", [depth, 1, NBROW])
    fng_in = din("fng", [128, 8])
    w_ada = din("w_ada", [depth, D, 6144]); w_in = din("w_in", [depth, D, 6912])
    w_pa = din("w_pa", [depth, 512, D]); w_pb = din("w_pb", [depth, 512, D]); w_pc = din("w_pc", [depth, 512, D])
    w_out = din("w_out", [depth, D, D]); w_up = din("w_up", [depth, D, 5632]); w_down = din("w_down", [depth, 2816, D])
    out_d = nc.dram_tensor("out", [128, 8, NLAT], F32, kind="ExternalOutput").ap()
    DBG = False
    if DBG:
        dbg16 = nc.dram_tensor("dbg16", [40, 128, 512], BF16, kind="ExternalOutput").ap()
        dbg32 = nc.dram_tensor("dbg32", [12, 128, 512], F32, kind="ExternalOutput").ap()

    def dump16(i, t):
        if DBG: P.op('sp', lambda e: e.dma_start(out=dbg16[i], in_=t.ap), r=[t], dma=True)

    def dump32(i, ap, trs, n=512):
        if DBG: P.op('sp', lambda e: e.dma_start(out=dbg32[i][:, 0:n], in_=ap), r=trs, dma=True)

    wada16 = dint("wada16", [depth, D, 6144], BF16); win16 = dint("win16", [depth, D, 6912], BF16)
    wpa16 = dint("wpa16", [depth, 512, D], BF16); wpb16 = dint("wpb16", [depth, 512, D], BF16)
    wpc16 = dint("wpc16", [depth, 512, D], BF16); wout16 = dint("wout16", [depth, D, D], BF16)
    wup16 = dint("wup16", [depth, D, 5632], BF16); wdn16 = dint("wdn16", [depth, 2816, D], BF16)
    EXC = 2048 + 16 * 128
    ex1_in = [dint(f"ex1i{i}", [128, EXC], BF16) for i in range(2)]
    ex1_out = [dint(f"ex1o{i}", [512, EXC], BF16) for i in range(2)]
    cxkv = dint("cxkv", [128, 256 + 2 * 128], BF16)
    ex2_in = [dint(f"ex2i{i}", [128, 1024], F32) for i in range(2)]
    ex2_out = [dint(f"ex2o{i}", [512, 1024], F32) for i in range(2)]
    ex3_in = [dint(f"ex3i{i}", [128, 240], F32) for i in range(2)]
    ex3_out = [dint(f"ex3o{i}", [512, 240], F32) for i in range(2)]
    T_ex1i = [T(a) for a in ex1_in]; T_ex1o = [T(a) for a in ex1_out]; T_cx = T(cxkv)
    T_ex2i = [T(a) for a in ex2_in]; T_ex2o = [T(a) for a in ex2_out]
    T_ex3i = [T(a) for a in ex3_in]; T_ex3o = [T(a) for a in ex3_out]
    Tw = {}

    def tw(name, l):
        if (name, l) not in Tw: Tw[(name, l)] = T(None)
        return Tw[(name, l)]
    RG = [[0, 1, 2, 3], [4, 5, 6, 7]]

    def sbt(name, shape, dt):
        t = es.enter_context(nc.sbuf_tensor("sb_" + name, shape, dt))
        return T(t[:])

    xT = sbt("xT", [128, 8, XW], F32)
    xTk = [T(None) for _ in range(8)]
    xTh = T(None)
    cst = sbt("cst", [128, NCST], F32)
    rkt = sbt("rkt", [128, 32], F32)
    pc = sbt("pc", [128, NPC], F32)
    brow = sbt("brow", [1, NBROW], BF16)
    fng = sbt("fng", [128, 8], F32)
    modTs = [sbt(f"modT{i}", [128, 48, 2], F32) for i in range(2)]
    badas = [sbt(f"bada{i}", [128, 48], F32) for i in range(2)]
    modT = modTs[0]
    mder = sbt("mder", [128, 4, 8, 2], F32)
    scT = sbt("scT", [128, 8, 2], BF16)
    cTs = sbt("cTs", [128, 8, 2], F32)
    onesb = sbt("onesb", [128, 128], BF16)
    blk64 = sbt("blk64", [128, 128], BF16)
    ident = sbt("ident", [128, 128], BF16)
    rtb = sbt("rtb", [128, 128], BF16)
    onesf = sbt("onesf", [128, 128], F32)
    lgt = sbt("lgt", [128, 24], F32)
    DTt = sbt("DTt", [128, 8, 128], BF16)
    RKL = sbt("RKL", [128, 512], BF16); RKH = sbt("RKH", [128, 512], BF16)
    BMASK = sbt("BMASK", [128, 512], BF16)
    QFt = sbt("QFt", [128, 2, 4, 128], F32)
    KFt = sbt("KFt", [128, 2, 8], F32)
    CDt = sbt("CDt", [128, 2, 4], F32)
    DECt = sbt("DECt", [128, 2, 16, 4], F32)
    coef = sbt("coef", [128, 10, 4], F32)
    smallf = sbt("smallf", [128, 64], F32)
    Wt = [sbt(f"W{i}", [128, 4096], BF16) for i in range(2)]
    Wrot = Rot(Wt)
    GT = 4
    KTG0 = Rot([sbt(f"KTG0{i}", [128, GT * 128], BF16) for i in range(2)])
    KTG1 = Rot([sbt(f"KTG1{i}", [128, GT * 128], BF16) for i in range(2)])
    VG0 = Rot([sbt(f"VG0{i}", [128, GT, 128], BF16) for i in range(2)])
    VG1 = Rot([sbt(f"VG1{i}", [128, GT, 128], BF16) for i in range(2)])
    SBL = Rot([sbt(f"SBL{i}", [128, 512], BF16) for i in range(2)])
    sbscr = dint("sbscr", [18, 128, 512], BF16)
    T_sbs = [T(None) for _ in range(18)]
    NG = 39
    Gall = [sbt(f"G{i}", [128, 512], BF16) for i in range(NG)]
    gfree = list(Gall)

    def galloc(n):
        r = gfree[:n]; del gfree[:n]
        assert len(r) == n, "G pool exhausted"
        return r

    def gfree_(lst): gfree.extend(lst)
    FT = Rot([sbt(f"FT{i}", [128, 514], F32) for i in range(4)])
    STG = [sbt(f"STG{i}", [128, 512], BF16) for i in range(3)]
    D0 = sbt("D0", [128, 512], F32); D1 = sbt("D1", [128, 512], F32); D2 = sbt("D2", [128, 512], F32)
    ACC = [sbt(f"ACC{i}", [128, 512], F32) for i in range(4)]
    SQ = Rot([sbt(f"SQ{i}", [128, 512], BF16) for i in range(3)])
    Sf = sbt("Sf", [128, 512], F32); Sf16 = sbt("Sf16", [128, 512], BF16)
    Sbk = sbt("Sbk", [128, 512], F32); Tfk = sbt("Tfk", [128, 512], F32)
    Ssf = sbt("Ssf", [128, 512], F32); Ssb = sbt("Ssb", [128, 512], F32)
    DG = Rot([sbt(f"DG{i}", [128, 128], BF16) for i in range(4)])
    uh = sbt("uh", [128, 4, 150], BF16)
    gth = sbt("gth", [128, 22, 10], F32)
    ubuf = [sbt(f"ubuf{i}", [128, 542], BF16) for i in range(4)]
    pst = [T(es.enter_context(nc.psum_tensor(f"ps{i}", [128, 512], F32))[:]) for i in range(8)]
    PSL = pst[0:4]
    PSR4 = Rot(pst[4:8]); PSR8 = Rot(pst[0:8])
    PSRh = [PSR4]

    class _PSR:
        def get(self): return PSRh[0].get()
    PSR = _PSR()

    def A(t): return t.ap

    def dma(eng, out_ap, in_ap, r=(), w=()):
        return P.op(eng, lambda e: e.dma_start(out=out_ap, in_=in_ap), r=r, w=w, dma=True)

    def mm(ps_ap, lhsT, rhs, start, stop, r, w):
        return P.op('pe', lambda e: e.matmul(ps_ap, lhsT, rhs, start=start, stop=stop), r=r, w=w)

    def act(out, in_, func, r, w, bias=None, scale=None, eng='act'):
        kw = {}
        if bias is not None: kw['bias'] = bias
        if scale is not None: kw['scale'] = scale
        return P.op('act', lambda e: e.activation(out, in_, func, **kw), r=r, w=w)

    def tt(eng, out, a, b, op, r, w):
        if eng == 'pool': eng = 'dve'
        return P.op(eng, lambda e: e.tensor_tensor(out, a, b, op), r=r, w=w)

    def rsq(out, in_, scale, eps, r, w):
        P.op('dve', lambda e: e.tensor_scalar(out, in_, scale, eps, ALU.mult, ALU.add), r=r, w=w)
        P.op('act', lambda e: e.activation(out, out, AF.Ln), r=w, w=w)
        P.op('act', lambda e: e.activation(out, out, AF.Exp, scale=-0.5), r=w, w=w)

    def ts(eng, out, a, s1, s2, op0, op1, r, w):
        eng = 'dve'
        if s2 is None:
            return P.op(eng, lambda e: e.tensor_scalar(out, a, s1, None, op0), r=r, w=w)
        return P.op(eng, lambda e: e.tensor_scalar(out, a, s1, s2, op0, op1), r=r, w=w)

    def stt(eng, out, a, s, b, op0, op1, r, w):
        eng = 'dve'
        return P.op(eng, lambda e: e.scalar_tensor_tensor(out, a, s, b, op0, op1), r=r, w=w)

    def cp(eng, out, in_, r, w):
        if eng == 'pool': eng = 'dve'
        if eng == 'act':
            return P.op('act', lambda e: e.activation(out, in_, AF.Copy), r=r, w=w)
        return P.op(eng, lambda e: e.tensor_copy(out, in_), r=r, w=w)

    def mset(eng, ap, val, w):
        return P.op(eng, lambda e: e.memset(ap, val), w=w)

    xa = A(xT)

    def xk(k): return [xT, xTk[k]]

    dma('sp', xa, xT_in, w=[xT] + xTk + [xTh])
    dma('sp', A(cst), cst_in, w=[cst])
    dma('sp', A(rkt), rkt_in, w=[rkt])
    dma('sp', A(fng), fng_in, w=[fng])
    dma('sp', A(cTs), cT_in, w=[cTs])
    for l in range(depth if (STOP > 0 and 0) else 0):
        def cast(nm, dst, src, rows, rstep):
            for r0 in range(0, rows, rstep):
                dma('pool', dst[r0:r0 + rstep, :], src[r0:r0 + rstep, :], w=[tw(nm, l)])
        cast('ada', wada16[l], w_ada[l], D, 256)
        for r0 in range(0, D, 256):
            rs = slice(r0, r0 + 256)
            dma('pool', win16[l][rs, 0:1024], w_in[l][rs, 0:1024], w=[tw('in', l)])
            for half in range(2):
                dma('pool', win16[l][rs, 1024:1536].rearrange("k (c h d) -> k h c d", c=4, h=2)[:, half],
                    w_in[l][rs, 1024:1536].rearrange("k (h c d) -> k h c d", h=2, c=4)[:, half], w=[tw('in', l)])
            dma('pool', win16[l][rs, 1536:6912], w_in[l][rs, 1536:6912], w=[tw('in', l)])
        cast('pa', wpa16[l], w_pa[l], 512, 512)
        for half in range(2):
            dma('pool', wpb16[l].rearrange("(c h d) n -> h c d n", c=4, h=2)[half],
                w_pb[l].rearrange("(h c d) n -> h c d n", h=2, c=4)[half], w=[tw('pb', l)])
        cast('pc', wpc16[l], w_pc[l], 512, 512)
        cast('out', wout16[l], w_out[l], D, 512)
        cast('up', wup16[l], w_up[l], D, 256)
        cast('dn', wdn16[l], w_down[l], 2816, 704)

    c_ = A(cst)
    RELA = c_[:, 0:128]; RELB = c_[:, 128:256]; EYE = c_[:, 256:384]
    IOTA1 = c_[:, 384:512]; IOTAC = c_[:, 512:640]; REV = c_[:, 640:641]; JCOL = c_[:, 641:642]
    EXPO = c_[:, 642:658]; RTF = c_[:, 658:786]; SWAP = c_[:, 786:914]
    rk_ = A(rkt)
    mset('dve', A(onesb), 1.0, w=[onesb]); mset('dve', A(onesf), 1.0, w=[onesf])
    mset('dve', A(blk64), 0.0, w=[blk64])
    mset('dve', A(blk64)[0:64, 0:64], 1.0, w=[blk64]); mset('dve', A(blk64)[64:128, 64:128], 1.0, w=[blk64])
    cp('dve', A(ident), EYE, r=[cst], w=[ident]); cp('dve', A(rtb), RTF, r=[cst], w=[rtb])
    for t in KTG0.items + KTG1.items + [RKL, RKH, BMASK]: mset('dve', A(t), 0.0, w=[t])
    for c in range(4):
        mset('dve', A(BMASK)[0:64, c * 128:c * 128 + 64], 1.0, w=[BMASK])
        mset('dve', A(BMASK)[64:128, c * 128 + 64:c * 128 + 128], 1.0, w=[BMASK])
    for t in VG0.items: mset('dve', A(t), 1.0, w=[t])
    for t in VG1.items: mset('dve', A(t), 1.0, w=[t])
    act(A(scT), A(cTs), AF.Silu, r=[cTs], w=[scT])

    BLK = [dict(ws=512 * j, mc=15 + 512 * j, N=512, lat=True, j=j) for j in range(4)]
    BLK.append(dict(ws=2078, mc=CTX0, N=256, lat=False, j=4))

    WE = ['pool']

    def wload(src_ap, kc, n, wtr, eng=None):
        eng = eng or WE[0]
        wt = Wrot.get()
        dst = A(wt)[:, 0:kc * n].rearrange("p (k n) -> p k n", k=kc)
        dma(eng, dst, src_ap.rearrange("(k p) n -> p k n", p=128), r=[wtr], w=[wt])
        return wt, dst

    def normrope(ps_in, bias_col, gcol, gscale, N, out_ap, out_tr):
        xb = FT.get()
        act(A(xb)[:, :N], A(ps_in)[:, :N], AF.Identity, r=[ps_in, pc], w=[xb], bias=bias_col)
        sq = SQ.get()
        act(A(sq)[:, :N], A(xb)[:, :N], AF.Square, r=[xb], w=[sq])
        ps2 = PSR.get()
        mm(A(ps2)[:, :N], A(blk64), A(sq)[:, :N], True, True, r=[blk64, sq], w=[ps2])
        Rr = FT.get()
        rsq(A(Rr)[:, :N], A(ps2)[:, :N], 1.0, 64 * EPS, [ps2], [Rr])
        ts('dve', A(xb)[:, :N], A(xb)[:, :N], gcol, gscale, ALU.mult, ALU.mult, r=[xb, pc], w=[xb])
        xg = SQ.get()
        cp('act', A(xg)[:, :N], A(xb)[:, :N], r=[xb], w=[xg])
        ps3 = PSR.get()
        mm(A(ps3)[:, :N], A(rtb), A(xg)[:, :N], True, True, r=[rtb, xg], w=[ps3])
        t1 = FT.get(); t2 = FT.get()
        tt('pool', A(t1)[:, :N], A(xb)[:, :N], A(D1)[:, :N], ALU.mult, r=[xb, D1], w=[t1])
        tt('dve', A(t2)[:, :N], A(ps3)[:, :N], A(D2)[:, :N], ALU.mult, r=[ps3, D2], w=[t2])
        tt('dve', A(t1)[:, :N], A(t1)[:, :N], A(t2)[:, :N], ALU.add, r=[t1, t2], w=[t1])
        tt('dve', out_ap, A(t1)[:, :N], A(Rr)[:, :N], ALU.mult, r=[t1, Rr], w=out_tr)

    def make_h(xsrc, rdeps, N, Gap, SHap, outs):
        ps = PSR.get()
        for k in range(8):
            sq = SQ.get()
            act(A(sq)[:, :N], xsrc[k], AF.Square, r=rdeps[k], w=[sq])
            mm(A(ps)[:, :N], A(onesb), A(sq)[:, :N], k == 0, k == 7, r=[onesb, sq], w=[ps])
        R = D0
        rsq(A(R)[:, :N], A(ps)[:, :N], 1.0, D * EPS, [ps], [R])
        for k in range(8):
            t = FT.get()
            stt('dve', A(t)[:, :N], xsrc[k], Gap(k), A(R)[:, :N], ALU.mult, ALU.mult, r=rdeps[k] + [R, mder], w=[t])
            oap, otr = outs[k]
            act(oap, A(t)[:, :N], AF.Identity, r=[t, MTH[0]], w=otr, bias=SHap(k))

    def exch_xhalo(par):
        exch_issue(par); exch_finish(par)

    def exch_issue(par):
        o = ex3_in[par].rearrange("p (k s) -> p k s", k=8)
        dma('sp', o[:, :, 0:15], xa[:, :, 15:30], r=[xT] + xTk, w=[T_ex3i[par]])
        dma('sp', o[:, :, 15:30], xa[:, :, 15 + 2033:15 + 2048], r=[xT] + xTk, w=[T_ex3i[par]])
        P.op('pool', lambda e: e.collective_compute("AllGather", ALU.bypass, replica_groups=RG,
                                                    ins=[ex3_in[par]], outs=[ex3_out[par]]),
             r=[T_ex3i[par]], w=[T_ex3o[par]], cc=True)

    def exch_finish(par):
        L = xa[:, :, 0:15]; Rr = xa[:, :, 2063:2078]
        for q in range(4):
            xq = FT.get()
            dma('sp', A(xq)[:, 0:240], ex3_out[par][q * 128:(q + 1) * 128, :], r=[T_ex3o[par]], w=[xq])
            xr = A(xq)[:, 0:240].rearrange("p (k s) -> p k s", k=8)
            if q == 0:
                ts('dve', L, xr[:, :, 15:30], rk_[:, 0:1], None, ALU.mult, None, r=[xq, rkt, xTh], w=[xTh])
                ts('dve', Rr, xr[:, :, 0:15], rk_[:, 4:5], None, ALU.mult, None, r=[xq, rkt, xTh], w=[xTh])
            else:
                stt('dve', L, xr[:, :, 15:30], rk_[:, q:q + 1], L, ALU.mult, ALU.add, r=[xq, rkt, xTh], w=[xTh])
                stt('dve', Rr, xr[:, :, 0:15], rk_[:, 4 + q:5 + q], Rr, ALU.mult, ALU.add, r=[xq, rkt, xTh], w=[xTh])

    pca = A(pc)

    def bcol(j): return pca[:, 64 + j:65 + j]

    md = A(mder)
    mo = A(modT)
    MTH = [modT]

    def cvt_items(l):
        items = []

        def plain(nm, dst, src, rows, cols, c_lo=0, c_hi=None):
            c_hi = cols if c_hi is None else c_hi
            for k in range(rows // 128):
                for c0 in range(c_lo, c_hi, 512):
                    n = min(512, c_hi - c0)
                    rs = slice(k * 128, (k + 1) * 128)
                    items.append(([(lambda sl, n=n: sl[:, 0:n], src[rs, c0:c0 + n])], dst[rs, c0:c0 + n], n, tw(nm, l)))
        plain('in', win16[l], w_in[l], D, 6912, 0, 1024)
        for k in range(8):
            rs = slice(k * 128, (k + 1) * 128)
            lds = []
            for c in range(4):
                for hf in range(2):
                    h_ = hf * 4 + c
                    lds.append((lambda sl, o=c * 128 + hf * 64: sl[:, o:o + 64], w_in[l][rs, 1024 + h_ * 64:1024 + (h_ + 1) * 64]))
            items.append((lds, win16[l][rs, 1024:1536], 512, tw('in', l)))
        plain('in', win16[l], w_in[l], D, 6912, 1536, 6912)
        plain('pa', wpa16[l], w_pa[l], 512, D)
        for c in range(4):
            for c0 in (0, 512):
                lds = []
                for hf in range(2):
                    h_ = hf * 4 + c
                    lds.append((lambda sl, hf=hf: sl[hf * 64:(hf + 1) * 64, 0:512], w_pb[l][h_ * 64:(h_ + 1) * 64, c0:c0 + 512]))
                items.append((lds, wpb16[l][c * 128:(c + 1) * 128, c0:c0 + 512], 512, tw('pb', l)))
        plain('pc', wpc16[l], w_pc[l], 512, D)
        plain('out', wout16[l], w_out[l], D, D)
        plain('up', wup16[l], w_up[l], D, 5632)
        plain('dn', wdn16[l], w_down[l], 2816, D)
        return items

    class Cvt:
        def __init__(self): self.q = []; self.pending = []; self.stg = None; self.i = 0

        def start(self, l, stg=None):
            self.q = cvt_items(l); self.stg = stg or STG; self.i = 0; self.pending = []; self.own = stg

        def step(self, nitems):
            for _ in range(nitems):
                if not self.q: break
                lds, dst, n, trk = self.q.pop(0)
                slot = self.stg[self.i % len(self.stg)]; self.i += 1
                for fn, src in lds:
                    dma('pool', fn(A(slot)), src, w=[slot])
                self.pending.append((slot, dst, n, trk))
                if len(self.pending) > len(self.stg) - 1: self.flush(len(self.stg) - 1)

        def flush(self, keep):
            while len(self.pending) > keep:
                slot, dst, n, trk = self.pending.pop(0)
                dma('pool', dst, A(slot)[:, 0:n], r=[slot], w=[trk])

        def finish(self):
            self.step(10 ** 9); self.flush(0)
            if self.own: gfree_(self.own); self.own = None
    CV = Cvt()

    for l in range(depth):
        par = l % 2
        WIN, WADA, WPA, WPB, WPC, WOUT, WUP, WDN = (win16[l], wada16[l], wpa16[l], wpb16[l], wpc16[l], wout16[l], wup16[l], wdn16[l])
        WENG = 'sp'
        WE[0] = WENG
        exch_issue(par)
        if l == 0:
            CV.start(0)
        elif l + 1 < depth:
            CV.start(l + 1)
            n_cv = (len(CV.q) + 9) // 10
        dma('sp', pca, pcol_in[l], w=[pc])
        dma('pool', A(brow), brow_in[l], w=[brow])
        def emit_adaln(ll):
            mt_ = modTs[ll % 2]; ba_ = badas[ll % 2]
            dma('sp', A(ba_), pcol_in[ll][:, 0:48], w=[ba_])
            for pcs in range(12):
                wt, wv = wload(w_ada[ll][:, pcs * 512:(pcs + 1) * 512], 8, 512, tw('ada_unused', ll), eng='pool')
                for jj in range(4):
                    j_ = pcs * 4 + jj
                    ps = PSR.get()
                    for k in range(8):
                        mm(A(ps)[:, 0:2], wv[:, k, jj * 128:(jj + 1) * 128], A(scT)[:, k, :], k == 0, k == 7,
                           r=[wt, scT], w=[ps])
                    act(A(mt_)[:, j_, :], A(ps)[:, 0:2], AF.Identity, r=[ps, ba_], w=[mt_], bias=A(ba_)[:, j_:j_ + 1])
        if l == 0: emit_adaln(0)
        modT = modTs[l % 2]; mo = A(modT); MTH[0] = modT
        for (gi, ncol, sccol) in ((0, 48, 8), (1, 56, 32)):
            ts('dve', md[:, gi], mo[:, sccol:sccol + 8, :], 1.0, 32.0, ALU.add, ALU.mult, r=[modT], w=[mder])
            tt('dve', md[:, gi], md[:, gi], pca[:, ncol:ncol + 8].unsqueeze(2).to_broadcast([128, 8, 2]), ALU.mult,
               r=[mder, pc], w=[mder])
        lg = A(lgt)
        act(lg, pca[:, 344:368], AF.Exp, r=[pc], w=[lgt], scale=-1.0)
        act(lg, lg, AF.Ln, r=[lgt], w=[lgt], bias=1.0)
        ts('dve', lg, lg, -1.0, None, ALU.mult, None, r=[lgt], w=[lgt])
        sf = A(smallf)
        for h in range(8):
            t1 = FT.get()
            ts('dve', A(t1)[:, 0:128], RELA, lg[:, 8 + h:9 + h], None, ALU.mult, None, r=[cst, lgt], w=[t1])
            stt('dve', A(t1)[:, 0:128], RELB, lg[:, 16 + h:17 + h], A(t1)[:, 0:128], ALU.mult, ALU.add, r=[cst, lgt, t1], w=[t1])
            act(A(t1)[:, 0:128], A(t1)[:, 0:128], AF.Exp, r=[t1], w=[t1])
            tt('dve', A(DTt)[:, h, :], A(t1)[:, 0:128], EYE, ALU.add, r=[t1, cst], w=[DTt])
        for d_ in range(2):
            for c in range(4):
                act(A(QFt)[:, d_, c, :], IOTA1 if d_ == 0 else IOTAC, AF.Exp, r=[cst, lgt], w=[QFt],
                    scale=lg[:, d_ * 4 + c:d_ * 4 + c + 1])
            ts('dve', A(KFt)[:, d_, :], lg[:, 8 + 8 * d_:16 + 8 * d_], REV if d_ == 0 else JCOL, None, ALU.mult, None,
               r=[lgt, cst], w=[KFt])
            act(A(KFt)[:, d_, :], A(KFt)[:, d_, :], AF.Exp, r=[KFt], w=[KFt])
            act(A(CDt)[:, d_, :], lg[:, d_ * 4:d_ * 4 + 4], AF.Exp, r=[lgt], w=[CDt], scale=128.0)
            tt('dve', A(DECt)[:, d_], lg[:, d_ * 4:d_ * 4 + 4].unsqueeze(1).to_broadcast([128, 16, 4]),
               EXPO.unsqueeze(2).to_broadcast([128, 16, 4]), ALU.mult, r=[lgt, cst], w=[DECt])
            act(A(DECt)[:, d_], A(DECt)[:, d_], AF.Exp, r=[DECt], w=[DECt])
            eo = 10 if d_ == 0 else 19
            for q in range(5):
                ecol = rk_[:, eo + q:eo + q + 1] if q < 4 else rk_[:, eo + 8:eo + 9]
                ci = d_ * 5 + q
                ts('dve', A(coef)[:, ci, :], lg[:, d_ * 4:d_ * 4 + 4], ecol, None, ALU.mult, None, r=[lgt, rkt], w=[coef])
                act(A(coef)[:, ci, :], A(coef)[:, ci, :], AF.Exp, r=[coef], w=[coef])
                if q < 4:
                    ts('dve', A(coef)[:, ci, :], A(coef)[:, ci, :], rk_[:, eo + 4 + q:eo + 5 + q], None, ALU.mult, None,
                       r=[coef, rkt], w=[coef])

        G1 = lambda t: (lambda k: md[:, 0, k, t:t + 1])
        SH1 = lambda t: (lambda k: mo[:, k, t:t + 1])
        G2 = lambda t: (lambda k: md[:, 1, k, t:t + 1])
        SH2 = lambda t: (lambda k: mo[:, 24 + k, t:t + 1])

        if STOP <= 1: break
        if l == 0:
            CV.step(112); CV.flush(0)

        if STOP <= 2: break
        PSRh[0] = PSR8
        v4 = lambda ap: ap.rearrange("p (c n) -> p c n", c=4)
        mset('dve', A(Sbk), 0.0, w=[Sbk]); mset('dve', A(Tfk), 0.0, w=[Tfk])
        order = [BLK[4], BLK[3], BLK[2], BLK[1], BLK[0]]
        for b in order:
            j = b['j']; N = b['N']; mc = b['mc']; lat = b['lat']; tcol = 0 if lat else 1
            nt = N // 128
            hT = galloc(8)
            make_h([xa[:, k, mc:mc + N] for k in range(8)], [xk(k) for k in range(8)], N, G1(tcol), SH1(tcol),
                   [(A(hT[k])[:, :N], [hT[k]]) for k in range(8)])
            wk, wkv = wload(WIN[:, 1536:1792], 8, 256, tw('in', l))
            dma('sp', A(D1)[:, :N], rope_in[:, 0, j, 0:N], w=[D1])
            dma('sp', A(D2)[:, :N], rope_in[:, 1, j, 0:N], w=[D2])
            ps = PSR.get()
            for k in range(8):
                mm(A(ps)[:, :N], wkv[:, k, 0:128], A(hT[k])[:, :N], k == 0, k == 7, r=[wk, hT[k]], w=[ps])
            kT = galloc(1)[0]
            normrope(ps, bcol(12), pca[:, 255:256], 8.0, N, A(kT)[:, :N], [kT])
            vt = galloc(1)[0]
            for t in range(nt):
                ps2 = PSR.get()
                for k in range(8):
                    mm(A(ps2)[:, 0:128], A(hT[k])[:, t * 128:(t + 1) * 128], wkv[:, k, 128:256], k == 0, False,
                       r=[wk, hT[k]], w=[ps2])
                mm(A(ps2)[:, 0:128], A(onesb)[0:1, :], A(brow)[0:1, 0:128], False, True, r=[onesb, brow], w=[ps2])
                cp('act', A(vt)[:, t * 128:(t + 1) * 128], A(ps2)[:, 0:128], r=[ps2], w=[vt])
            if lat:
                dma('pool', ex1_in[par][:, j * 512:(j + 1) * 512], A(kT)[:, :N], r=[kT], w=[T_ex1i[par]])
                dma('pool', ex1_in[par][:, 2048 + j * 512:2048 + (j + 1) * 512], A(vt)[:, :N], r=[vt], w=[T_ex1i[par]])
            else:
                dma('pool', cxkv[:, 0:256], A(kT)[:, :N], r=[kT], w=[T_cx])
                dma('pool', cxkv[:, 256:512], A(vt)[:, :N], r=[vt], w=[T_cx])
            gfree_([kT, vt])
            wrk, wrkv = wload(WIN[:, 2304:2816], 8, 512, tw('in', l))
            wrv, wrvv = wload(WIN[:, 2816:3328], 8, 512, tw('in', l))
            def p1_proj(t):
                rkt_ = galloc(1)[0]; rvt_ = galloc(1)[0]
                for (wt_, wv_, boff, dst, scale) in ((wrk, wrkv, 128, rkt_, 0.125), (wrv, wrvv, 640, rvt_, 1.0)):
                    ps3 = PSR.get()
                    for k in range(8):
                        mm(A(ps3), A(hT[k])[:, t * 128:(t + 1) * 128], wv_[:, k, :], k == 0, False, r=[wt_, hT[k]], w=[ps3])
                    mm(A(ps3), A(onesb)[0:1, :], A(brow)[0:1, boff:boff + 512], False, True, r=[onesb, brow], w=[ps3])
                    act(A(dst), A(ps3), AF.Copy, r=[ps3], w=[dst], scale=scale)
                kd = galloc(2)
                for d_ in range(2):
                    tt('dve', A(kd[d_]).rearrange("p (h d) -> p h d", h=8), A(rkt_).rearrange("p (h d) -> p h d", h=8),
                       A(KFt)[:, d_, :].unsqueeze(2).to_broadcast([128, 8, 64]), ALU.mult, r=[rkt_, KFt], w=[kd[d_]])
                return (t, rkt_, rvt_, kd)

            def p1_scan(st):
                t, rkt_, rvt_, kd = st
                cidx = (j * 4 + t) if lat else (16 + t)
                dci = (j * 4 + t) if lat else (14 + t)
                sbl = SBL.get()
                tt('dve', A(sbl), A(Sbk), A(BMASK), ALU.mult, r=[Sbk, BMASK], w=[sbl])
                if True:
                    dma('pool', sbscr[cidx], A(sbl), r=[sbl], w=[T_sbs[cidx]])
                psB = PSR.get(); psF = PSR.get()
                for c in range(4):
                    cs_ = slice(c * 128, (c + 1) * 128)
                    mm(A(psB)[:, cs_], A(kd[1])[:, cs_], A(rvt_)[:, cs_], True, True, r=[kd[1], rvt_], w=[psB])
                    mm(A(psF)[:, cs_], A(kd[0])[:, cs_], A(rvt_)[:, cs_], True, True, r=[kd[0], rvt_], w=[psF])
                tt('dve', v4(A(Sbk)), v4(A(Sbk)), A(CDt)[:, 1, :].unsqueeze(2).to_broadcast([128, 4, 128]), ALU.mult,
                   r=[Sbk, CDt], w=[Sbk])
                tt('dve', A(Sbk), A(Sbk), A(psB), ALU.add, r=[Sbk, psB], w=[Sbk])
                tf = FT.get()
                tt('dve', v4(A(tf)[:, 0:512]), v4(A(psF)), A(DECt)[:, 0, dci, :].unsqueeze(2).to_broadcast([128, 4, 128]),
                   ALU.mult, r=[psF, DECt], w=[tf])
                tt('pool', A(Tfk), A(Tfk), A(tf)[:, 0:512], ALU.add, r=[Tfk, tf], w=[Tfk])
                gfree_([rkt_, rvt_] + kd)
            pend1 = None
            for t in reversed(range(nt)):
                st = p1_proj(t)
                if pend1 is not None: p1_scan(pend1)
                pend1 = st
            p1_scan(pend1)
            if not lat:
                tt('dve', v4(A(Ssf)), v4(A(Tfk)), A(coef)[:, 4, :].unsqueeze(2).to_broadcast([128, 4, 128]), ALU.mult,
                   r=[Tfk, coef], w=[Ssf])
                tt('dve', v4(A(Ssb)), v4(A(Sbk)), A(coef)[:, 9, :].unsqueeze(2).to_broadcast([128, 4, 128]), ALU.mult,
                   r=[Sbk, coef], w=[Ssb])
                mset('dve', A(Sbk), 0.0, w=[Sbk]); mset('dve', A(Tfk), 0.0, w=[Tfk])
            gfree_(hT)
        dma('pool', ex2_in[par][:, 0:512], A(Tfk), r=[Tfk], w=[T_ex2i[par]])
        dma('pool', ex2_in[par][:, 512:1024], A(Sbk), r=[Sbk], w=[T_ex2i[par]])
        PSRh[0] = PSR4
        if STOP <= 3: break
        P.op('pool', lambda e, par=par: e.collective_compute("AllGather", ALU.bypass, replica_groups=RG,
                                                              ins=[ex1_in[par]], outs=[ex1_out[par]]),
             r=[T_ex1i[par]], w=[T_ex1o[par]], cc=True)
        P.op('pool', lambda e, par=par: e.collective_compute("AllGather", ALU.bypass, replica_groups=RG,
                                                              ins=[ex2_in[par]], outs=[ex2_out[par]]),
             r=[T_ex2i[par]], w=[T_ex2o[par]], cc=True)
        exch_finish(par)
        xhk = lambda k: A(ACC[k // 2])[:, (k % 2) * 256:(k % 2) * 256 + 150]
        xht = lambda k: ACC[k // 2]
        for b in BLK:
            j = b['j']; ws = b['ws']; N = b['N']
            for k in range(8):
                cp('pool', xhk(k)[:, j * 30:j * 30 + 15], xa[:, k, ws:ws + 15], r=xk(k) + [xTh], w=[xht(k)])
                cp('pool', xhk(k)[:, j * 30 + 15:j * 30 + 30], xa[:, k, ws + 15 + N:ws + 30 + N], r=xk(k) + [xTh], w=[xht(k)])
        hhs = galloc(3)
        hhk = lambda k: A(hhs[k // 3])[:, (k % 3) * 150:(k % 3) * 150 + 150]
        hht = lambda k: hhs[k // 3]
        make_h([xhk(k)[:, 0:120] for k in range(8)], [[xht(k)] for k in range(8)], 120, G1(0), SH1(0),
               [(hhk(k)[:, 0:120], [hht(k)]) for k in range(8)])
        make_h([xhk(k)[:, 120:150] for k in range(8)], [[xht(k)] for k in range(8)], 30, G1(1), SH1(1),
               [(hhk(k)[:, 120:150], [hht(k)]) for k in range(8)])
        wa, wav = wload(WIN[:, 0:512], 8, 512, tw('in', l))
        wb, wbv = wload(WIN[:, 512:1024], 8, 512, tw('in', l))
        uha = A(uh)
        for c in range(4):
            psa = PSR.get(); psb = PSR.get()
            for k in range(8):
                mm(A(psa)[:, :150], wav[:, k, c * 128:(c + 1) * 128], hhk(k), k == 0, k == 7, r=[wa, hht(k)], w=[psa])
            for k in range(8):
                mm(A(psb)[:, :150], wbv[:, k, c * 128:(c + 1) * 128], hhk(k), k == 0, k == 7, r=[wb, hht(k)], w=[psb])
            sg = FT.get()
            act(A(sg)[:, :150], A(psb)[:, :150], AF.Sigmoid, r=[psb, pc], w=[sg], bias=bcol(4 + c))
            stt('dve', uha[:, c, :], A(psa)[:, :150], bcol(c), A(sg)[:, :150], ALU.add, ALU.mult, r=[psa, sg, pc], w=[uh])
            ts('dve', uha[:, c, 0:15], uha[:, c, 0:15], rk_[:, 8:9], None, ALU.mult, None, r=[uh, rkt], w=[uh])
            ts('dve', uha[:, c, 105:120], uha[:, c, 105:120], rk_[:, 9:10], None, ALU.mult, None, r=[uh, rkt], w=[uh])
            mset('dve', uha[:, c, 120:150], 0.0, w=[uh])
        gfree_(hhs)

        if STOP <= 4: break
        if l == 0:
            CV.finish()
            if depth > 1:
                CV.start(1)
                n_cv = (len(CV.q) + 9) // 10
        for b in BLK:
            j = b['j']; N = b['N']; mc = b['mc']; ws = b['ws']; lat = b['lat']; tcol = 0 if lat else 1
            nt = N // 128
            if not lat: exch_issue(1 - par)
            if l == depth - 1 and not lat: continue
            if l + 1 < depth: CV.step(n_cv)
            hT = galloc(8)
            make_h([xa[:, k, mc:mc + N] for k in range(8)], [xk(k) for k in range(8)], N, G1(tcol), SH1(tcol),
                   [(A(hT[k])[:, :N], [hT[k]]) for k in range(8)])

            if l == 0 and j == 0:
                for k in range(8): dump16(k, hT[k])
            wrq, wrqv = wload(WIN[:, 1792:2304], 8, 512, tw('in', l))
            rqT = galloc(4); rkT = galloc(4)
            for c in range(4):
                ps = PSR.get()
                for k in range(8):
                    mm(A(ps)[:, :N], wrqv[:, k, c * 128:(c + 1) * 128], A(hT[k])[:, :N], k == 0, k == 7, r=[wrq, hT[k]], w=[ps])
                act(A(rqT[c])[:, :N], A(ps)[:, :N], AF.Identity, r=[ps, pc], w=[rqT[c]], bias=bcol(14 + c))
            wrk, wrkv = wload(WIN[:, 2304:2816], 8, 512, tw('in', l))
            for c in range(4):
                ps = PSR.get()
                for k in range(8):
                    mm(A(ps)[:, :N], wrkv[:, k, c * 128:(c + 1) * 128], A(hT[k])[:, :N], k == 0, k == 7, r=[wrk, hT[k]], w=[ps])
                ts('dve', A(rkT[c])[:, :N], A(ps)[:, :N], bcol(18 + c), 0.125, ALU.add, ALU.mult, r=[ps, pc], w=[rkT[c]])
            rkM = galloc(nt); rvM = galloc(nt); rgM = galloc(nt)
            for t in range(nt):
                ps3 = PSR.get()
                for k in range(8):
                    mm(A(ps3), A(hT[k])[:, t * 128:(t + 1) * 128], wrkv[:, k, :], k == 0, False, r=[wrk, hT[k]], w=[ps3])
                mm(A(ps3), A(onesb)[0:1, :], A(brow)[0:1, 128:640], False, True, r=[onesb, brow], w=[ps3])
                act(A(rkM[t]), A(ps3), AF.Copy, r=[ps3], w=[rkM[t]], scale=0.125)
            if STOP <= 4.05: break
            wrv, wrvv = wload(WIN[:, 2816:3328], 8, 512, tw('in', l))
            for t in range(nt):
                ps3 = PSR.get()
                for k in range(8):
                    mm(A(ps3), A(hT[k])[:, t * 128:(t + 1) * 128], wrvv[:, k, :], k == 0, False, r=[wrv, hT[k]], w=[ps3])
                mm(A(ps3), A(onesb)[0:1, :], A(brow)[0:1, 640:1152], False, True, r=[onesb, brow], w=[ps3])
                cp('act', A(rvM[t]), A(ps3), r=[ps3], w=[rvM[t]])
            wrg, wrgv = wload(WIN[:, 3328:3840], 8, 512, tw('in', l))
            for t in range(nt):
                ps3 = PSR.get()
                for k in range(8):
                    mm(A(ps3), A(hT[k])[:, t * 128:(t + 1) * 128], wrgv[:, k, :], k == 0, False, r=[wrg, hT[k]], w=[ps3])
                mm(A(ps3), A(onesb)[0:1, :], A(brow)[0:1, 1152:1664], False, True, r=[onesb, brow], w=[ps3])
                act(A(rgM[t]), A(ps3), AF.Silu, r=[ps3], w=[rgM[t]])
            wa, wav = wload(WIN[:, 0:512], 8, 512, tw('in', l))
            wb, wbv = wload(WIN[:, 512:1024], 8, 512, tw('in', l))
            for c in range(4):
                psa = PSR.get(); psb = PSR.get()
                for k in range(8):
                    mm(A(psa)[:, :N], wav[:, k, c * 128:(c + 1) * 128], A(hT[k])[:, :N], k == 0, k == 7, r=[wa, hT[k]], w=[psa])
                for k in range(8):
                    mm(A(psb)[:, :N], wbv[:, k, c * 128:(c + 1) * 128], A(hT[k])[:, :N], k == 0, k == 7, r=[wb, hT[k]], w=[psb])
                sg = FT.get()
                act(A(sg)[:, :N], A(psb)[:, :N], AF.Sigmoid, r=[psb, pc], w=[sg], bias=bcol(4 + c))
                u = ubuf[c]
                stt('dve', A(u)[:, 15:15 + N], A(psa)[:, :N], bcol(c), A(sg)[:, :N], ALU.add, ALU.mult, r=[psa, sg, pc], w=[u])
                cp('pool', A(u)[:, 0:15], A(uh)[:, c, j * 30:j * 30 + 15], r=[uh], w=[u])
                cp('pool', A(u)[:, 15 + N:30 + N], A(uh)[:, c, j * 30 + 15:j * 30 + 30], r=[uh], w=[u])
            wc = lambda kk, c: pca[:, 118 + c * 31 + kk:119 + c * 31 + kk]
            def conv_pe(c):
                psc = PSR.get()
                for kk in range(31):
                    dg = DG.get()
                    act(A(dg), A(ident), AF.Copy, r=[ident, pc], w=[dg], scale=wc(kk, c))
                    mm(A(psc)[:, :N], A(dg), A(ubuf[c])[:, kk:kk + N], kk == 0, kk == 30, r=[dg, ubuf[c]], w=[psc])
                act(A(ACC[c])[:, :N], A(psc)[:, :N], AF.Identity, r=[psc, pc], w=[ACC[c]], bias=pca[:, 242 + c:243 + c])
            cpc = 4 // nt
            if j == 0:
                v4 = lambda ap: ap.rearrange("p (c n) -> p c n", c=4)
                for d_, Sst in enumerate((Ssf, Ssb)):
                    for q in range(4):
                        tl = FT.get()
                        dma('sp', A(tl)[:, 0:512], ex2_out[par][q * 128:(q + 1) * 128, d_ * 512:(d_ + 1) * 512], r=[T_ex2o[par]], w=[tl])
                        tt('dve', v4(A(tl)[:, 0:512]), v4(A(tl)[:, 0:512]),
                           A(coef)[:, d_ * 5 + q, :].unsqueeze(2).to_broadcast([128, 4, 128]), ALU.mult, r=[tl, coef], w=[tl])
                        tt('dve', A(Sst), A(Sst), A(tl)[:, 0:512], ALU.add, r=[Sst, tl], w=[Sst])

                tt('dve', A(Ssb), A(Ssb), A(BMASK), ALU.mult, r=[Ssb, BMASK], w=[Ssb])
            if j == 0:
                cp('dve', A(Sf), A(Ssf), r=[Ssf], w=[Sf])
            if not lat:
                mset('dve', A(Sf), 0.0, w=[Sf])
            retT = galloc(4)
            v8 = lambda ap: ap.rearrange("p (h e) -> p h e", h=8)
            for t in range(nt):
                cidx = (j * 4 + t) if lat else (16 + t)
                tsl = slice(t * 128, (t + 1) * 128)
                tt('dve', A(Sf16), A(Sf), A(BMASK), ALU.mult, r=[Sf, BMASK], w=[Sf16])
                sbl = SBL.get()
                dma('sp', A(sbl), sbscr[cidx], r=[T_sbs[cidx]], w=[sbl])
                if STOP <= 4.101: break
                if lat:
                    sbu = galloc(1)[0]
                    tl = FT.get()
                    tt('dve', v4(A(tl)[:, 0:512]), v4(A(Ssb)), A(DECt)[:, 1, j * 4 + t, :].unsqueeze(2).to_broadcast([128, 4, 128]),
                       ALU.mult, r=[Ssb, DECt], w=[tl])
                    tt('dve', A(sbu), A(sbl), A(tl)[:, 0:512], ALU.add, r=[sbl, tl], w=[sbu])
                else:
                    sbu = sbl
                if STOP <= 4.102: break
                qd = galloc(2)
                for d_ in range(2):
                    for c in range(4):
                        tt('dve' if d_ == 0 else 'pool', A(qd[d_])[:, c * 128:(c + 1) * 128], A(rqT[c])[:, tsl], A(QFt)[:, d_, c, :],
                           ALU.mult, r=[rqT[c], QFt], w=[qd[d_]])
                if STOP <= 4.103: break
                kdf = galloc(1)[0]
                tt('pool', v8(A(kdf)), v8(A(rkM[t])), A(KFt)[:, 0, :].unsqueeze(2).to_broadcast([128, 8, 64]), ALU.mult,
                   r=[rkM[t], KFt], w=[kdf])
                if STOP <= 4.104: break
                WT = galloc(2)
                for c in range(4):
                    cp('act', A(RKL)[0:64, c * 128:(c + 1) * 128], A(rkT[c])[0:64, tsl], r=[rkT[c]], w=[RKL])
                    cp('act', A(RKH)[64:128, c * 128:(c + 1) * 128], A(rkT[c])[64:128, tsl], r=[rkT[c]], w=[RKH])
                for hp in range(2):
                    psa = PSR.get()
                    for hh_ in range(4):
                        h = hp * 4 + hh_; c = h // 2
                        RKm = RKL if h % 2 == 0 else RKH
                        mm(A(psa)[:, hh_ * 128:(hh_ + 1) * 128], A(RKm)[:, c * 128:(c + 1) * 128], A(rqT[c])[:, tsl], True, True,
                           r=[RKm, rqT[c]], w=[psa])
                    tt('dve', A(WT[hp]), A(psa), A(DTt)[:, hp * 4:(hp + 1) * 4, :].rearrange("p h n -> p (h n)"), ALU.mult,
                       r=[psa, DTt], w=[WT[hp]])
                if STOP <= 4.11: break
                po = PSL[t % 4]
                for h in range(8):
                    c = h // 2; off = (h % 2) * 64
                    osl = slice(h * 64, (h + 1) * 64)
                    mm(A(po)[:, osl], A(WT[h // 4])[:, (h % 4) * 128:(h % 4 + 1) * 128], A(rvM[t])[:, osl], True, False,
                       r=[WT[h // 4], rvM[t]], w=[po])
                    mm(A(po)[:, osl], A(qd[0])[:, c * 128:(c + 1) * 128],
                       A(Sf16)[:, c * 128 + off:c * 128 + off + 64], False, False, r=[qd[0], Sf16], w=[po])
                    mm(A(po)[:, osl], A(qd[1])[:, c * 128:(c + 1) * 128],
                       A(sbu)[:, c * 128 + off:c * 128 + off + 64], False, True, r=[qd[1], sbu], w=[po])
                ob = FT.get(); sq = FT.get()
                cp('act', A(ob)[:, 0:512], A(po), r=[po], w=[ob])
                act(A(sq)[:, 0:512], A(po), AF.Square, r=[po], w=[sq])
                psS = PSR.get()
                for c in range(4):
                    cs_ = slice(c * 128, (c + 1) * 128)
                    mm(A(psS)[:, cs_], A(kdf)[:, cs_], A(rvM[t])[:, cs_], True, True, r=[kdf, rvM[t]], w=[psS])
                tt('dve', v4(A(Sf)), v4(A(Sf)), A(CDt)[:, 0, :].unsqueeze(2).to_broadcast([128, 4, 128]), ALU.mult, r=[Sf, CDt], w=[Sf])
                tt('dve', A(Sf), A(Sf), A(psS), ALU.add, r=[Sf, psS], w=[Sf])
                for c in range(t * cpc, (t + 1) * cpc): conv_pe(c)
                s1 = sf[:, 0:8]; s2 = sf[:, 8:16]; s3 = sf[:, 16:24]
                P.op('dve', lambda e, ob=ob, s1=s1: e.tensor_reduce(s1, v8(A(ob)[:, 0:512]), AX.X, ALU.add), r=[ob], w=[smallf])
                P.op('dve', lambda e, sq=sq, s2=s2: e.tensor_reduce(s2, v8(A(sq)[:, 0:512]), AX.X, ALU.add), r=[sq], w=[smallf])
                ts('dve', s1, s1, 1.0 / 64, None, ALU.mult, None, r=[smallf], w=[smallf])
                tt('dve', s3, s1, s1, ALU.mult, r=[smallf], w=[smallf])
                stt('dve', s2, s2, 1.0 / 64, s3, ALU.mult, ALU.subtract, r=[smallf], w=[smallf])
                rsq(s2, s2, 1.0, EPS, [smallf], [smallf])
                tt('dve', v8(A(ob)[:, 0:512]), v8(A(ob)[:, 0:512]), s1.unsqueeze(2).to_broadcast([128, 8, 64]), ALU.subtract,
                   r=[ob, smallf], w=[ob])
                tt('dve', v8(A(ob)[:, 0:512]), v8(A(ob)[:, 0:512]), s2.unsqueeze(2).to_broadcast([128, 8, 64]), ALU.mult,
                   r=[ob, smallf], w=[ob])
                yb = galloc(1)[0]
                tt('dve', A(yb), A(ob)[:, 0:512], A(rgM[t]), ALU.mult, r=[ob, rgM[t]], w=[yb])
                pst_ = PSR.get()
                for c in range(4):
                    mm(A(pst_)[:, c * 128:(c + 1) * 128], A(yb)[:, c * 128:(c + 1) * 128], A(ident), True, True, r=[yb, ident], w=[pst_])
                for c in range(4):
                    act(A(retT[c])[:, tsl], A(pst_)[:, c * 128:(c + 1) * 128], AF.Copy, r=[pst_, pc], w=[retT[c]],
                        scale=pca[:, 368 + c:369 + c])
                gfree_(qd + [kdf, yb] + WT + ([sbu] if lat else []))
            if l == 0 and j == 0:
                for c in range(4): dump16(8 + c, retT[c])
                for c in range(4): dump16(28 + c, rqT[c])
                for c in range(4): dump16(32 + c, rkT[c])
                for c in range(4): dump16(36 + c, rvM[c])
            gfree_(rqT + rkT + rkM + rvM + rgM)

            if STOP <= 4.2: break
            aT = galloc(4)
            ps1 = PSR.get(); ps2 = PSR.get()
            for c in range(4):
                mm(A(ps1)[:, :N], A(onesf), A(ACC[c])[:, :N], c == 0, c == 3, r=[onesf, ACC[c]], w=[ps1])
            for c in range(4):
                sq = FT.get()
                act(A(sq)[:, :N], A(ACC[c])[:, :N], AF.Square, r=[ACC[c]], w=[sq])
                mm(A(ps2)[:, :N], A(onesf), A(sq)[:, :N], c == 0, c == 3, r=[onesf, sq], w=[ps2])
            mt = D1; vt_ = D2
            ts('dve', A(mt)[:, :N], A(ps1)[:, :N], 1.0 / 512, None, ALU.mult, None, r=[ps1], w=[mt])
            tt('dve', A(vt_)[:, :N], A(mt)[:, :N], A(mt)[:, :N], ALU.mult, r=[mt], w=[vt_])
            stt('dve', A(vt_)[:, :N], A(ps2)[:, :N], 1.0 / 512, A(vt_)[:, :N], ALU.mult, ALU.subtract, r=[ps2, vt_], w=[vt_])
            rsq(A(vt_)[:, :N], A(vt_)[:, :N], 1.0, EPS, [vt_], [vt_])
            for c in range(4):
                acc = ACC[c]
                tt('dve', A(acc)[:, :N], A(acc)[:, :N], A(mt)[:, :N], ALU.subtract, r=[acc, mt], w=[acc])
                tt('pool', A(acc)[:, :N], A(acc)[:, :N], A(vt_)[:, :N], ALU.mult, r=[acc, vt_], w=[acc])
                ts('dve', A(acc)[:, :N], A(acc)[:, :N], pca[:, 246 + c:247 + c], pca[:, 250 + c:251 + c], ALU.mult, ALU.add,
                   r=[acc, pc], w=[acc])
                act(A(aT[c])[:, :N], A(acc)[:, :N], AF.Silu, r=[acc], w=[aT[c]])

            if STOP <= 4.3: break
            if False:
                wq = Wrot.get()
                wqv = A(wq)[:, 0:4096].rearrange("p (k n) -> p k n", k=8)
                for c in range(4):
                    for hf in range(2):
                        h_ = hf * 4 + c
                        dma('pool', wqv[:, :, c * 128 + hf * 64:c * 128 + hf * 64 + 64],
                            w_in[0][:, 1024 + h_ * 64:1024 + (h_ + 1) * 64].rearrange("(k p) n -> p k n", p=128), w=[wq])
            else:
                wq, wqv = wload(WIN[:, 1024:1536], 8, 512, tw('in', l))
            dma('sp', A(D1)[:, :N], rope_in[:, 0, j, 0:N], w=[D1])
            dma('sp', A(D2)[:, :N], rope_in[:, 1, j, 0:N], w=[D2])
            qT = galloc(4)
            for c in range(4):
                ps = PSR.get()
                for k in range(8):
                    mm(A(ps)[:, :N], wqv[:, k, c * 128:(c + 1) * 128], A(hT[k])[:, :N], k == 0, k == 7, r=[wq, hT[k]], w=[ps])
                normrope(ps, bcol(8 + c), pca[:, 254:255], 1.0, N, A(qT[c])[:, :N], [qT[c]])
            if l == 0 and j == 0:
                for c in range(4): dump16(12 + c, aT[c])
                for c in range(4): dump16(24 + c, qT[c])
            attT = galloc(4)
            NS_ = 16 // GT
            groups = [('c', 0)] + ([(q, s_) for q in range(4) for s_ in range(NS_)] if lat else [])
            for half in range(2):
                hs = slice(half * 64, half * 64 + 64)
                os_ = slice((1 - half) * 64, (1 - half) * 64 + 64)
                VGr = VG0 if half == 0 else VG1
                voff = 0 if half == 0 else 64
                first = True
                pend = []

                def flush_pv(keep):
                    while len(pend) > keep:
                        (po_, vg_, t_, pt_, st_, sp_) = pend.pop(0)
                        mm(A(po_)[:, :N], A(vg_)[:, t_, :], A(pt_)[:, :N], st_, sp_, r=[vg_, pt_], w=[po_])
                for gi, (gq, gs) in enumerate(groups):
                    ktg = (KTG0 if half == 0 else KTG1).get(); vg = VGr.get()
                    if gq == 'c':
                        ntile = 2
                        dma('sp', A(ktg)[hs, 0:256], cxkv[hs, 0:256], r=[T_cx], w=[ktg])
                        dma('sp', A(vg)[:, 0:2, voff:voff + 64],
                            cxkv[:, 256:512].rearrange("p (t e) -> p t e", t=2)[:, :, half * 64:half * 64 + 64],
                            r=[T_cx], w=[vg])
                    else:
                        ntile = GT
                        GW = GT * 128
                        dma('sp', A(ktg)[hs, :], ex1_out[par][gq * 128 + half * 64:gq * 128 + half * 64 + 64, gs * GW:(gs + 1) * GW],
                            r=[T_ex1o[par]], w=[ktg])
                        dma('sp', A(vg)[:, :, voff:voff + 64],
                            ex1_out[par][gq * 128:(gq + 1) * 128, 2048 + gs * GW:2048 + (gs + 1) * GW]
                            .rearrange("p (t e) -> p t e", t=GT)[:, :, half * 64:half * 64 + 64],
                            r=[T_ex1o[par]], w=[vg])
                    last_g = gi == len(groups) - 1
                    for c in range(4):
                        po = PSL[c]
                        for t in range(ntile):
                            ps = PSR.get()
                            mm(A(ps)[:, :N], A(ktg)[:, t * 128:(t + 1) * 128], A(qT[c])[:, :N], True, True, r=[ktg, qT[c]], w=[ps])
                            pt = SQ.get()
                            act(A(pt)[:, :N], A(ps)[:, :N], AF.Exp, r=[ps], w=[pt])
                            pend.append((po, vg, t, pt, first and t == 0, last_g and t == ntile - 1))
                            flush_pv(2)
                    first = False
                flush_pv(0)
                for c in range(4):
                    po = PSL[c]
                    rc = FT.get()
                    mset('dve', A(rc)[hs, :N], 0.0, w=[rc])
                    P.op('act', lambda e, rc=rc, po=po, os_=os_, N=N: e.activation(A(rc)[os_, :N], A(po)[os_, :N], AF.Ln), r=[po], w=[rc])
                    P.op('act', lambda e, rc=rc, os_=os_, N=N: e.activation(A(rc)[os_, :N], A(rc)[os_, :N], AF.Exp, scale=-1.0), r=[rc], w=[rc])
                    ob = FT.get()
                    cp('act', A(ob)[hs, :N], A(po)[hs, :N], r=[po], w=[ob])
                    ps = PSR.get()
                    mm(A(ps)[:, :N], SWAP, A(rc)[:, :N], True, True, r=[cst, rc], w=[ps])
                    tt('dve', A(attT[c])[hs, :N], A(ob)[hs, :N], A(ps)[hs, :N], ALU.mult, r=[ob, ps], w=[attT[c]])
            gfree_(qT)

            if STOP <= 4.4: break
            zT = galloc(8)
            for gsec, (wsrc, wnm, br) in enumerate(((WPA, 'pa', aT), (WPB, 'pb', attT), (WPC, 'pc', retT))):
                for og in range(2):
                    if False:
                        wp_ = Wrot.get()
                        wpv = A(wp_)[:, 0:4096].rearrange("p (k n) -> p k n", k=4)
                        for c in range(4):
                            for hf in range(2):
                                h_ = hf * 4 + c
                                dma('pool', wpv[hf * 64:(hf + 1) * 64, c, :], w_pb[0][h_ * 64:(h_ + 1) * 64, :], w=[wp_])
                    else:
                        wp_, wpv = wload(wsrc, 4, 1024, tw(wnm, l))
                    c0 = 3840 + gsec * 1024 + og * 512
                    wg, wgv = wload(WIN[:, c0:c0 + 512], 8, 512, tw('in', l))
                    for oo in range(4):
                        o = og * 4 + oo
                        psg = PSR.get(); psp = PSR.get()
                        for k in range(8):
                            mm(A(psg)[:, :N], wgv[:, k, oo * 128:(oo + 1) * 128], A(hT[k])[:, :N], k == 0, k == 7, r=[wg, hT[k]], w=[psg])
                        for c in range(4):
                            mm(A(psp)[:, :N], wpv[:, c, o * 128:(o + 1) * 128], A(br[c])[:, :N], c == 0, c == 3, r=[wp_, br[c]], w=[psp])
                        sg = FT.get()
                        act(A(sg)[:, :N], A(psg)[:, :N], AF.Sigmoid, r=[psg, pc], w=[sg], bias=bcol(30 + gsec * 8 + o))
                        if gsec == 0:
                            tt('dve', A(zT[o])[:, :N], A(psp)[:, :N], A(sg)[:, :N], ALU.mult, r=[psp, sg], w=[zT[o]])
                        else:
                            tt('dve', A(sg)[:, :N], A(psp)[:, :N], A(sg)[:, :N], ALU.mult, r=[psp, sg], w=[sg])
                            tt('pool', A(zT[o])[:, :N], A(zT[o])[:, :N], A(sg)[:, :N], ALU.add, r=[zT[o], sg], w=[zT[o]])
            if l == 0 and j == 0:
                for c in range(4): dump16(16 + c, attT[c])
                for o in range(4): dump16(20 + o, zT[o])
            gfree_(aT + attT + retT + hT)
            for og in range(2):
                wo, wov = wload(WOUT[:, og * 512:(og + 1) * 512], 8, 512, tw('out', l))
                for oo in range(4):
                    o = og * 4 + oo
                    ps = PSR.get()
                    for k in range(8):
                        mm(A(ps)[:, :N], wov[:, k, oo * 128:(oo + 1) * 128], A(zT[k])[:, :N], k == 0, k == 7, r=[wo, zT[k]], w=[ps])
                    stt('dve', xa[:, o, mc:mc + N], A(ps)[:, :N], mo[:, 16 + o, tcol:tcol + 1], xa[:, o, mc:mc + N], ALU.mult, ALU.add,
                        r=[ps, modT] + xk(o), w=[xTk[o]])
            gfree_(zT)
            if l == 0 and j == 0:
                for o in range(8): dump32(o, xa[:, o, mc:mc + N], xk(o))
                dump32(8, mo.rearrange("p j t -> p (j t)"), [modT], 96)

        if STOP <= 5: break
        exch_finish(1 - par)
        PSRh[0] = PSR8
        for b in BLK:
            j = b['j']; ws = b['ws']; N = b['N']
            for k in range(8):
                cp('pool', xhk(k)[:, 2 * j:2 * j + 1], xa[:, k, ws + 14:ws + 15], r=xk(k) + [xTh], w=[xht(k)])
                cp('pool', xhk(k)[:, 2 * j + 1:2 * j + 2], xa[:, k, ws + 15 + N:ws + 16 + N], r=xk(k) + [xTh], w=[xht(k)])
        hhs = galloc(3)
        make_h([xhk(k)[:, 0:8] for k in range(8)], [[xht(k)] for k in range(8)], 8, G2(0), SH2(0), [(hhk(k)[:, 0:8], [hht(k)]) for k in range(8)])
        make_h([xhk(k)[:, 8:10] for k in range(8)], [[xht(k)] for k in range(8)], 2, G2(1), SH2(1), [(hhk(k)[:, 8:10], [hht(k)]) for k in range(8)])
        for k in range(8):
            ts('dve', hhk(k)[:, 0:1], hhk(k)[:, 0:1], rk_[:, 8:9], None, ALU.mult, None, r=[hht(k), rkt], w=[hht(k)])
            ts('dve', hhk(k)[:, 7:8], hhk(k)[:, 7:8], rk_[:, 9:10], None, ALU.mult, None, r=[hht(k), rkt], w=[hht(k)])
            mset('dve', hhk(k)[:, 8:10], 0.0, w=[hht(k)])
        ga = A(gth)
        for b in BLK:
            j = b['j']; N = b['N']; mc = b['mc']; lat = b['lat']; tcol = 0 if lat else 1
            if l == depth - 1 and not lat: continue
            if l + 1 < depth: CV.step(n_cv)
            hT = galloc(8)
            make_h([xa[:, k, mc:mc + N] for k in range(8)], [xk(k) for k in range(8)], N, G2(tcol), SH2(tcol),
                   [(A(hT[k])[:, :N], [hT[k]]) for k in range(8)])
            actT = galloc(22)
            for pcs in range(11):
                wt = Wrot.get()
                wv = A(wt)[:, 0:4096].rearrange("p (k n) -> p k n", k=8)
                dma(WENG, wv[:, :, 0:256], WUP[:, pcs * 256:(pcs + 1) * 256].rearrange("(k p) n -> p k n", p=128), r=[tw('up', l)], w=[wt])
                dma(WENG, wv[:, :, 256:512], WUP[:, 2816 + pcs * 256:2816 + (pcs + 1) * 256].rearrange("(k p) n -> p k n", p=128),
                    r=[tw('up', l)], w=[wt])
                for ff in range(2):
                    f = pcs * 2 + ff
                    if j == 0:
                        psh = PSR.get()
                        for k in range(8):
                            mm(A(psh)[:, 0:10], wv[:, k, ff * 128:(ff + 1) * 128], hhk(k)[:, 0:10], k == 0, k == 7, r=[wt, hht(k)], w=[psh])
                        cp('act', A(gth)[:, f, :], A(psh)[:, 0:10], r=[psh], w=[gth])
                    psg = PSR.get(); psv = PSR.get()
                    for k in range(8):
                        mm(A(psg)[:, :N], wv[:, k, ff * 128:(ff + 1) * 128], A(hT[k])[:, :N], k == 0, k == 7, r=[wt, hT[k]], w=[psg])
                    for k in range(8):
                        mm(A(psv)[:, :N], wv[:, k, 256 + ff * 128:256 + (ff + 1) * 128], A(hT[k])[:, :N], k == 0, k == 7, r=[wt, hT[k]], w=[psv])
                    gt = FT.get()
                    cp('act', A(gt)[:, 1:1 + N], A(psg)[:, :N], r=[psg], w=[gt])
                    cp('act', A(gt)[:, 0:1], ga[:, f, 2 * j:2 * j + 1], r=[gth], w=[gt])
                    cp('act', A(gt)[:, 1 + N:2 + N], ga[:, f, 2 * j + 1:2 * j + 2], r=[gth], w=[gt])
                    acc = FT.get()
                    wf = lambda kk: pca[:, 256 + f * 3 + kk:257 + f * 3 + kk]
                    ts('dve', A(acc)[:, :N], A(gt)[:, 0:N], wf(0), pca[:, 322 + f:323 + f], ALU.mult, ALU.add, r=[gt, pc], w=[acc])
                    stt('dve', A(acc)[:, :N], A(gt)[:, 1:1 + N], wf(1), A(acc)[:, :N], ALU.mult, ALU.add, r=[gt, pc, acc], w=[acc])
                    stt('pool', A(acc)[:, :N], A(gt)[:, 2:2 + N], wf(2), A(acc)[:, :N], ALU.mult, ALU.add, r=[gt, pc, acc], w=[acc])
                    act(A(acc)[:, :N], A(acc)[:, :N], AF.Silu, r=[acc], w=[acc])
                    tt('dve', A(actT[f])[:, :N], A(acc)[:, :N], A(psv)[:, :N], ALU.mult, r=[acc, psv], w=[actT[f]])
            for o in range(8):
                wt = Wrot.get()
                wv = A(wt)[:, 0:22 * 128].rearrange("p (k n) -> p k n", k=22)
                dma(WENG, wv, WDN[:, o * 128:(o + 1) * 128].rearrange("(k p) n -> p k n", p=128), r=[tw('dn', l)], w=[wt])
                ps = PSR.get()
                for f in range(22):
                    mm(A(ps)[:, :N], wv[:, f, :], A(actT[f])[:, :N], f == 0, f == 21, r=[wt, actT[f]], w=[ps])
                stt('dve', xa[:, o, mc:mc + N], A(ps)[:, :N], mo[:, 40 + o, tcol:tcol + 1], xa[:, o, mc:mc + N], ALU.mult, ALU.add,
                    r=[ps, modT] + xk(o), w=[xTk[o]])
            gfree_(hT + actT)
            if j == 0: gfree_(hhs)
            if j == 1 and l + 1 < depth: emit_adaln(l + 1)
        if l + 1 < depth: CV.finish()
        PSRh[0] = PSR4

    fa = A(fng)
    for j in range(4):
        mc = 15 + 512 * j
        ps = PSR.get()
        for k in range(8):
            sq = SQ.get()
            act(A(sq), xa[:, k, mc:mc + 512], AF.Square, r=xk(k), w=[sq])
            mm(A(ps), A(onesb), A(sq), k == 0, k == 7, r=[onesb, sq], w=[ps])
        R = D0
        rsq(A(R)[:, 0:512], A(ps), 1.0 / D, EPS, [ps], [R])
        for k in range(8):
            t = FT.get()
            stt('dve', A(t)[:, 0:512], xa[:, k, mc:mc + 512], fa[:, k:k + 1], A(R)[:, 0:512], ALU.mult, ALU.mult, r=xk(k) + [R, fng], w=[t])
            dma('sp', out_d[:, k, j * 512:(j + 1) * 512], A(t)[:, 0:512], r=[t], w=[])

    P.emit(nc, es)
    es.close()
    return nc


def _fm(v, nch):
    return np.ascontiguousarray(np.asarray(v, np.float32).reshape(nch, 128).T)


def _consts():
    c = np.zeros((128, NCST), np.float32)
    j = np.arange(128, dtype=np.float32)[:, None]; i = np.arange(128, dtype=np.float32)[None, :]
    c[:, 0:128] = np.maximum(i - j, 0); c[:, 128:256] = np.maximum(j - i, 0); c[:, 256:384] = np.eye(128)
    c[:, 384:512] = i + 1; c[:, 512:640] = 128 - i
    c[:, 640] = 127 - j[:, 0]; c[:, 641] = j[:, 0]
    c[:, 642:658] = 128.0 * (15 - np.arange(16))[None, :]
    rt = np.zeros((128, 128), np.float32)
    for g in range(4):
        for t in range(16):
            a = g * 32 + t
            rt[a + 16, a] = -1.0
            rt[a, a + 16] = 1.0
    c[:, 658:786] = rt
    sw = np.zeros((128, 128), np.float32)
    for k in range(128): sw[k, (k + 64) % 128] = 1.0
    c[:, 786:914] = sw
    return c


def _rope(start):
    tab = np.zeros((128, 2, 5, 512), np.float32)
    tab[:, 0, 4, :] = 1.0
    p = np.arange(128); d = p % 64; f = (d % 16).astype(np.float32)
    inv = (np.float32(10000.0) ** (-f / np.float32(16.0))).astype(np.float32)
    for j in range(4):
        t = start + j * 512 + np.arange(512)
        row = (t // 64).astype(np.float32); col = (t % 64).astype(np.float32)
        pos = np.where((d < 32)[:, None], row[None, :], col[None, :]).astype(np.float32)
        ang = (pos * inv[:, None]).astype(np.float32)
        tab[:, 0, j, :] = np.cos(ang); tab[:, 1, j, :] = np.sin(ang)
    return tab


def _prep(inputs, depth):
    f = lambda k: np.asarray(inputs[k], np.float32)
    x = f("x"); c = f("c"); ctx = f("ctx"); c_ctx = f("c_ctx")
    pcol = np.zeros((depth, 128, NPC), np.float32)
    brow = np.zeros((depth, 1, NBROW), np.float32)
    p = np.arange(128)
    for l in range(depth):
        pc = pcol[l]
        pc[:, 0:48] = _fm(f("b_ada")[l], 48)
        pc[:, 48:56] = _fm(f("norm1_g")[l], 8); pc[:, 56:64] = _fm(f("norm2_g")[l], 8)
        b = f("b_in")[l].copy()
        b[1024:1536] = b[1024:1536].reshape(2, 4, 64).transpose(1, 0, 2).reshape(512)
        pc[:, 64:118] = _fm(b, 54)
        pc[:, 118:242] = f("conv_dw_w")[l].T.reshape(4, 128, 31).transpose(1, 0, 2).reshape(128, 124)
        pc[:, 242:246] = _fm(f("conv_dw_b")[l], 4); pc[:, 246:250] = _fm(f("conv_ln_g")[l], 4)
        pc[:, 250:254] = _fm(f("conv_ln_b")[l], 4)
        pc[:, 254] = f("q_norm_g")[l][p % 64]; pc[:, 255] = f("k_norm_g")[l][p % 64]
        pc[:, 256:322] = f("ffn_dw_w")[l].T.reshape(22, 128, 3).transpose(1, 0, 2).reshape(128, 66)
        pc[:, 322:344] = _fm(f("ffn_dw_b")[l], 22)
        lg = f("ret_decay_logit")[l]
        for d_ in range(2):
            for cc in range(4):
                pc[:, 344 + d_ * 4 + cc] = lg[d_, 2 * cc + p // 64]
            pc[:, 352 + d_ * 8:360 + d_ * 8] = lg[d_][None, :]
        pc[:, 368:372] = _fm(f("ret_gn_g")[l], 4)
        bi = f("b_in")[l]
        brow[l, 0] = np.concatenate([bi[1664:1792], bi[2304:2816], bi[2816:3328], bi[3328:3840]])
    cst = _consts()
    fng = _fm(f("final_norm_g"), 8)
    shared = dict(cst=cst, pcol=pcol, brow=brow, fng=fng)
    for k in ("w_ada", "w_in", "w_pa", "w_pb", "w_pc", "w_out", "w_up", "w_down"):
        shared[k] = np.ascontiguousarray(f(k)[:depth])
    maps = []
    for r in range(8):
        b_ = r // 4; q = r % 4; start = q * NLAT
        xe = np.zeros((XW, D), np.float32)
        xe[15:15 + NLAT] = x[b_, start:start + NLAT]
        xe[CTX0:CTX0 + CTX] = ctx[b_]
        xT = np.ascontiguousarray(xe.T.reshape(8, 128, XW).transpose(1, 0, 2))
        cv = np.stack([c[b_], c_ctx], 0)
        cT = np.ascontiguousarray(cv.T.reshape(8, 128, 2).transpose(1, 0, 2))
        rk = np.zeros((128, 32), np.float32)
        for q2 in range(4):
            rk[:, q2] = 1.0 if q2 == q - 1 else 0.0
            rk[:, 4 + q2] = 1.0 if q2 == q + 1 else 0.0
            rk[:, 10 + q2] = 2048.0 * max(q - 1 - q2, 0); rk[:, 14 + q2] = 1.0 if q2 < q else 0.0
            rk[:, 19 + q2] = 2048.0 * max(q2 - q - 1, 0); rk[:, 23 + q2] = 1.0 if q2 > q else 0.0
        rk[:, 8] = 1.0 if q > 0 else 0.0; rk[:, 9] = 1.0 if q < 3 else 0.0
        rk[:, 18] = 2048.0 * q; rk[:, 27] = 2048.0 * (3 - q)
        m = dict(shared); m.update(xT=xT, cT=cT, rope=_rope(start), rkt=rk)
        maps.append(m)
    return maps


_NC = {}


def kernel(**inputs):
    depth = int(inputs.pop("_depth", DEPTH))
    if depth not in _NC:
        _NC[depth] = build(depth)
    maps = _prep(inputs, depth)
    res = run_bass_kernel_spmd(_NC[depth], maps, core_ids=list(range(8)))
    out = np.zeros((2, SEQ, D), np.float32)
    for r in range(8):
        o = np.asarray(res.results[r]["out"], np.float32)
        out[r // 4, (r % 4) * NLAT:(r % 4 + 1) * NLAT] = o.transpose(2, 1, 0).reshape(NLAT, D)
    return out
```

```python
import numpy as np
STOP = 9.0
from contextlib import ExitStack
import concourse.bass as bass
import concourse.mybir as mybir
from concourse.bass_utils import run_bass_kernel_spmd

F32 = mybir.dt.float32
BF16 = mybir.dt.bfloat16
ALU = mybir.AluOpType
AF = mybir.ActivationFunctionType
AX = mybir.AxisListType

D = 1024; DEPTH = 4; SEQ = 8192; CTX = 256; NLAT = 2048
XW = 2364; CTX0 = 2093
EPS = 1e-6
NPC = 372; NBROW = 1664; NCST = 914


class T:
    def __init__(self, ap):
        self.ap = ap; self.lw = {}; self.rd = {}


class Prog:
    NDS = 10

    def __init__(self):
        self.ops = []
        self.dma_rr = {}

    def op(self, eng, fn, r=(), w=(), dma=False, cc=False):
        idx = len(self.ops)
        if cc:
            cls = ('cc',)
        elif dma:
            k = self.dma_rr.get(eng, 0); self.dma_rr[eng] = k + 1
            cls = ('dma', eng, k % self.NDS)
        else:
            cls = (eng,)
        deps = {}

        def add(c, i):
            if i is not None and deps.get(c, -1) < i:
                deps[c] = i
        for b in r:
            for c, i in b.lw.items(): add(c, i)
        for b in w:
            for c, i in b.lw.items(): add(c, i)
            for c, i in b.rd.items(): add(c, i)
        for b in r: b.rd[cls] = idx
        for b in w: b.lw[cls] = idx; b.rd = {}
        self.ops.append(dict(eng=eng, fn=fn, cls=cls, deps=deps, dma=dma, cc=cc))
        return idx

    def emit(self, nc, es):
        ops = self.ops
        last_in_cls = {}
        for i, o in enumerate(ops):
            if o['dma'] or o['cc']:
                p = last_in_cls.get(o['cls'])
                if p is not None: o['deps'][o['cls']] = max(o['deps'].get(o['cls'], -1), p)
                last_in_cls[o['cls']] = i
        needed = set()
        for o in ops:
            for c, i in o['deps'].items():
                if c == ('pe',) and o['eng'] == 'pe' and not o['dma']:
                    continue
                needed.add(i)
        sems = {}
        classes = sorted({o['cls'] for o in ops}, key=str)
        for c in classes:
            sems[c] = es.enter_context(nc.semaphore("s_" + "_".join(str(x) for x in c)))
        cnt = {c: 0 for c in classes}
        tok = {}
        for i, o in enumerate(ops):
            if i in needed or o['dma'] or o['cc']:
                inc = 16 if o['dma'] else 1
                cnt[o['cls']] += inc
                tok[i] = cnt[o['cls']]
                o['inc'] = inc
            else:
                o['inc'] = 0
        engs = ['pe', 'act', 'dve', 'pool', 'sp']
        streams = {e: [o for o in ops if o['eng'] == e] for e in engs}
        idx_of = {id(o): i for i, o in enumerate(ops)}
        block = es.enter_context(nc.Block())

        def run(ename, e):
            waited = {}
            for o in streams[ename]:
                for c, i in sorted(o['deps'].items(), key=lambda kv: str(kv[0])):
                    if c == ('pe',) and ename == 'pe' and not o['dma']:
                        continue
                    v = tok[i]
                    if waited.get(c, 0) < v:
                        e.wait_ge(sems[c], v); waited[c] = v
                ins = o['fn'](e)
                if o['inc']:
                    ins.then_inc(sems[o['cls']], o['inc'])
            if ename == 'sp':
                for c in classes:
                    if cnt[c] > 0 and (c[0] == 'dma'):
                        e.wait_ge(sems[c], cnt[c])

        @block.tensor
        def _(e): run('pe', e)

        @block.scalar
        def _(e): run('act', e)

        @block.vector
        def _(e): run('dve', e)

        @block.gpsimd
        def _(e): run('pool', e)

        @block.sync
        def _(e): run('sp', e)


class Rot:
    def __init__(self, items): self.items = items; self.i = 0

    def get(self):
        b = self.items[self.i % len(self.items)]; self.i += 1
        return b


def build(depth=DEPTH):
    nc = bass.Bass("TRN2", target_bir_lowering=False)
    P = Prog()
    es = ExitStack()

    def din(name, shape, dt=F32):
        return nc.dram_tensor(name, shape, dt, kind="ExternalInput").ap()

    def dint(name, shape, dt):
        return nc.dram_tensor(name, shape, dt, kind="Internal").ap()

    xT_in = din("xT", [128, 8, XW])
    cT_in = din("cT", [128, 8, 2])
    rope_in = din("rope", [128, 2, 5, 512])
    rkt_in = din("rkt", [128, 32])
    cst_in = din("cst", [128, NCST])
    pcol_in = din("pcol", [depth, 128, NPC])
    brow_in = din("brow", [depth, 1, NBROW])
    fng_in = din("fng", [128, 8])
    w_ada = din("w_ada", [depth, D, 6144]); w_in = din("w_in", [depth, D, 6912])
    w_pa = din("w_pa", [depth, 512, D]); w_pb = din("w_pb", [depth, 512, D]); w_pc = din("w_pc", [depth, 512, D])
    w_out = din("w_out", [depth, D, D]); w_up = din("w_up", [depth, D, 5632]); w_down = din("w_down", [depth, 2816, D])
    out_d = nc.dram_tensor("out", [128, 8, NLAT], F32, kind="ExternalOutput").ap()
    DBG = False
    if DBG:
        dbg16 = nc.dram_tensor("dbg16", [40, 128, 512], BF16, kind="ExternalOutput").ap()
        dbg32 = nc.dram_tensor("dbg32", [12, 128, 512], F32, kind="ExternalOutput").ap()

    def dump16(i, t):
        if DBG: P.op('sp', lambda e: e.dma_start(out=dbg16[i], in_=t.ap), r=[t], dma=True)

    def dump32(i, ap, trs, n=512):
        if DBG: P.op('sp', lambda e: e.dma_start(out=dbg32[i][:, 0:n], in_=ap), r=trs, dma=True)

    wada16 = dint("wada16", [depth, D, 6144], BF16); win16 = dint("win16", [depth, D, 6912], BF16)
    wpa16 = dint("wpa16", [depth, 512, D], BF16); wpb16 = dint("wpb16", [depth, 512, D], BF16)
    wpc16 = dint("wpc16", [depth, 512, D], BF16); wout16 = dint("wout16", [depth, D, D], BF16)
    wup16 = dint("wup16", [depth, D, 5632], BF16); wdn16 = dint("wdn16", [depth, 2816, D], BF16)
    EXC = 2048 + 16 * 128
    ex1_in = [dint(f"ex1i{i}", [128, EXC], BF16) for i in range(2)]
    ex1_out = [dint(f"ex1o{i}", [512, EXC], BF16) for i in range(2)]
    cxkv = dint("cxkv", [128, 256 + 2 * 128], BF16)
    ex2_in = [dint(f"ex2i{i}", [128, 1024], F32) for i in range(2)]
    ex2_out = [dint(f"ex2o{i}", [512, 1024], F32) for i in range(2)]
    ex3_in = [dint(f"ex3i{i}", [128, 240], F32) for i in range(2)]
    ex3_out = [dint(f"ex3o{i}", [512, 240], F32) for i in range(2)]
    T_ex1i = [T(a) for a in ex1_in]; T_ex1o = [T(a) for a in ex1_out]; T_cx = T(cxkv)
    T_ex2i = [T(a) for a in ex2_in]; T_ex2o = [T(a) for a in ex2_out]
    T_ex3i = [T(a) for a in ex3_in]; T_ex3o = [T(a) for a in ex3_out]
    Tw = {}

    def tw(name, l):
        if (name, l) not in Tw: Tw[(name, l)] = T(None)
        return Tw[(name, l)]
    RG = [[0, 1, 2, 3], [4, 5, 6, 7]]

    def sbt(name, shape, dt):
        t = es.enter_context(nc.sbuf_tensor("sb_" + name, shape, dt))
        return T(t[:])

    xT = sbt("xT", [128, 8, XW], F32)
    xTk = [T(None) for _ in range(8)]
    xTh = T(None)
    cst = sbt("cst", [128, NCST], F32)
    rkt = sbt("rkt", [128, 32], F32)
    pc = sbt("pc", [128, NPC], F32)
    brow = sbt("brow", [1, NBROW], BF16)
    fng = sbt("fng", [128, 8], F32)
    modTs = [sbt(f"modT{i}", [128, 48, 2], F32) for i in range(2)]
    badas = [sbt(f"bada{i}", [128, 48], F32) for i in range(2)]
    modT = modTs[0]
    mder = sbt("mder", [128, 4, 8, 2], F32)
    scT = sbt("scT", [128, 8, 2], BF16)
    cTs = sbt("cTs", [128, 8, 2], F32)
    onesb = sbt("onesb", [128, 128], BF16)
    blk64 = sbt("blk64", [128, 128], BF16)
    ident = sbt("ident", [128, 128], BF16)
    rtb = sbt("rtb", [128, 128], BF16)
    onesf = sbt("onesf", [128, 128], F32)
    lgt = sbt("lgt", [128, 24], F32)
    DTt = sbt("DTt", [128, 8, 128], BF16)
    RKL = sbt("RKL", [128, 512], BF16); RKH = sbt("RKH", [128, 512], BF16)
    BMASK = sbt("BMASK", [128, 512], BF16)
    QFt = sbt("QFt", [128, 2, 4, 128], F32)
    KFt = sbt("KFt", [128, 2, 8], F32)
    CDt = sbt("CDt", [128, 2, 4], F32)
    DECt = sbt("DECt", [128, 2, 16, 4], F32)
    coef = sbt("coef", [128, 10, 4], F32)
    smallf = sbt("smallf", [128, 64], F32)
    Wt = [sbt(f"W{i}", [128, 4096], BF16) for i in range(2)]
    Wrot = Rot(Wt)
    GT = 4
    KTG0 = Rot([sbt(f"KTG0{i}", [128, GT * 128], BF16) for i in range(2)])
    KTG1 = Rot([sbt(f"KTG1{i}", [128, GT * 128], BF16) for i in range(2)])
    VG0 = Rot([sbt(f"VG0{i}", [128, GT, 128], BF16) for i in range(2)])
    VG1 = Rot([sbt(f"VG1{i}", [128, GT, 128], BF16) for i in range(2)])
    SBL = Rot([sbt(f"SBL{i}", [128, 512], BF16) for i in range(2)])
    sbscr = dint("sbscr", [18, 128, 512], BF16)
    T_sbs = [T(None) for _ in range(18)]
    NG = 39
    Gall = [sbt(f"G{i}", [128, 512], BF16) for i in range(NG)]
    gfree = list(Gall)

    def galloc(n):
        r = gfree[:n]; del gfree[:n]
        assert len(r) == n, "G pool exhausted"
        return r

    def gfree_(lst): gfree.extend(lst)
    FT = Rot([sbt(f"FT{i}", [128, 514], F32) for i in range(4)])
    STG = [sbt(f"STG{i}", [128, 512], BF16) for i in range(3)]
    D0 = sbt("D0", [128, 512], F32); D1 = sbt("D1", [128, 512], F32); D2 = sbt("D2", [128, 512], F32)
    ACC = [sbt(f"ACC{i}", [128, 512], F32) for i in range(4)]
    SQ = Rot([sbt(f"SQ{i}", [128, 512], BF16) for i in range(3)])
    Sf = sbt("Sf", [128, 512], F32); Sf16 = sbt("Sf16", [128, 512], BF16)
    Sbk = sbt("Sbk", [128, 512], F32); Tfk = sbt("Tfk", [128, 512], F32)
    Ssf = sbt("Ssf", [128, 512], F32); Ssb = sbt("Ssb", [128, 512], F32)
    DG = Rot([sbt(f"DG{i}", [128, 128], BF16) for i in range(4)])
    uh = sbt("uh", [128, 4, 150], BF16)
    gth = sbt("gth", [128, 22, 10], F32)
    gthf = [T(None) for _ in range(22)]
    ubuf = [sbt(f"ubuf{i}", [128, 542], BF16) for i in range(4)]
    pst = [T(es.enter_context(nc.psum_tensor(f"ps{i}", [128, 512], F32))[:]) for i in range(8)]
    PSL = pst[0:4]
    PSR4 = Rot(pst[4:8]); PSR8 = Rot(pst[0:8])
    PSRh = [PSR4]

    class _PSR:
        def get(self): return PSRh[0].get()
    PSR = _PSR()

    def A(t): return t.ap

    def dma(eng, out_ap, in_ap, r=(), w=()):
        return P.op(eng, lambda e: e.dma_start(out=out_ap, in_=in_ap), r=r, w=w, dma=True)

    def mm(ps_ap, lhsT, rhs, start, stop, r, w):
        return P.op('pe', lambda e: e.matmul(ps_ap, lhsT, rhs, start=start, stop=stop), r=r, w=w)

    def act(out, in_, func, r, w, bias=None, scale=None, eng='act'):
        kw = {}
        if bias is not None: kw['bias'] = bias
        if scale is not None: kw['scale'] = scale
        return P.op('act', lambda e: e.activation(out, in_, func, **kw), r=r, w=w)

    def tt(eng, out, a, b, op, r, w):
        if eng == 'pool': eng = 'dve'
        return P.op(eng, lambda e: e.tensor_tensor(out, a, b, op), r=r, w=w)

    def rsq(out, in_, scale, eps, r, w):
        P.op('dve', lambda e: e.tensor_scalar(out, in_, scale, eps, ALU.mult, ALU.add), r=r, w=w)
        P.op('act', lambda e: e.activation(out, out, AF.Ln), r=w, w=w)
        P.op('act', lambda e: e.activation(out, out, AF.Exp, scale=-0.5), r=w, w=w)

    def ts(eng, out, a, s1, s2, op0, op1, r, w):
        eng = 'dve'
        if s2 is None:
            return P.op(eng, lambda e: e.tensor_scalar(out, a, s1, None, op0), r=r, w=w)
        return P.op(eng, lambda e: e.tensor_scalar(out, a, s1, s2, op0, op1), r=r, w=w)

    def stt(eng, out, a, s, b, op0, op1, r, w):
        eng = 'dve'
        return P.op(eng, lambda e: e.scalar_tensor_tensor(out, a, s, b, op0, op1), r=r, w=w)

    def cp(eng, out, in_, r, w):
        if eng == 'pool': eng = 'dve'
        if eng == 'act':
            return P.op('act', lambda e: e.activation(out, in_, AF.Copy), r=r, w=w)
        return P.op(eng, lambda e: e.tensor_copy(out, in_), r=r, w=w)

    def mset(eng, ap, val, w):
        return P.op(eng, lambda e: e.memset(ap, val), w=w)

    xa = A(xT)

    def xk(k): return [xT, xTk[k]]

    dma('sp', xa, xT_in, w=[xT] + xTk + [xTh])
    dma('sp', A(cst), cst_in, w=[cst])
    dma('sp', A(rkt), rkt_in, w=[rkt])
    dma('sp', A(fng), fng_in, w=[fng])
    dma('sp', A(cTs), cT_in, w=[cTs])
    for l in range(depth if (STOP > 0 and 0) else 0):
        def cast(nm, dst, src, rows, rstep):
            for r0 in range(0, rows, rstep):
                dma('pool', dst[r0:r0 + rstep, :], src[r0:r0 + rstep, :], w=[tw(nm, l)])
        cast('ada', wada16[l], w_ada[l], D, 256)
        for r0 in range(0, D, 256):
            rs = slice(r0, r0 + 256)
            dma('pool', win16[l][rs, 0:1024], w_in[l][rs, 0:1024], w=[tw('in', l)])
            for half in range(2):
                dma('pool', win16[l][rs, 1024:1536].rearrange("k (c h d) -> k h c d", c=4, h=2)[:, half],
                    w_in[l][rs, 1024:1536].rearrange("k (h c d) -> k h c d", h=2, c=4)[:, half], w=[tw('in', l)])
            dma('pool', win16[l][rs, 1536:6912], w_in[l][rs, 1536:6912], w=[tw('in', l)])
        cast('pa', wpa16[l], w_pa[l], 512, 512)
        for half in range(2):
            dma('pool', wpb16[l].rearrange("(c h d) n -> h c d n", c=4, h=2)[half],
                w_pb[l].rearrange("(h c d) n -> h c d n", h=2, c=4)[half], w=[tw('pb', l)])
        cast('pc', wpc16[l], w_pc[l], 512, 512)
        cast('out', wout16[l], w_out[l], D, 512)
        cast('up', wup16[l], w_up[l], D, 256)
        cast('dn', wdn16[l], w_down[l], 2816, 704)

    c_ = A(cst)
    RELA = c_[:, 0:128]; RELB = c_[:, 128:256]; EYE = c_[:, 256:384]
    IOTA1 = c_[:, 384:512]; IOTAC = c_[:, 512:640]; REV = c_[:, 640:641]; JCOL = c_[:, 641:642]
    EXPO = c_[:, 642:658]; RTF = c_[:, 658:786]; SWAP = c_[:, 786:914]
    rk_ = A(rkt)
    mset('dve', A(onesb), 1.0, w=[onesb]); mset('dve', A(onesf), 1.0, w=[onesf])
    mset('dve', A(blk64), 0.0, w=[blk64])
    mset('dve', A(blk64)[0:64, 0:64], 1.0, w=[blk64]); mset('dve', A(blk64)[64:128, 64:128], 1.0, w=[blk64])
    cp('dve', A(ident), EYE, r=[cst], w=[ident]); cp('dve', A(rtb), RTF, r=[cst], w=[rtb])
    for t in KTG0.items + KTG1.items + [RKL, RKH, BMASK]: mset('dve', A(t), 0.0, w=[t])
    for c in range(4):
        mset('dve', A(BMASK)[0:64, c * 128:c * 128 + 64], 1.0, w=[BMASK])
        mset('dve', A(BMASK)[64:128, c * 128 + 64:c * 128 + 128], 1.0, w=[BMASK])
    for t in VG0.items: mset('dve', A(t), 1.0, w=[t])
    for t in VG1.items: mset('dve', A(t), 1.0, w=[t])
    act(A(scT), A(cTs), AF.Silu, r=[cTs], w=[scT])

    BLK = [dict(ws=512 * j, mc=15 + 512 * j, N=512, lat=True, j=j) for j in range(4)]
    BLK.append(dict(ws=2078, mc=CTX0, N=256, lat=False, j=4))

    WE = ['pool']

    def wload(src_ap, kc, n, wtr, eng=None):
        eng = eng or WE[0]
        wt = Wrot.get()
        dst = A(wt)[:, 0:kc * n].rearrange("p (k n) -> p k n", k=kc)
        dma(eng, dst, src_ap.rearrange("(k p) n -> p k n", p=128), r=[wtr], w=[wt])
        return wt, dst

    def normrope(ps_in, bias_col, gcol, gscale, N, out_ap, out_tr):
        xb = FT.get()
        act(A(xb)[:, :N], A(ps_in)[:, :N], AF.Identity, r=[ps_in, pc], w=[xb], bias=bias_col)
        sq = SQ.get()
        act(A(sq)[:, :N], A(xb)[:, :N], AF.Square, r=[xb], w=[sq])
        ps2 = PSR.get()
        mm(A(ps2)[:, :N], A(blk64), A(sq)[:, :N], True, True, r=[blk64, sq], w=[ps2])
        Rr = FT.get()
        rsq(A(Rr)[:, :N], A(ps2)[:, :N], 1.0, 64 * EPS, [ps2], [Rr])
        ts('dve', A(xb)[:, :N], A(xb)[:, :N], gcol, gscale, ALU.mult, ALU.mult, r=[xb, pc], w=[xb])
        xg = SQ.get()
        cp('act', A(xg)[:, :N], A(xb)[:, :N], r=[xb], w=[xg])
        ps3 = PSR.get()
        mm(A(ps3)[:, :N], A(rtb), A(xg)[:, :N], True, True, r=[rtb, xg], w=[ps3])
        t1 = FT.get(); t2 = FT.get()
        tt('pool', A(t1)[:, :N], A(xb)[:, :N], A(D1)[:, :N], ALU.mult, r=[xb, D1], w=[t1])
        tt('dve', A(t2)[:, :N], A(ps3)[:, :N], A(D2)[:, :N], ALU.mult, r=[ps3, D2], w=[t2])
        tt('dve', A(t1)[:, :N], A(t1)[:, :N], A(t2)[:, :N], ALU.add, r=[t1, t2], w=[t1])
        tt('dve', out_ap, A(t1)[:, :N], A(Rr)[:, :N], ALU.mult, r=[t1, Rr], w=out_tr)

    def make_h(xsrc, rdeps, N, Gap, SHap, outs):
        ps = PSR.get()
        for k in range(8):
            sq = SQ.get()
            act(A(sq)[:, :N], xsrc[k], AF.Square, r=rdeps[k], w=[sq])
            mm(A(ps)[:, :N], A(onesb), A(sq)[:, :N], k == 0, k == 7, r=[onesb, sq], w=[ps])
        R = D0
        rsq(A(R)[:, :N], A(ps)[:, :N], 1.0, D * EPS, [ps], [R])
        for k in range(8):
            t = FT.get()
            stt('dve', A(t)[:, :N], xsrc[k], Gap(k), A(R)[:, :N], ALU.mult, ALU.mult, r=rdeps[k] + [R, mder], w=[t])
            oap, otr = outs[k]
            act(oap, A(t)[:, :N], AF.Identity, r=[t, MTH[0]], w=otr, bias=SHap(k))

    def exch_xhalo(par):
        exch_issue(par); exch_finish(par)

    def exch_issue(par):
        o = ex3_in[par].rearrange("p (k s) -> p k s", k=8)
        dma('sp', o[:, :, 0:15], xa[:, :, 15:30], r=[xT] + xTk, w=[T_ex3i[par]])
        dma('sp', o[:, :, 15:30], xa[:, :, 15 + 2033:15 + 2048], r=[xT] + xTk, w=[T_ex3i[par]])
        P.op('pool', lambda e: e.collective_compute("AllGather", ALU.bypass, replica_groups=RG,
                                                    ins=[ex3_in[par]], outs=[ex3_out[par]]),
             r=[T_ex3i[par]], w=[T_ex3o[par]], cc=True)

    def exch_finish(par):
        L = xa[:, :, 0:15]; Rr = xa[:, :, 2063:2078]
        for q in range(4):
            xq = FT.get()
            dma('sp', A(xq)[:, 0:240], ex3_out[par][q * 128:(q + 1) * 128, :], r=[T_ex3o[par]], w=[xq])
            xr = A(xq)[:, 0:240].rearrange("p (k s) -> p k s", k=8)
            if q == 0:
                ts('dve', L, xr[:, :, 15:30], rk_[:, 0:1], None, ALU.mult, None, r=[xq, rkt, xTh], w=[xTh])
                ts('dve', Rr, xr[:, :, 0:15], rk_[:, 4:5], None, ALU.mult, None, r=[xq, rkt, xTh], w=[xTh])
            else:
                stt('dve', L, xr[:, :, 15:30], rk_[:, q:q + 1], L, ALU.mult, ALU.add, r=[xq, rkt, xTh], w=[xTh])
                stt('dve', Rr, xr[:, :, 0:15], rk_[:, 4 + q:5 + q], Rr, ALU.mult, ALU.add, r=[xq, rkt, xTh], w=[xTh])

    pca = A(pc)

    def bcol(j): return pca[:, 64 + j:65 + j]

    md = A(mder)
    mo = A(modT)
    MTH = [modT]

    def cvt_items(l):
        items = []

        def plain(nm, dst, src, rows, cols, c_lo=0, c_hi=None):
            c_hi = cols if c_hi is None else c_hi
            for k in range(rows // 128):
                for c0 in range(c_lo, c_hi, 512):
                    n = min(512, c_hi - c0)
                    rs = slice(k * 128, (k + 1) * 128)
                    items.append(([(lambda sl, n=n: sl[:, 0:n], src[rs, c0:c0 + n])], dst[rs, c0:c0 + n], n, tw(nm, l)))
        plain('in', win16[l], w_in[l], D, 6912, 0, 1024)
        for k in range(8):
            rs = slice(k * 128, (k + 1) * 128)
            lds = []
            for c in range(4):
                for hf in range(2):
                    h_ = hf * 4 + c
                    lds.append((lambda sl, o=c * 128 + hf * 64: sl[:, o:o + 64], w_in[l][rs, 1024 + h_ * 64:1024 + (h_ + 1) * 64]))
            items.append((lds, win16[l][rs, 1024:1536], 512, tw('in', l)))
        plain('in', win16[l], w_in[l], D, 6912, 1536, 6912)
        plain('pa', wpa16[l], w_pa[l], 512, D)
        for c in range(4):
            for c0 in (0, 512):
                lds = []
                for hf in range(2):
                    h_ = hf * 4 + c
                    lds.append((lambda sl, hf=hf: sl[hf * 64:(hf + 1) * 64, 0:512], w_pb[l][h_ * 64:(h_ + 1) * 64, c0:c0 + 512]))
                items.append((lds, wpb16[l][c * 128:(c + 1) * 128, c0:c0 + 512], 512, tw('pb', l)))
        plain('pc', wpc16[l], w_pc[l], 512, D)
        plain('out', wout16[l], w_out[l], D, D)
        plain('up', wup16[l], w_up[l], D, 5632)
        plain('dn', wdn16[l], w_down[l], 2816, D)
        return items

    class Cvt:
        def __init__(self): self.q = []; self.pending = []; self.stg = None; self.i = 0

        def start(self, l, stg=None):
            self.q = cvt_items(l); self.stg = stg or STG; self.i = 0; self.pending = []; self.own = stg

        def step(self, nitems):
            for _ in range(nitems):
                if not self.q: break
                lds, dst, n, trk = self.q.pop(0)
                slot = self.stg[self.i % len(self.stg)]; self.i += 1
                for fn, src in lds:
                    dma('pool', fn(A(slot)), src, w=[slot])
                self.pending.append((slot, dst, n, trk))
                if len(self.pending) > len(self.stg) - 1: self.flush(len(self.stg) - 1)

        def flush(self, keep):
            while len(self.pending) > keep:
                slot, dst, n, trk = self.pending.pop(0)
                dma('pool', dst, A(slot)[:, 0:n], r=[slot], w=[trk])

        def finish(self):
            self.step(10 ** 9); self.flush(0)
            if self.own: gfree_(self.own); self.own = None
    CV = Cvt()

    for l in range(depth):
        par = l % 2
        WIN, WADA, WPA, WPB, WPC, WOUT, WUP, WDN = (win16[l], wada16[l], wpa16[l], wpb16[l], wpc16[l], wout16[l], wup16[l], wdn16[l])
        WENG = 'sp'
        WE[0] = WENG
        exch_issue(par)
        if l == 0:
            CV.start(0)
        elif l + 1 < depth:
            CV.start(l + 1)
            n_cv = (len(CV.q) + 9) // 10
        dma('sp', pca, pcol_in[l], w=[pc])
        dma('pool', A(brow), brow_in[l], w=[brow])
        def emit_adaln(ll):
            mt_ = modTs[ll % 2]; ba_ = badas[ll % 2]
            dma('sp', A(ba_), pcol_in[ll][:, 0:48], w=[ba_])
            for pcs in range(12):
                wt, wv = wload(w_ada[ll][:, pcs * 512:(pcs + 1) * 512], 8, 512, tw('ada_unused', ll), eng='pool')
                for jj in range(4):
                    j_ = pcs * 4 + jj
                    ps = PSR.get()
                    for k in range(8):
                        mm(A(ps)[:, 0:2], wv[:, k, jj * 128:(jj + 1) * 128], A(scT)[:, k, :], k == 0, k == 7,
                           r=[wt, scT], w=[ps])
                    act(A(mt_)[:, j_, :], A(ps)[:, 0:2], AF.Identity, r=[ps, ba_], w=[mt_], bias=A(ba_)[:, j_:j_ + 1])
        if l == 0: emit_adaln(0)
        modT = modTs[l % 2]; mo = A(modT); MTH[0] = modT
        for (gi, ncol, sccol) in ((0, 48, 8), (1, 56, 32)):
            ts('dve', md[:, gi], mo[:, sccol:sccol + 8, :], 1.0, 32.0, ALU.add, ALU.mult, r=[modT], w=[mder])
            tt('dve', md[:, gi], md[:, gi], pca[:, ncol:ncol + 8].unsqueeze(2).to_broadcast([128, 8, 2]), ALU.mult,
               r=[mder, pc], w=[mder])
        lg = A(lgt)
        act(lg, pca[:, 344:368], AF.Exp, r=[pc], w=[lgt], scale=-1.0)
        act(lg, lg, AF.Ln, r=[lgt], w=[lgt], bias=1.0)
        ts('dve', lg, lg, -1.0, None, ALU.mult, None, r=[lgt], w=[lgt])
        sf = A(smallf)
        for h in range(8):
            t1 = FT.get()
            ts('dve', A(t1)[:, 0:128], RELA, lg[:, 8 + h:9 + h], None, ALU.mult, None, r=[cst, lgt], w=[t1])
            stt('dve', A(t1)[:, 0:128], RELB, lg[:, 16 + h:17 + h], A(t1)[:, 0:128], ALU.mult, ALU.add, r=[cst, lgt, t1], w=[t1])
            act(A(t1)[:, 0:128], A(t1)[:, 0:128], AF.Exp, r=[t1], w=[t1])
            tt('dve', A(DTt)[:, h, :], A(t1)[:, 0:128], EYE, ALU.add, r=[t1, cst], w=[DTt])
        for d_ in range(2):
            for c in range(4):
                act(A(QFt)[:, d_, c, :], IOTA1 if d_ == 0 else IOTAC, AF.Exp, r=[cst, lgt], w=[QFt],
                    scale=lg[:, d_ * 4 + c:d_ * 4 + c + 1])
            ts('dve', A(KFt)[:, d_, :], lg[:, 8 + 8 * d_:16 + 8 * d_], REV if d_ == 0 else JCOL, None, ALU.mult, None,
               r=[lgt, cst], w=[KFt])
            act(A(KFt)[:, d_, :], A(KFt)[:, d_, :], AF.Exp, r=[KFt], w=[KFt])
            act(A(CDt)[:, d_, :], lg[:, d_ * 4:d_ * 4 + 4], AF.Exp, r=[lgt], w=[CDt], scale=128.0)
            tt('dve', A(DECt)[:, d_], lg[:, d_ * 4:d_ * 4 + 4].unsqueeze(1).to_broadcast([128, 16, 4]),
               EXPO.unsqueeze(2).to_broadcast([128, 16, 4]), ALU.mult, r=[lgt, cst], w=[DECt])
            act(A(DECt)[:, d_], A(DECt)[:, d_], AF.Exp, r=[DECt], w=[DECt])
            eo = 10 if d_ == 0 else 19
            for q in range(5):
                ecol = rk_[:, eo + q:eo + q + 1] if q < 4 else rk_[:, eo + 8:eo + 9]
                ci = d_ * 5 + q
                ts('dve', A(coef)[:, ci, :], lg[:, d_ * 4:d_ * 4 + 4], ecol, None, ALU.mult, None, r=[lgt, rkt], w=[coef])
                act(A(coef)[:, ci, :], A(coef)[:, ci, :], AF.Exp, r=[coef], w=[coef])
                if q < 4:
                    ts('dve', A(coef)[:, ci, :], A(coef)[:, ci, :], rk_[:, eo + 4 + q:eo + 5 + q], None, ALU.mult, None,
                       r=[coef, rkt], w=[coef])

        G1 = lambda t: (lambda k: md[:, 0, k, t:t + 1])
        SH1 = lambda t: (lambda k: mo[:, k, t:t + 1])
        G2 = lambda t: (lambda k: md[:, 1, k, t:t + 1])
        SH2 = lambda t: (lambda k: mo[:, 24 + k, t:t + 1])

        if STOP <= 1: break
        if l == 0:
            CV.step(112); CV.flush(0)

        if STOP <= 2: break
        PSRh[0] = PSR8
        v4 = lambda ap: ap.rearrange("p (c n) -> p c n", c=4)
        mset('dve', A(Sbk), 0.0, w=[Sbk]); mset('dve', A(Tfk), 0.0, w=[Tfk])
        order = [BLK[4], BLK[3], BLK[2], BLK[1], BLK[0]]
        for b in order:
            j = b['j']; N = b['N']; mc = b['mc']; lat = b['lat']; tcol = 0 if lat else 1
            nt = N // 128
            hT = galloc(8)
            make_h([xa[:, k, mc:mc + N] for k in range(8)], [xk(k) for k in range(8)], N, G1(tcol), SH1(tcol),
                   [(A(hT[k])[:, :N], [hT[k]]) for k in range(8)])
            wk, wkv = wload(WIN[:, 1536:1792], 8, 256, tw('in', l))
            dma('sp', A(D1)[:, :N], rope_in[:, 0, j, 0:N], w=[D1])
            dma('sp', A(D2)[:, :N], rope_in[:, 1, j, 0:N], w=[D2])
            ps = PSR.get()
            for k in range(8):
                mm(A(ps)[:, :N], wkv[:, k, 0:128], A(hT[k])[:, :N], k == 0, k == 7, r=[wk, hT[k]], w=[ps])
            kT = galloc(1)[0]
            normrope(ps, bcol(12), pca[:, 255:256], 8.0, N, A(kT)[:, :N], [kT])
            vt = galloc(1)[0]
            for t in range(nt):
                ps2 = PSR.get()
                for k in range(8):
                    mm(A(ps2)[:, 0:128], A(hT[k])[:, t * 128:(t + 1) * 128], wkv[:, k, 128:256], k == 0, False,
                       r=[wk, hT[k]], w=[ps2])
                mm(A(ps2)[:, 0:128], A(onesb)[0:1, :], A(brow)[0:1, 0:128], False, True, r=[onesb, brow], w=[ps2])
                cp('act', A(vt)[:, t * 128:(t + 1) * 128], A(ps2)[:, 0:128], r=[ps2], w=[vt])
            if lat:
                dma('pool', ex1_in[par][:, j * 512:(j + 1) * 512], A(kT)[:, :N], r=[kT], w=[T_ex1i[par]])
                dma('pool', ex1_in[par][:, 2048 + j * 512:2048 + (j + 1) * 512], A(vt)[:, :N], r=[vt], w=[T_ex1i[par]])
            else:
                dma('pool', cxkv[:, 0:256], A(kT)[:, :N], r=[kT], w=[T_cx])
                dma('pool', cxkv[:, 256:512], A(vt)[:, :N], r=[vt], w=[T_cx])
            gfree_([kT, vt])
            wrk, wrkv = wload(WIN[:, 2304:2816], 8, 512, tw('in', l))
            wrv, wrvv = wload(WIN[:, 2816:3328], 8, 512, tw('in', l))
            def p1_proj(t):
                rkt_ = galloc(1)[0]; rvt_ = galloc(1)[0]
                for (wt_, wv_, boff, dst, scale) in ((wrk, wrkv, 128, rkt_, 0.125), (wrv, wrvv, 640, rvt_, 1.0)):
                    ps3 = PSR.get()
                    for k in range(8):
                        mm(A(ps3), A(hT[k])[:, t * 128:(t + 1) * 128], wv_[:, k, :], k == 0, False, r=[wt_, hT[k]], w=[ps3])
                    mm(A(ps3), A(onesb)[0:1, :], A(brow)[0:1, boff:boff + 512], False, True, r=[onesb, brow], w=[ps3])
                    act(A(dst), A(ps3), AF.Copy, r=[ps3], w=[dst], scale=scale)
                kd = galloc(2)
                for d_ in range(2):
                    tt('dve', A(kd[d_]).rearrange("p (h d) -> p h d", h=8), A(rkt_).rearrange("p (h d) -> p h d", h=8),
                       A(KFt)[:, d_, :].unsqueeze(2).to_broadcast([128, 8, 64]), ALU.mult, r=[rkt_, KFt], w=[kd[d_]])
                return (t, rkt_, rvt_, kd)

            def p1_scan(st):
                t, rkt_, rvt_, kd = st
                cidx = (j * 4 + t) if lat else (16 + t)
                dci = (j * 4 + t) if lat else (14 + t)
                sbl = SBL.get()
                tt('dve', A(sbl), A(Sbk), A(BMASK), ALU.mult, r=[Sbk, BMASK], w=[sbl])
                if True:
                    dma('pool', sbscr[cidx], A(sbl), r=[sbl], w=[T_sbs[cidx]])
                psB = PSR.get(); psF = PSR.get()
                for c in range(4):
                    cs_ = slice(c * 128, (c + 1) * 128)
                    mm(A(psB)[:, cs_], A(kd[1])[:, cs_], A(rvt_)[:, cs_], True, True, r=[kd[1], rvt_], w=[psB])
                    mm(A(psF)[:, cs_], A(kd[0])[:, cs_], A(rvt_)[:, cs_], True, True, r=[kd[0], rvt_], w=[psF])
                tt('dve', v4(A(Sbk)), v4(A(Sbk)), A(CDt)[:, 1, :].unsqueeze(2).to_broadcast([128, 4, 128]), ALU.mult,
                   r=[Sbk, CDt], w=[Sbk])
                tt('dve', A(Sbk), A(Sbk), A(psB), ALU.add, r=[Sbk, psB], w=[Sbk])
                tf = FT.get()
                tt('dve', v4(A(tf)[:, 0:512]), v4(A(psF)), A(DECt)[:, 0, dci, :].unsqueeze(2).to_broadcast([128, 4, 128]),
                   ALU.mult, r=[psF, DECt], w=[tf])
                tt('pool', A(Tfk), A(Tfk), A(tf)[:, 0:512], ALU.add, r=[Tfk, tf], w=[Tfk])
                gfree_([rkt_, rvt_] + kd)
            pend1 = None
            for t in reversed(range(nt)):
                st = p1_proj(t)
                if pend1 is not None: p1_scan(pend1)
                pend1 = st
            p1_scan(pend1)
            if not lat:
                tt('dve', v4(A(Ssf)), v4(A(Tfk)), A(coef)[:, 4, :].unsqueeze(2).to_broadcast([128, 4, 128]), ALU.mult,
                   r=[Tfk, coef], w=[Ssf])
                tt('dve', v4(A(Ssb)), v4(A(Sbk)), A(coef)[:, 9, :].unsqueeze(2).to_broadcast([128, 4, 128]), ALU.mult,
                   r=[Sbk, coef], w=[Ssb])
                mset('dve', A(Sbk), 0.0, w=[Sbk]); mset('dve', A(Tfk), 0.0, w=[Tfk])
            gfree_(hT)
        dma('pool', ex2_in[par][:, 0:512], A(Tfk), r=[Tfk], w=[T_ex2i[par]])
        dma('pool', ex2_in[par][:, 512:1024], A(Sbk), r=[Sbk], w=[T_ex2i[par]])
        PSRh[0] = PSR4
        if STOP <= 3: break
        P.op('pool', lambda e, par=par: e.collective_compute("AllGather", ALU.bypass, replica_groups=RG,
                                                              ins=[ex1_in[par]], outs=[ex1_out[par]]),
             r=[T_ex1i[par]], w=[T_ex1o[par]], cc=True)
        P.op('pool', lambda e, par=par: e.collective_compute("AllGather", ALU.bypass, replica_groups=RG,
                                                              ins=[ex2_in[par]], outs=[ex2_out[par]]),
             r=[T_ex2i[par]], w=[T_ex2o[par]], cc=True)
        exch_finish(par)
        xhk = lambda k: A(ACC[k // 2])[:, (k % 2) * 256:(k % 2) * 256 + 150]
        xht = lambda k: ACC[k // 2]
        for b in BLK:
            j = b['j']; ws = b['ws']; N = b['N']
            for k in range(8):
                cp('pool', xhk(k)[:, j * 30:j * 30 + 15], xa[:, k, ws:ws + 15], r=xk(k) + [xTh], w=[xht(k)])
                cp('pool', xhk(k)[:, j * 30 + 15:j * 30 + 30], xa[:, k, ws + 15 + N:ws + 30 + N], r=xk(k) + [xTh], w=[xht(k)])
        hhs = galloc(3)
        hhk = lambda k: A(hhs[k // 3])[:, (k % 3) * 150:(k % 3) * 150 + 150]
        hht = lambda k: hhs[k // 3]
        make_h([xhk(k)[:, 0:120] for k in range(8)], [[xht(k)] for k in range(8)], 120, G1(0), SH1(0),
               [(hhk(k)[:, 0:120], [hht(k)]) for k in range(8)])
        make_h([xhk(k)[:, 120:150] for k in range(8)], [[xht(k)] for k in range(8)], 30, G1(1), SH1(1),
               [(hhk(k)[:, 120:150], [hht(k)]) for k in range(8)])
        wa, wav = wload(WIN[:, 0:512], 8, 512, tw('in', l))
        wb, wbv = wload(WIN[:, 512:1024], 8, 512, tw('in', l))
        uha = A(uh)
        for c in range(4):
            psa = PSR.get(); psb = PSR.get()
            for k in range(8):
                mm(A(psa)[:, :150], wav[:, k, c * 128:(c + 1) * 128], hhk(k), k == 0, k == 7, r=[wa, hht(k)], w=[psa])
            for k in range(8):
                mm(A(psb)[:, :150], wbv[:, k, c * 128:(c + 1) * 128], hhk(k), k == 0, k == 7, r=[wb, hht(k)], w=[psb])
            sg = FT.get()
            act(A(sg)[:, :150], A(psb)[:, :150], AF.Sigmoid, r=[psb, pc], w=[sg], bias=bcol(4 + c))
            stt('dve', uha[:, c, :], A(psa)[:, :150], bcol(c), A(sg)[:, :150], ALU.add, ALU.mult, r=[psa, sg, pc], w=[uh])
            ts('dve', uha[:, c, 0:15], uha[:, c, 0:15], rk_[:, 8:9], None, ALU.mult, None, r=[uh, rkt], w=[uh])
            ts('dve', uha[:, c, 105:120], uha[:, c, 105:120], rk_[:, 9:10], None, ALU.mult, None, r=[uh, rkt], w=[uh])
            mset('dve', uha[:, c, 120:150], 0.0, w=[uh])
        gfree_(hhs)

        if STOP <= 4: break
        if l == 0:
            CV.finish()
            if depth > 1:
                CV.start(1)
                n_cv = (len(CV.q) + 9) // 10
        for b in BLK:
            j = b['j']; N = b['N']; mc = b['mc']; ws = b['ws']; lat = b['lat']; tcol = 0 if lat else 1
            nt = N // 128
            if not lat: exch_issue(1 - par)
            if l == depth - 1 and not lat: continue
            if l + 1 < depth: CV.step(n_cv)
            hT = galloc(8)
            make_h([xa[:, k, mc:mc + N] for k in range(8)], [xk(k) for k in range(8)], N, G1(tcol), SH1(tcol),
                   [(A(hT[k])[:, :N], [hT[k]]) for k in range(8)])

            if l == 0 and j == 0:
                for k in range(8): dump16(k, hT[k])
            wrq, wrqv = wload(WIN[:, 1792:2304], 8, 512, tw('in', l))
            rqT = galloc(4); rkT = galloc(4)
            for c in range(4):
                ps = PSR.get()
                for k in range(8):
                    mm(A(ps)[:, :N], wrqv[:, k, c * 128:(c + 1) * 128], A(hT[k])[:, :N], k == 0, k == 7, r=[wrq, hT[k]], w=[ps])
                act(A(rqT[c])[:, :N], A(ps)[:, :N], AF.Identity, r=[ps, pc], w=[rqT[c]], bias=bcol(14 + c))
            wrk, wrkv = wload(WIN[:, 2304:2816], 8, 512, tw('in', l))
            for c in range(4):
                ps = PSR.get()
                for k in range(8):
                    mm(A(ps)[:, :N], wrkv[:, k, c * 128:(c + 1) * 128], A(hT[k])[:, :N], k == 0, k == 7, r=[wrk, hT[k]], w=[ps])
                ts('dve', A(rkT[c])[:, :N], A(ps)[:, :N], bcol(18 + c), 0.125, ALU.add, ALU.mult, r=[ps, pc], w=[rkT[c]])
            rkM = galloc(nt); rvM = galloc(nt); rgM = galloc(nt)
            for t in range(nt):
                ps3 = PSR.get()
                for k in range(8):
                    mm(A(ps3), A(hT[k])[:, t * 128:(t + 1) * 128], wrkv[:, k, :], k == 0, False, r=[wrk, hT[k]], w=[ps3])
                mm(A(ps3), A(onesb)[0:1, :], A(brow)[0:1, 128:640], False, True, r=[onesb, brow], w=[ps3])
                act(A(rkM[t]), A(ps3), AF.Copy, r=[ps3], w=[rkM[t]], scale=0.125)
            if STOP <= 4.05: break
            wrv, wrvv = wload(WIN[:, 2816:3328], 8, 512, tw('in', l))
            for t in range(nt):
                ps3 = PSR.get()
                for k in range(8):
                    mm(A(ps3), A(hT[k])[:, t * 128:(t + 1) * 128], wrvv[:, k, :], k == 0, False, r=[wrv, hT[k]], w=[ps3])
                mm(A(ps3), A(onesb)[0:1, :], A(brow)[0:1, 640:1152], False, True, r=[onesb, brow], w=[ps3])
                cp('act', A(rvM[t]), A(ps3), r=[ps3], w=[rvM[t]])
            wrg, wrgv = wload(WIN[:, 3328:3840], 8, 512, tw('in', l))
            for t in range(nt):
                ps3 = PSR.get()
                for k in range(8):
                    mm(A(ps3), A(hT[k])[:, t * 128:(t + 1) * 128], wrgv[:, k, :], k == 0, False, r=[wrg, hT[k]], w=[ps3])
                mm(A(ps3), A(onesb)[0:1, :], A(brow)[0:1, 1152:1664], False, True, r=[onesb, brow], w=[ps3])
                act(A(rgM[t]), A(ps3), AF.Silu, r=[ps3], w=[rgM[t]])
            wa, wav = wload(WIN[:, 0:512], 8, 512, tw('in', l))
            wb, wbv = wload(WIN[:, 512:1024], 8, 512, tw('in', l))
            for c in range(4):
                psa = PSR.get(); psb = PSR.get()
                for k in range(8):
                    mm(A(psa)[:, :N], wav[:, k, c * 128:(c + 1) * 128], A(hT[k])[:, :N], k == 0, k == 7, r=[wa, hT[k]], w=[psa])
                for k in range(8):
                    mm(A(psb)[:, :N], wbv[:, k, c * 128:(c + 1) * 128], A(hT[k])[:, :N], k == 0, k == 7, r=[wb, hT[k]], w=[psb])
                sg = FT.get()
                act(A(sg)[:, :N], A(psb)[:, :N], AF.Sigmoid, r=[psb, pc], w=[sg], bias=bcol(4 + c))
                u = ubuf[c]
                stt('dve', A(u)[:, 15:15 + N], A(psa)[:, :N], bcol(c), A(sg)[:, :N], ALU.add, ALU.mult, r=[psa, sg, pc], w=[u])
                cp('pool', A(u)[:, 0:15], A(uh)[:, c, j * 30:j * 30 + 15], r=[uh], w=[u])
                cp('pool', A(u)[:, 15 + N:30 + N], A(uh)[:, c, j * 30 + 15:j * 30 + 30], r=[uh], w=[u])
            wc = lambda kk, c: pca[:, 118 + c * 31 + kk:119 + c * 31 + kk]
            def conv_pe(c):
                psc = PSR.get()
                for kk in range(31):
                    dg = DG.get()
                    act(A(dg), A(ident), AF.Copy, r=[ident, pc], w=[dg], scale=wc(kk, c))
                    mm(A(psc)[:, :N], A(dg), A(ubuf[c])[:, kk:kk + N], kk == 0, kk == 30, r=[dg, ubuf[c]], w=[psc])
                act(A(ACC[c])[:, :N], A(psc)[:, :N], AF.Identity, r=[psc, pc], w=[ACC[c]], bias=pca[:, 242 + c:243 + c])
            cpc = 4 // nt
            if j == 0:
                v4 = lambda ap: ap.rearrange("p (c n) -> p c n", c=4)
                for d_, Sst in enumerate((Ssf, Ssb)):
                    for q in range(4):
                        tl = FT.get()
                        dma('sp', A(tl)[:, 0:512], ex2_out[par][q * 128:(q + 1) * 128, d_ * 512:(d_ + 1) * 512], r=[T_ex2o[par]], w=[tl])
                        tt('dve', v4(A(tl)[:, 0:512]), v4(A(tl)[:, 0:512]),
                           A(coef)[:, d_ * 5 + q, :].unsqueeze(2).to_broadcast([128, 4, 128]), ALU.mult, r=[tl, coef], w=[tl])
                        tt('dve', A(Sst), A(Sst), A(tl)[:, 0:512], ALU.add, r=[Sst, tl], w=[Sst])

                tt('dve', A(Ssb), A(Ssb), A(BMASK), ALU.mult, r=[Ssb, BMASK], w=[Ssb])
            if j == 0:
                cp('dve', A(Sf), A(Ssf), r=[Ssf], w=[Sf])
            if not lat:
                mset('dve', A(Sf), 0.0, w=[Sf])
            retT = galloc(4)
            v8 = lambda ap: ap.rearrange("p (h e) -> p h e", h=8)
            for t in range(nt):
                cidx = (j * 4 + t) if lat else (16 + t)
                tsl = slice(t * 128, (t + 1) * 128)
                tt('dve', A(Sf16), A(Sf), A(BMASK), ALU.mult, r=[Sf, BMASK], w=[Sf16])
                sbl = SBL.get()
                dma('sp', A(sbl), sbscr[cidx], r=[T_sbs[cidx]], w=[sbl])
                if STOP <= 4.101: break
                if lat:
                    sbu = galloc(1)[0]
                    tl = FT.get()
                    tt('dve', v4(A(tl)[:, 0:512]), v4(A(Ssb)), A(DECt)[:, 1, j * 4 + t, :].unsqueeze(2).to_broadcast([128, 4, 128]),
                       ALU.mult, r=[Ssb, DECt], w=[tl])
                    tt('dve', A(sbu), A(sbl), A(tl)[:, 0:512], ALU.add, r=[sbl, tl], w=[sbu])
                else:
                    sbu = sbl
                if STOP <= 4.102: break
                qd = galloc(2)
                for d_ in range(2):
                    for c in range(4):
                        tt('dve' if d_ == 0 else 'pool', A(qd[d_])[:, c * 128:(c + 1) * 128], A(rqT[c])[:, tsl], A(QFt)[:, d_, c, :],
                           ALU.mult, r=[rqT[c], QFt], w=[qd[d_]])
                if STOP <= 4.103: break
                kdf = galloc(1)[0]
                tt('pool', v8(A(kdf)), v8(A(rkM[t])), A(KFt)[:, 0, :].unsqueeze(2).to_broadcast([128, 8, 64]), ALU.mult,
                   r=[rkM[t], KFt], w=[kdf])
                if STOP <= 4.104: break
                WT = galloc(2)
                for c in range(4):
                    cp('act', A(RKL)[0:64, c * 128:(c + 1) * 128], A(rkT[c])[0:64, tsl], r=[rkT[c]], w=[RKL])
                    cp('act', A(RKH)[64:128, c * 128:(c + 1) * 128], A(rkT[c])[64:128, tsl], r=[rkT[c]], w=[RKH])
                for hp in range(2):
                    psa = PSR.get()
                    for hh_ in range(4):
                        h = hp * 4 + hh_; c = h // 2
                        RKm = RKL if h % 2 == 0 else RKH
                        mm(A(psa)[:, hh_ * 128:(hh_ + 1) * 128], A(RKm)[:, c * 128:(c + 1) * 128], A(rqT[c])[:, tsl], True, True,
                           r=[RKm, rqT[c]], w=[psa])
                    tt('dve', A(WT[hp]), A(psa), A(DTt)[:, hp * 4:(hp + 1) * 4, :].rearrange("p h n -> p (h n)"), ALU.mult,
                       r=[psa, DTt], w=[WT[hp]])
                if STOP <= 4.11: break
                po = PSL[t % 4]
                for h in range(8):
                    c = h // 2; off = (h % 2) * 64
                    osl = slice(h * 64, (h + 1) * 64)
                    mm(A(po)[:, osl], A(WT[h // 4])[:, (h % 4) * 128:(h % 4 + 1) * 128], A(rvM[t])[:, osl], True, False,
                       r=[WT[h // 4], rvM[t]], w=[po])
                    mm(A(po)[:, osl], A(qd[0])[:, c * 128:(c + 1) * 128],
                       A(Sf16)[:, c * 128 + off:c * 128 + off + 64], False, False, r=[qd[0], Sf16], w=[po])
                    mm(A(po)[:, osl], A(qd[1])[:, c * 128:(c + 1) * 128],
                       A(sbu)[:, c * 128 + off:c * 128 + off + 64], False, True, r=[qd[1], sbu], w=[po])
                ob = FT.get(); sq = FT.get()
                cp('act', A(ob)[:, 0:512], A(po), r=[po], w=[ob])
                act(A(sq)[:, 0:512], A(po), AF.Square, r=[po], w=[sq])
                psS = PSR.get()
                for c in range(4):
                    cs_ = slice(c * 128, (c + 1) * 128)
                    mm(A(psS)[:, cs_], A(kdf)[:, cs_], A(rvM[t])[:, cs_], True, True, r=[kdf, rvM[t]], w=[psS])
                tt('dve', v4(A(Sf)), v4(A(Sf)), A(CDt)[:, 0, :].unsqueeze(2).to_broadcast([128, 4, 128]), ALU.mult, r=[Sf, CDt], w=[Sf])
                tt('dve', A(Sf), A(Sf), A(psS), ALU.add, r=[Sf, psS], w=[Sf])
                for c in range(t * cpc, (t + 1) * cpc): conv_pe(c)
                s1 = sf[:, 0:8]; s2 = sf[:, 8:16]; s3 = sf[:, 16:24]
                P.op('dve', lambda e, ob=ob, s1=s1: e.tensor_reduce(s1, v8(A(ob)[:, 0:512]), AX.X, ALU.add), r=[ob], w=[smallf])
                P.op('dve', lambda e, sq=sq, s2=s2: e.tensor_reduce(s2, v8(A(sq)[:, 0:512]), AX.X, ALU.add), r=[sq], w=[smallf])
                ts('dve', s1, s1, 1.0 / 64, None, ALU.mult, None, r=[smallf], w=[smallf])
                tt('dve', s3, s1, s1, ALU.mult, r=[smallf], w=[smallf])
                stt('dve', s2, s2, 1.0 / 64, s3, ALU.mult, ALU.subtract, r=[smallf], w=[smallf])
                rsq(s2, s2, 1.0, EPS, [smallf], [smallf])
                tt('dve', v8(A(ob)[:, 0:512]), v8(A(ob)[:, 0:512]), s1.unsqueeze(2).to_broadcast([128, 8, 64]), ALU.subtract,
                   r=[ob, smallf], w=[ob])
                tt('dve', v8(A(ob)[:, 0:512]), v8(A(ob)[:, 0:512]), s2.unsqueeze(2).to_broadcast([128, 8, 64]), ALU.mult,
                   r=[ob, smallf], w=[ob])
                yb = galloc(1)[0]
                tt('dve', A(yb), A(ob)[:, 0:512], A(rgM[t]), ALU.mult, r=[ob, rgM[t]], w=[yb])
                pst_ = PSR.get()
                for c in range(4):
                    mm(A(pst_)[:, c * 128:(c + 1) * 128], A(yb)[:, c * 128:(c + 1) * 128], A(ident), True, True, r=[yb, ident], w=[pst_])
                for c in range(4):
                    act(A(retT[c])[:, tsl], A(pst_)[:, c * 128:(c + 1) * 128], AF.Copy, r=[pst_, pc], w=[retT[c]],
                        scale=pca[:, 368 + c:369 + c])
                gfree_(qd + [kdf, yb] + WT + ([sbu] if lat else []))
            if l == 0 and j == 0:
                for c in range(4): dump16(8 + c, retT[c])
                for c in range(4): dump16(28 + c, rqT[c])
                for c in range(4): dump16(32 + c, rkT[c])
                for c in range(4): dump16(36 + c, rvM[c])
            gfree_(rqT + rkT + rkM + rvM + rgM)

            if STOP <= 4.2: break
            aT = galloc(4)
            ps1 = PSR.get(); ps2 = PSR.get()
            for c in range(4):
                mm(A(ps1)[:, :N], A(onesf), A(ACC[c])[:, :N], c == 0, c == 3, r=[onesf, ACC[c]], w=[ps1])
            for c in range(4):
                sq = FT.get()
                act(A(sq)[:, :N], A(ACC[c])[:, :N], AF.Square, r=[ACC[c]], w=[sq])
                mm(A(ps2)[:, :N], A(onesf), A(sq)[:, :N], c == 0, c == 3, r=[onesf, sq], w=[ps2])
            mt = D1; vt_ = D2
            ts('dve', A(mt)[:, :N], A(ps1)[:, :N], 1.0 / 512, None, ALU.mult, None, r=[ps1], w=[mt])
            tt('dve', A(vt_)[:, :N], A(mt)[:, :N], A(mt)[:, :N], ALU.mult, r=[mt], w=[vt_])
            stt('dve', A(vt_)[:, :N], A(ps2)[:, :N], 1.0 / 512, A(vt_)[:, :N], ALU.mult, ALU.subtract, r=[ps2, vt_], w=[vt_])
            rsq(A(vt_)[:, :N], A(vt_)[:, :N], 1.0, EPS, [vt_], [vt_])
            for c in range(4):
                acc = ACC[c]
                tt('dve', A(acc)[:, :N], A(acc)[:, :N], A(mt)[:, :N], ALU.subtract, r=[acc, mt], w=[acc])
                tt('pool', A(acc)[:, :N], A(acc)[:, :N], A(vt_)[:, :N], ALU.mult, r=[acc, vt_], w=[acc])
                ts('dve', A(acc)[:, :N], A(acc)[:, :N], pca[:, 246 + c:247 + c], pca[:, 250 + c:251 + c], ALU.mult, ALU.add,
                   r=[acc, pc], w=[acc])
                act(A(aT[c])[:, :N], A(acc)[:, :N], AF.Silu, r=[acc], w=[aT[c]])

            if STOP <= 4.3: break
            if False:
                wq = Wrot.get()
                wqv = A(wq)[:, 0:4096].rearrange("p (k n) -> p k n", k=8)
                for c in range(4):
                    for hf in range(2):
                        h_ = hf * 4 + c
                        dma('pool', wqv[:, :, c * 128 + hf * 64:c * 128 + hf * 64 + 64],
                            w_in[0][:, 1024 + h_ * 64:1024 + (h_ + 1) * 64].rearrange("(k p) n -> p k n", p=128), w=[wq])
            else:
                wq, wqv = wload(WIN[:, 1024:1536], 8, 512, tw('in', l))
            dma('sp', A(D1)[:, :N], rope_in[:, 0, j, 0:N], w=[D1])
            dma('sp', A(D2)[:, :N], rope_in[:, 1, j, 0:N], w=[D2])
            qT = galloc(4)
            for c in range(4):
                ps = PSR.get()
                for k in range(8):
                    mm(A(ps)[:, :N], wqv[:, k, c * 128:(c + 1) * 128], A(hT[k])[:, :N], k == 0, k == 7, r=[wq, hT[k]], w=[ps])
                normrope(ps, bcol(8 + c), pca[:, 254:255], 1.0, N, A(qT[c])[:, :N], [qT[c]])
            if l == 0 and j == 0:
                for c in range(4): dump16(12 + c, aT[c])
                for c in range(4): dump16(24 + c, qT[c])
            attT = galloc(4)
            NS_ = 16 // GT
            groups = [('c', 0)] + ([(q, s_) for q in range(4) for s_ in range(NS_)] if lat else [])
            for half in range(2):
                hs = slice(half * 64, half * 64 + 64)
                os_ = slice((1 - half) * 64, (1 - half) * 64 + 64)
                VGr = VG0 if half == 0 else VG1
                voff = 0 if half == 0 else 64
                first = True
                pend = []

                def flush_pv(keep):
                    while len(pend) > keep:
                        (po_, vg_, t_, pt_, st_, sp_) = pend.pop(0)
                        mm(A(po_)[:, :N], A(vg_)[:, t_, :], A(pt_)[:, :N], st_, sp_, r=[vg_, pt_], w=[po_])
                for gi, (gq, gs) in enumerate(groups):
                    ktg = (KTG0 if half == 0 else KTG1).get(); vg = VGr.get()
                    if gq == 'c':
                        ntile = 2
                        dma('sp', A(ktg)[hs, 0:256], cxkv[hs, 0:256], r=[T_cx], w=[ktg])
                        dma('sp', A(vg)[:, 0:2, voff:voff + 64],
                            cxkv[:, 256:512].rearrange("p (t e) -> p t e", t=2)[:, :, half * 64:half * 64 + 64],
                            r=[T_cx], w=[vg])
                    else:
                        ntile = GT
                        GW = GT * 128
                        dma('sp', A(ktg)[hs, :], ex1_out[par][gq * 128 + half * 64:gq * 128 + half * 64 + 64, gs * GW:(gs + 1) * GW],
                            r=[T_ex1o[par]], w=[ktg])
                        dma('sp', A(vg)[:, :, voff:voff + 64],
                            ex1_out[par][gq * 128:(gq + 1) * 128, 2048 + gs * GW:2048 + (gs + 1) * GW]
                            .rearrange("p (t e) -> p t e", t=GT)[:, :, half * 64:half * 64 + 64],
                            r=[T_ex1o[par]], w=[vg])
                    last_g = gi == len(groups) - 1
                    for c in range(4):
                        po = PSL[c]
                        for t in range(ntile):
                            ps = PSR.get()
                            mm(A(ps)[:, :N], A(ktg)[:, t * 128:(t + 1) * 128], A(qT[c])[:, :N], True, True, r=[ktg, qT[c]], w=[ps])
                            pt = SQ.get()
                            act(A(pt)[:, :N], A(ps)[:, :N], AF.Exp, r=[ps], w=[pt])
                            pend.append((po, vg, t, pt, first and t == 0, last_g and t == ntile - 1))
                            flush_pv(2)
                    first = False
                flush_pv(0)
                for c in range(4):
                    po = PSL[c]
                    rc = FT.get()
                    mset('dve', A(rc)[hs, :N], 0.0, w=[rc])
                    P.op('act', lambda e, rc=rc, po=po, os_=os_, N=N: e.activation(A(rc)[os_, :N], A(po)[os_, :N], AF.Ln), r=[po], w=[rc])
                    P.op('act', lambda e, rc=rc, os_=os_, N=N: e.activation(A(rc)[os_, :N], A(rc)[os_, :N], AF.Exp, scale=-1.0), r=[rc], w=[rc])
                    ob = FT.get()
                    cp('act', A(ob)[hs, :N], A(po)[hs, :N], r=[po], w=[ob])
                    ps = PSR.get()
                    mm(A(ps)[:, :N], SWAP, A(rc)[:, :N], True, True, r=[cst, rc], w=[ps])
                    tt('dve', A(attT[c])[hs, :N], A(ob)[hs, :N], A(ps)[hs, :N], ALU.mult, r=[ob, ps], w=[attT[c]])
            gfree_(qT)

            if STOP <= 4.4: break
            zT = galloc(8)
            for gsec, (wsrc, wnm, br) in enumerate(((WPA, 'pa', aT), (WPB, 'pb', attT), (WPC, 'pc', retT))):
                for og in range(2):
                    if False:
                        wp_ = Wrot.get()
                        wpv = A(wp_)[:, 0:4096].rearrange("p (k n) -> p k n", k=4)
                        for c in range(4):
                            for hf in range(2):
                                h_ = hf * 4 + c
                                dma('pool', wpv[hf * 64:(hf + 1) * 64, c, :], w_pb[0][h_ * 64:(h_ + 1) * 64, :], w=[wp_])
                    else:
                        wp_, wpv = wload(wsrc, 4, 1024, tw(wnm, l))
                    c0 = 3840 + gsec * 1024 + og * 512
                    wg, wgv = wload(WIN[:, c0:c0 + 512], 8, 512, tw('in', l))
                    for oo in range(4):
                        o = og * 4 + oo
                        psg = PSR.get(); psp = PSR.get()
                        for k in range(8):
                            mm(A(psg)[:, :N], wgv[:, k, oo * 128:(oo + 1) * 128], A(hT[k])[:, :N], k == 0, k == 7, r=[wg, hT[k]], w=[psg])
                        for c in range(4):
                            mm(A(psp)[:, :N], wpv[:, c, o * 128:(o + 1) * 128], A(br[c])[:, :N], c == 0, c == 3, r=[wp_, br[c]], w=[psp])
                        sg = FT.get()
                        act(A(sg)[:, :N], A(psg)[:, :N], AF.Sigmoid, r=[psg, pc], w=[sg], bias=bcol(30 + gsec * 8 + o))
                        if gsec == 0:
                            tt('dve', A(zT[o])[:, :N], A(psp)[:, :N], A(sg)[:, :N], ALU.mult, r=[psp, sg], w=[zT[o]])
                        else:
                            tt('dve', A(sg)[:, :N], A(psp)[:, :N], A(sg)[:, :N], ALU.mult, r=[psp, sg], w=[sg])
                            tt('pool', A(zT[o])[:, :N], A(zT[o])[:, :N], A(sg)[:, :N], ALU.add, r=[zT[o], sg], w=[zT[o]])
            if l == 0 and j == 0:
                for c in range(4): dump16(16 + c, attT[c])
                for o in range(4): dump16(20 + o, zT[o])
            gfree_(aT + attT + retT + hT)
            for og in range(2):
                wo, wov = wload(WOUT[:, og * 512:(og + 1) * 512], 8, 512, tw('out', l))
                for oo in range(4):
                    o = og * 4 + oo
                    ps = PSR.get()
                    for k in range(8):
                        mm(A(ps)[:, :N], wov[:, k, oo * 128:(oo + 1) * 128], A(zT[k])[:, :N], k == 0, k == 7, r=[wo, zT[k]], w=[ps])
                    stt('dve', xa[:, o, mc:mc + N], A(ps)[:, :N], mo[:, 16 + o, tcol:tcol + 1], xa[:, o, mc:mc + N], ALU.mult, ALU.add,
                        r=[ps, modT] + xk(o), w=[xTk[o]])
            gfree_(zT)
            if l == 0 and j == 0:
                for o in range(8): dump32(o, xa[:, o, mc:mc + N], xk(o))
                dump32(8, mo.rearrange("p j t -> p (j t)"), [modT], 96)

        if STOP <= 5: break
        exch_finish(1 - par)
        PSRh[0] = PSR8
        for b in BLK:
            j = b['j']; ws = b['ws']; N = b['N']
            for k in range(8):
                cp('pool', xhk(k)[:, 2 * j:2 * j + 1], xa[:, k, ws + 14:ws + 15], r=xk(k) + [xTh], w=[xht(k)])
                cp('pool', xhk(k)[:, 2 * j + 1:2 * j + 2], xa[:, k, ws + 15 + N:ws + 16 + N], r=xk(k) + [xTh], w=[xht(k)])
        hhs = galloc(3)
        make_h([xhk(k)[:, 0:8] for k in range(8)], [[xht(k)] for k in range(8)], 8, G2(0), SH2(0), [(hhk(k)[:, 0:8], [hht(k)]) for k in range(8)])
        make_h([xhk(k)[:, 8:10] for k in range(8)], [[xht(k)] for k in range(8)], 2, G2(1), SH2(1), [(hhk(k)[:, 8:10], [hht(k)]) for k in range(8)])
        for k in range(8):
            ts('dve', hhk(k)[:, 0:1], hhk(k)[:, 0:1], rk_[:, 8:9], None, ALU.mult, None, r=[hht(k), rkt], w=[hht(k)])
            ts('dve', hhk(k)[:, 7:8], hhk(k)[:, 7:8], rk_[:, 9:10], None, ALU.mult, None, r=[hht(k), rkt], w=[hht(k)])
            mset('dve', hhk(k)[:, 8:10], 0.0, w=[hht(k)])
        ga = A(gth)
        for b in BLK:
            j = b['j']; N = b['N']; mc = b['mc']; lat = b['lat']; tcol = 0 if lat else 1
            if l == depth - 1 and not lat: continue
            if l + 1 < depth: CV.step(n_cv)
            hT = galloc(8)
            make_h([xa[:, k, mc:mc + N] for k in range(8)], [xk(k) for k in range(8)], N, G2(tcol), SH2(tcol),
                   [(A(hT[k])[:, :N], [hT[k]]) for k in range(8)])
            actT = galloc(22)
            for pcs in range(11):
                wt = Wrot.get()
                wv = A(wt)[:, 0:4096].rearrange("p (k n) -> p k n", k=8)
                dma(WENG, wv[:, :, 0:256], WUP[:, pcs * 256:(pcs + 1) * 256].rearrange("(k p) n -> p k n", p=128), r=[tw('up', l)], w=[wt])
                dma(WENG, wv[:, :, 256:512], WUP[:, 2816 + pcs * 256:2816 + (pcs + 1) * 256].rearrange("(k p) n -> p k n", p=128),
                    r=[tw('up', l)], w=[wt])
                for ff in range(2):
                    f = pcs * 2 + ff
                    if j == 0:
                        psh = PSR.get()
                        for k in range(8):
                            mm(A(psh)[:, 0:10], wv[:, k, ff * 128:(ff + 1) * 128], hhk(k)[:, 0:10], k == 0, k == 7, r=[wt, hht(k)], w=[psh])
                        cp('act', A(gth)[:, f, :], A(psh)[:, 0:10], r=[psh], w=[gthf[f]])
                    psg = PSR.get(); psv = PSR.get()
                    for k in range(8):
                        mm(A(psg)[:, :N], wv[:, k, ff * 128:(ff + 1) * 128], A(hT[k])[:, :N], k == 0, k == 7, r=[wt, hT[k]], w=[psg])
                    for k in range(8):
                        mm(A(psv)[:, :N], wv[:, k, 256 + ff * 128:256 + (ff + 1) * 128], A(hT[k])[:, :N], k == 0, k == 7, r=[wt, hT[k]], w=[psv])
                    gt = FT.get()
                    cp('act', A(gt)[:, 1:1 + N], A(psg)[:, :N], r=[psg], w=[gt])
                    cp('act', A(gt)[:, 0:1], ga[:, f, 2 * j:2 * j + 1], r=[gthf[f]], w=[gt])
                    cp('act', A(gt)[:, 1 + N:2 + N], ga[:, f, 2 * j + 1:2 * j + 2], r=[gthf[f]], w=[gt])
                    acc = FT.get()
                    wf = lambda kk: pca[:, 256 + f * 3 + kk:257 + f * 3 + kk]
                    ts('dve', A(acc)[:, :N], A(gt)[:, 0:N], wf(0), pca[:, 322 + f:323 + f], ALU.mult, ALU.add, r=[gt, pc], w=[acc])
                    stt('dve', A(acc)[:, :N], A(gt)[:, 1:1 + N], wf(1), A(acc)[:, :N], ALU.mult, ALU.add, r=[gt, pc, acc], w=[acc])
                    stt('pool', A(acc)[:, :N], A(gt)[:, 2:2 + N], wf(2), A(acc)[:, :N], ALU.mult, ALU.add, r=[gt, pc, acc], w=[acc])
                    act(A(acc)[:, :N], A(acc)[:, :N], AF.Silu, r=[acc], w=[acc])
                    tt('dve', A(actT[f])[:, :N], A(acc)[:, :N], A(psv)[:, :N], ALU.mult, r=[acc, psv], w=[actT[f]])
            for o in range(8):
                wt = Wrot.get()
                wv = A(wt)[:, 0:22 * 128].rearrange("p (k n) -> p k n", k=22)
                dma(WENG, wv, WDN[:, o * 128:(o + 1) * 128].rearrange("(k p) n -> p k n", p=128), r=[tw('dn', l)], w=[wt])
                ps = PSR.get()
                for f in range(22):
                    mm(A(ps)[:, :N], wv[:, f, :], A(actT[f])[:, :N], f == 0, f == 21, r=[wt, actT[f]], w=[ps])
                stt('dve', xa[:, o, mc:mc + N], A(ps)[:, :N], mo[:, 40 + o, tcol:tcol + 1], xa[:, o, mc:mc + N], ALU.mult, ALU.add,
                    r=[ps, modT] + xk(o), w=[xTk[o]])
            gfree_(hT + actT)
            if j == 0: gfree_(hhs)
            if j == 1 and l + 1 < depth: emit_adaln(l + 1)
        if l + 1 < depth: CV.finish()
        PSRh[0] = PSR4

    fa = A(fng)
    for j in range(4):
        mc = 15 + 512 * j
        ps = PSR.get()
        for k in range(8):
            sq = SQ.get()
            act(A(sq), xa[:, k, mc:mc + 512], AF.Square, r=xk(k), w=[sq])
            mm(A(ps), A(onesb), A(sq), k == 0, k == 7, r=[onesb, sq], w=[ps])
        R = D0
        rsq(A(R)[:, 0:512], A(ps), 1.0 / D, EPS, [ps], [R])
        for k in range(8):
            t = FT.get()
            stt('dve', A(t)[:, 0:512], xa[:, k, mc:mc + 512], fa[:, k:k + 1], A(R)[:, 0:512], ALU.mult, ALU.mult, r=xk(k) + [R, fng], w=[t])
            dma('sp', out_d[:, k, j * 512:(j + 1) * 512], A(t)[:, 0:512], r=[t], w=[])

    P.emit(nc, es)
    es.close()
    return nc


def _fm(v, nch):
    return np.ascontiguousarray(np.asarray(v, np.float32).reshape(nch, 128).T)


def _consts():
    c = np.zeros((128, NCST), np.float32)
    j = np.arange(128, dtype=np.float32)[:, None]; i = np.arange(128, dtype=np.float32)[None, :]
    c[:, 0:128] = np.maximum(i - j, 0); c[:, 128:256] = np.maximum(j - i, 0); c[:, 256:384] = np.eye(128)
    c[:, 384:512] = i + 1; c[:, 512:640] = 128 - i
    c[:, 640] = 127 - j[:, 0]; c[:, 641] = j[:, 0]
    c[:, 642:658] = 128.0 * (15 - np.arange(16))[None, :]
    rt = np.zeros((128, 128), np.float32)
    for g in range(4):
        for t in range(16):
            a = g * 32 + t
            rt[a + 16, a] = -1.0
            rt[a, a + 16] = 1.0
    c[:, 658:786] = rt
    sw = np.zeros((128, 128), np.float32)
    for k in range(128): sw[k, (k + 64) % 128] = 1.0
    c[:, 786:914] = sw
    return c


def _rope(start):
    tab = np.zeros((128, 2, 5, 512), np.float32)
    tab[:, 0, 4, :] = 1.0
    p = np.arange(128); d = p % 64; f = (d % 16).astype(np.float32)
    inv = (np.float32(10000.0) ** (-f / np.float32(16.0))).astype(np.float32)
    for j in range(4):
        t = start + j * 512 + np.arange(512)
        row = (t // 64).astype(np.float32); col = (t % 64).astype(np.float32)
        pos = np.where((d < 32)[:, None], row[None, :], col[None, :]).astype(np.float32)
        ang = (pos * inv[:, None]).astype(np.float32)
        tab[:, 0, j, :] = np.cos(ang); tab[:, 1, j, :] = np.sin(ang)
    return tab


def _prep(inputs, depth):
    f = lambda k: np.asarray(inputs[k], np.float32)
    x = f("x"); c = f("c"); ctx = f("ctx"); c_ctx = f("c_ctx")
    pcol = np.zeros((depth, 128, NPC), np.float32)
    brow = np.zeros((depth, 1, NBROW), np.float32)
    p = np.arange(128)
    for l in range(depth):
        pc = pcol[l]
        pc[:, 0:48] = _fm(f("b_ada")[l], 48)
        pc[:, 48:56] = _fm(f("norm1_g")[l], 8); pc[:, 56:64] = _fm(f("norm2_g")[l], 8)
        b = f("b_in")[l].copy()
        b[1024:1536] = b[1024:1536].reshape(2, 4, 64).transpose(1, 0, 2).reshape(512)
        pc[:, 64:118] = _fm(b, 54)
        pc[:, 118:242] = f("conv_dw_w")[l].T.reshape(4, 128, 31).transpose(1, 0, 2).reshape(128, 124)
        pc[:, 242:246] = _fm(f("conv_dw_b")[l], 4); pc[:, 246:250] = _fm(f("conv_ln_g")[l], 4)
        pc[:, 250:254] = _fm(f("conv_ln_b")[l], 4)
        pc[:, 254] = f("q_norm_g")[l][p % 64]; pc[:, 255] = f("k_norm_g")[l][p % 64]
        pc[:, 256:322] = f("ffn_dw_w")[l].T.reshape(22, 128, 3).transpose(1, 0, 2).reshape(128, 66)
        pc[:, 322:344] = _fm(f("ffn_dw_b")[l], 22)
        lg = f("ret_decay_logit")[l]
        for d_ in range(2):
            for cc in range(4):
                pc[:, 344 + d_ * 4 + cc] = lg[d_, 2 * cc + p // 64]
            pc[:, 352 + d_ * 8:360 + d_ * 8] = lg[d_][None, :]
        pc[:, 368:372] = _fm(f("ret_gn_g")[l], 4)
        bi = f("b_in")[l]
        brow[l, 0] = np.concatenate([bi[1664:1792], bi[2304:2816], bi[2816:3328], bi[3328:3840]])
    cst = _consts()
    fng = _fm(f("final_norm_g"), 8)
    shared = dict(cst=cst, pcol=pcol, brow=brow, fng=fng)
    for k in ("w_ada", "w_in", "w_pa", "w_pb", "w_pc", "w_out", "w_up", "w_down"):
        shared[k] = np.ascontiguousarray(f(k)[:depth])
    maps = []
    for r in range(8):
        b_ = r // 4; q = r % 4; start = q * NLAT
        xe = np.zeros((XW, D), np.float32)
        xe[15:15 + NLAT] = x[b_, start:start + NLAT]
        xe[CTX0:CTX0 + CTX] = ctx[b_]
        xT = np.ascontiguousarray(xe.T.reshape(8, 128, XW).transpose(1, 0, 2))
        cv = np.stack([c[b_], c_ctx], 0)
        cT = np.ascontiguousarray(cv.T.reshape(8, 128, 2).transpose(1, 0, 2))
        rk = np.zeros((128, 32), np.float32)
        for q2 in range(4):
            rk[:, q2] = 1.0 if q2 == q - 1 else 0.0
            rk[:, 4 + q2] = 1.0 if q2 == q + 1 else 0.0
            rk[:, 10 + q2] = 2048.0 * max(q - 1 - q2, 0); rk[:, 14 + q2] = 1.0 if q2 < q else 0.0
            rk[:, 19 + q2] = 2048.0 * max(q2 - q - 1, 0); rk[:, 23 + q2] = 1.0 if q2 > q else 0.0
        rk[:, 8] = 1.0 if q > 0 else 0.0; rk[:, 9] = 1.0 if q < 3 else 0.0
        rk[:, 18] = 2048.0 * q; rk[:, 27] = 2048.0 * (3 - q)
        m = dict(shared); m.update(xT=xT, cT=cT, rope=_rope(start), rkt=rk)
        maps.append(m)
    return maps


_NC = {}


def kernel(**inputs):
    depth = int(inputs.pop("_depth", DEPTH))
    if depth not in _NC:
        _NC[depth] = build(depth)
    maps = _prep(inputs, depth)
    res = run_bass_kernel_spmd(_NC[depth], maps, core_ids=list(range(8)))
    out = np.zeros((2, SEQ, D), np.float32)
    for r in range(8):
        o = np.asarray(res.results[r]["out"], np.float32)
        out[r // 4, (r % 4) * NLAT:(r % 4 + 1) * NLAT] = o.transpose(2, 1, 0).reshape(NLAT, D)
    return out
```
